# Optimizing a Trainium2 kernel written in Bass

```python
import jax, jax.numpy as jnp
from jax import lax
import numpy as np

D_MODEL = 1024
BATCH = 8
SEQ = 4096
DEPTH = 2

HEAD_DIM = 64
N_SB_HEADS = 12
SB_WIDTH = N_SB_HEADS * HEAD_DIM
N_MEM_HEADS = 4
MEM_WIDTH = N_MEM_HEADS * HEAD_DIM
MEM_TOKENS = 256
MIX_WIDTH = SB_WIDTH + MEM_WIDTH
POOL_WINDOWS = (2, 4, 8, 16)
N_POOL_GROUPS = len(POOL_WINDOWS)
POOL_WIDTH = SB_WIDTH
POOL_GROUP = POOL_WIDTH // N_POOL_GROUPS
D_FF = -(-8 * D_MODEL // (3 * 256)) * 256
SB_BLOCK = 128
N_A_LAYERS = DEPTH // 2
N_B_LAYERS = DEPTH - N_A_LAYERS
EPS = 1e-6

kernel_name = "yoco_pool_stickbreak_hybrid"


def rmsnorm(x, g):
    xf = x.astype(jnp.float32)
    y = xf * lax.rsqrt(jnp.mean(xf * xf, axis=-1, keepdims=True) + EPS)
    return (y * g.astype(jnp.float32)).astype(x.dtype)


def swiglu(h, w_gu, w_down):
    gate, up = jnp.split(h @ w_gu, 2, axis=-1)
    return (jax.nn.silu(gate) * up) @ w_down


def multiscale_pool(u):
    s = u.shape[1]
    uf = u.astype(jnp.float32)
    cs = jnp.cumsum(uf, axis=1)
    pos = jnp.arange(s)
    outs = []
    for g, w in enumerate(POOL_WINDOWS):
        c = cs[..., g * POOL_GROUP:(g + 1) * POOL_GROUP]
        prev = jnp.pad(c, ((0, 0), (w, 0), (0, 0)))[:, :s]
        cnt = jnp.minimum(pos + 1, w).astype(jnp.float32)[None, :, None]
        outs.append((c - prev) / cnt)
    pooled = jnp.concatenate(outs, axis=-1)
    return (pooled - uf).astype(u.dtype)


def memory_kv(mem, mem_norm, w_mem_kv):
    b, m, _ = mem.shape
    k, v = jnp.split(rmsnorm(mem, mem_norm) @ w_mem_kv, 2, axis=-1)
    return (k.reshape(b, m, N_MEM_HEADS, HEAD_DIM), v.reshape(b, m, N_MEM_HEADS, HEAD_DIM))


def memory_attention(q, mk, mv):
    b, s, _ = q.shape
    qh = q.reshape(b, s, N_MEM_HEADS, HEAD_DIM)
    logits = jnp.einsum("bshd,bmhd->bhsm", qh, mk).astype(jnp.float32) * (HEAD_DIM ** -0.5)
    p = jax.nn.softmax(logits, axis=-1).astype(mv.dtype)
    return jnp.einsum("bhsm,bmhd->bshd", p, mv).reshape(b, s, MEM_WIDTH)


def stick_breaking_attention(q, k, v):
    s = q.shape[2]
    scale = HEAD_DIM ** -0.5
    outs = []
    for i in range(s // SB_BLOCK):
        q0 = i * SB_BLOCK
        end = q0 + SB_BLOCK
        qb = q[:, :, q0:end]
        kb = k[:, :, :end]
        vb = v[:, :, :end]
        z = jnp.einsum("bhqd,bhkd->bhqk", qb, kb).astype(jnp.float32) * scale
        tpos = q0 + jnp.arange(SB_BLOCK)
        spos = jnp.arange(end)
        mask = spos[None, :] < tpos[:, None]
        log_not = jnp.where(mask, jax.nn.log_sigmoid(-z), 0.0)
        later = lax.cumsum(log_not, axis=3, reverse=True) - log_not
        wts = jnp.where(mask, jnp.exp(jax.nn.log_sigmoid(z) + later), 0.0)
        outs.append(jnp.einsum("bhqk,bhkd->bhqd", wts.astype(vb.dtype), vb))
    return jnp.concatenate(outs, axis=2)


def pool_layer(x, mem, mem_norm, norm_mix, w_in, w_group, scale, w_mem_kv, w_out, norm_ffn, w_gu, w_down):
    b, s, _ = x.shape
    proj = rmsnorm(x, norm_mix) @ w_in
    u_pool, q_mem = proj[..., :POOL_WIDTH], proj[..., POOL_WIDTH:]
    pooled = multiscale_pool(u_pool).reshape(b, s, N_POOL_GROUPS, POOL_GROUP)
    grouped = jnp.einsum("bsgc,gcd->bsgd", pooled, w_group).reshape(b, s, POOL_WIDTH) * scale
    mk, mv = memory_kv(mem, mem_norm, w_mem_kv)
    mem_out = memory_attention(q_mem, mk, mv)
    x = x + jnp.concatenate([grouped, mem_out], axis=-1) @ w_out
    return x + swiglu(rmsnorm(x, norm_ffn), w_gu, w_down)


def sb_layer(x, mem, k_sh, v_sh, mem_norm, norm_mix, w_q, w_mem_kv, w_out, norm_ffn, w_gu, w_down):
    b, s, _ = x.shape
    proj = rmsnorm(x, norm_mix) @ w_q
    q_sb = proj[..., :SB_WIDTH].reshape(b, s, N_SB_HEADS, HEAD_DIM).transpose(0, 2, 1, 3)
    q_mem = proj[..., SB_WIDTH:]
    sb_out = stick_breaking_attention(q_sb, k_sh, v_sh).transpose(0, 2, 1, 3).reshape(b, s, SB_WIDTH)
    mk, mv = memory_kv(mem, mem_norm, w_mem_kv)
    mem_out = memory_attention(q_mem, mk, mv)
    x = x + jnp.concatenate([sb_out, mem_out], axis=-1) @ w_out
    return x + swiglu(rmsnorm(x, norm_ffn), w_gu, w_down)


def setup_inputs(seed: int = 0) -> dict:
    key = jax.random.key(seed)
    ks = jax.random.split(key, 24)
    f32 = jnp.float32
    out_gain = (2.0 * DEPTH) ** -0.5

    def w(k, shape, fan_in, gain=1.0):
        return jax.random.normal(k, shape, f32) * (gain * fan_in ** -0.5)

    def g(k, shape):
        return 1.0 + 0.05 * jax.random.normal(k, shape, f32)

    na, nb = N_A_LAYERS, N_B_LAYERS
    return {
        "x": jax.random.normal(ks[0], (BATCH, SEQ, D_MODEL), f32),
        "mem": jax.random.normal(ks[1], (BATCH, MEM_TOKENS, D_MODEL), f32),
        "mem_norm": g(ks[2], (D_MODEL,)),
        "a_norm_mix": g(ks[3], (na, D_MODEL)),
        "a_w_in": w(ks[4], (na, D_MODEL, MIX_WIDTH), D_MODEL),
        "a_w_group": w(ks[5], (na, N_POOL_GROUPS, POOL_GROUP, POOL_GROUP), POOL_GROUP),
        "a_scale": g(ks[6], (na, POOL_WIDTH)),
        "a_w_mem_kv": w(ks[7], (na, D_MODEL, 2 * MEM_WIDTH), D_MODEL),
        "a_w_out": w(ks[8], (na, MIX_WIDTH, D_MODEL), MIX_WIDTH, out_gain),
        "a_norm_ffn": g(ks[9], (na, D_MODEL)),
        "a_w_gu": w(ks[10], (na, D_MODEL, 2 * D_FF), D_MODEL),
        "a_w_down": w(ks[11], (na, D_FF, D_MODEL), D_FF, out_gain),
        "kv_norm": g(ks[12], (D_MODEL,)),
        "w_kv": w(ks[13], (D_MODEL, 2 * SB_WIDTH), D_MODEL),
        "b_norm_mix": g(ks[14], (nb, D_MODEL)),
        "b_w_q": w(ks[15], (nb, D_MODEL, MIX_WIDTH), D_MODEL),
        "b_w_mem_kv": w(ks[16], (nb, D_MODEL, 2 * MEM_WIDTH), D_MODEL),
        "b_w_out": w(ks[17], (nb, MIX_WIDTH, D_MODEL), MIX_WIDTH, out_gain),
        "b_norm_ffn": g(ks[18], (nb, D_MODEL)),
        "b_w_gu": w(ks[19], (nb, D_MODEL, 2 * D_FF), D_MODEL),
        "b_w_down": w(ks[20], (nb, D_FF, D_MODEL), D_FF, out_gain),
        "final_norm": g(ks[21], (D_MODEL,)),
    }


def reference(x, mem, mem_norm, a_norm_mix, a_w_in, a_w_group, a_scale, a_w_mem_kv, a_w_out,
              a_norm_ffn, a_w_gu, a_w_down, kv_norm, w_kv, b_norm_mix, b_w_q, b_w_mem_kv,
              b_w_out, b_norm_ffn, b_w_gu, b_w_down, final_norm):
    b, s, _ = x.shape
    k_sh = v_sh = None
    for layer in range(DEPTH):
        if layer < N_A_LAYERS:
            i = layer
            x = pool_layer(x, mem, mem_norm, a_norm_mix[i], a_w_in[i], a_w_group[i], a_scale[i],
                           a_w_mem_kv[i], a_w_out[i], a_norm_ffn[i], a_w_gu[i], a_w_down[i])
        else:
            if layer == N_A_LAYERS:
                kv = rmsnorm(x, kv_norm) @ w_kv
                k_sh = kv[..., :SB_WIDTH].reshape(b, s, N_SB_HEADS, HEAD_DIM).transpose(0, 2, 1, 3)
                v_sh = kv[..., SB_WIDTH:].reshape(b, s, N_SB_HEADS, HEAD_DIM).transpose(0, 2, 1, 3)
            j = layer - N_A_LAYERS
            x = sb_layer(x, mem, k_sh, v_sh, mem_norm, b_norm_mix[j], b_w_q[j], b_w_mem_kv[j],
                         b_w_out[j], b_norm_ffn[j], b_w_gu[j], b_w_down[j])
    return rmsnorm(x, final_norm)
```

```python
from contextlib import ExitStack

import numpy as np
import concourse.bass as bass
import concourse.mybir as mybir
from concourse.bass_utils import run_bass_kernel_spmd

F32 = mybir.dt.float32
BF16 = mybir.dt.bfloat16
AF = mybir.ActivationFunctionType
ALU = mybir.AluOpType

D = 1024
NCH = 8
T = 512
DFF = 2816
NF = 22
NHEAD = 12
EPS = 1e-6
FILL = 2
FILL_N = 256
RPULL = 4
WAIT_NORM = 4
WAIT_POOL = 5
PULL_A = 2
NSLOT = 4
PIECE = 16
CONVB = 32
NBLK = 1408
NPIECE = NBLK // PIECE
NCONV = NBLK // CONVB
SEC_A = (0, 656)
SEC_KVQ = (656, 816)
SEC_B = (816, 1408)

O_WIN = 0
O_WOUTA = 64
O_WGUA = 128
O_WDA = 480
O_WK = 656
O_WV = 704
O_WQ = 752
O_WOUTB = 816
O_WGUB = 880
O_WDB = 1232

WIN_ORDER = (6, 7, 4, 5, 0, 1, 2, 3)

G_AMIX, G_AFFN, G_KV, G_BMIX, G_BFFN, G_FINAL, G_MEM = range(7)
G_SCALE = 56


class Tile:
    __slots__ = ("name", "w", "readers")

    def __init__(self, name):
        self.name = name
        self.w = None
        self.readers = []


class DmaSem:
    __slots__ = ("name", "count", "handle")

    def __init__(self, name):
        self.name = name
        self.count = 0
        self.handle = None


class _Instr:
    __slots__ = ("fn", "waits", "late", "signal", "dsem", "idx")

    def __init__(self, fn):
        self.fn = fn
        self.waits = []
        self.late = []
        self.signal = False
        self.dsem = None
        self.idx = 0


ENGS = ("pe", "act", "dve", "pool", "sp")
LATE_DEFAULT = 1


class Prog:
    def __init__(self):
        self.ins = {e: [] for e in ENGS}
        self.clock = {e: {} for e in ENGS}
        self.dsems = []

    def dsem(self, name):
        d = DmaSem(name)
        self.dsems.append(d)
        return d

    def _record(self, eng, fn, reads, writes, dsem=None, update=True, nlhs=None):
        ins = _Instr(fn)
        lst = self.ins[eng]
        lst.append(ins)
        ins.idx = len(lst)
        clock = self.clock[eng]
        need = []
        early_keys = set()
        for n, t in enumerate(reads):
            if t.w is not None:
                need.append(t.w)
                if nlhs is not None and n < nlhs:
                    early_keys.add(t.w[0])
        for t in writes:
            if t.w is not None:
                need.append(t.w)
            need.extend(t.readers)
        best = {}
        for ev in need:
            key, val, evclock = ev
            if key == "pe" and eng == "pe":
                continue
            if clock.get(key, 0) >= val:
                continue
            if best.get(key, (0, None))[0] < val:
                best[key] = (val, evclock)
        for key, (val, evclock) in best.items():
            if clock.get(key, 0) >= val:
                continue
            if nlhs is not None and key not in early_keys:
                ins.late.append((key, val))
            else:
                ins.waits.append((key, val))
            if isinstance(key, str):
                self.ins[key][val - 1].signal = True
            for k2, v2 in evclock.items():
                if clock.get(k2, 0) < v2:
                    clock[k2] = v2
            clock[key] = max(clock.get(key, 0), val)
        if dsem is not None:
            dsem.count += 16
            ins.dsem = dsem
            evclock = dict(clock)
            evclock[dsem] = dsem.count
            ev = (dsem, dsem.count, evclock)
        else:
            evclock = dict(clock)
            evclock[eng] = ins.idx
            ev = (eng, ins.idx, evclock)
        if update:
            for t in writes:
                t.w = ev
                t.readers = []
            for t in reads:
                t.readers.append(ev)
        return ins

    def op(self, eng, fn, reads=(), writes=(), nlhs=None):
        if eng == "pe" and nlhs is None:
            nlhs = LATE_DEFAULT
        if eng != "pe":
            nlhs = None
        return self._record(eng, fn, list(reads), list(writes), nlhs=nlhs)

    def dma(self, eng, dsem, out_ap, in_ap, reads=(), writes=()):
        def fn(e, out_ap=out_ap, in_ap=in_ap):
            return e.dma_start(out=out_ap, in_=in_ap)

        return self._record(eng, fn, list(reads), list(writes), dsem=dsem)

    def wait_all(self, eng, tiles):
        return self._record(eng, None, list(tiles), list(tiles), update=False)

    def emit(self, nc):
        with ExitStack() as st:
            sems = {}
            for e in ENGS:
                sems[e] = st.enter_context(nc.semaphore("s_" + e))
            for d in self.dsems:
                d.handle = st.enter_context(nc.semaphore("d_" + d.name))
            block = st.enter_context(nc.Block())
            rank = {}
            for e in ENGS:
                r = 0
                rk = []
                for ins in self.ins[e]:
                    if ins.signal:
                        r += 1
                    rk.append(r)
                rank[e] = rk

            def semval(key, val):
                if isinstance(key, str):
                    return sems[key], rank[key][val - 1]
                return key.handle, val

            def run(e, eh):
                for ins in self.ins[e]:
                    late = list(ins.late)
                    late.sort(key=lambda kv: 1 if isinstance(kv[0], str) and kv[0] in ("dve", "act") else 0)
                    attach = late.pop() if (late and ins.fn is not None) else None
                    for key, val in list(ins.waits) + late:
                        sh, sv = semval(key, val)
                        eh.wait_ge(sh, sv)
                    if ins.fn is None:
                        continue
                    bi = ins.fn(eh)
                    if attach is not None:
                        sh, sv = semval(*attach)
                        bi._wait_ge(sh, sv)
                    if ins.dsem is not None:
                        bi.then_inc(ins.dsem.handle, 16)
                    elif ins.signal:
                        bi.then_inc(sems[e], 1)

            @block.tensor
            def _(eh):
                run("pe", eh)

            @block.scalar
            def _(eh):
                run("act", eh)

            @block.vector
            def _(eh):
                run("dve", eh)

            @block.gpsimd
            def _(eh):
                run("pool", eh)

            @block.sync
            def _(eh):
                run("sp", eh)


def build_nc(NT):
    S = NT * T
    NKB = S // 128
    nc = bass.Bass("TRN2", target_bir_lowering=False)
    xT_d = nc.dram_tensor("xT", [D, S], F32, kind="ExternalInput").ap()
    memT_d = nc.dram_tensor("memT", [D, 256], F32, kind="ExternalInput").ap()
    wall_d = nc.dram_tensor("wall", [128, NBLK * 128], F32, kind="ExternalInput").ap()
    wgrp_d = nc.dram_tensor("wgrp", [128, 14 * 128], F32, kind="ExternalInput").ap()
    wmem_d = nc.dram_tensor("wmem", [128, 2 * 8 * 512], F32, kind="ExternalInput").ap()
    cst_d = nc.dram_tensor("cst", [128, 8 * 128], F32, kind="ExternalInput").ap()
    gvec_d = nc.dram_tensor("gvec", [128, 64], F32, kind="ExternalInput").ap()
    fac_d = nc.dram_tensor("fac", [128, 64], F32, kind="ExternalInput").ap()
    outT_d = nc.dram_tensor("outT", [D, S], F32, kind="ExternalOutput").ap()
    wsc_d = nc.dram_tensor("wsc", [128, NBLK * 128], BF16).ap()
    kTs_d = nc.dram_tensor("kTs", [6 * 128, S], BF16).ap()

    P = Prog()
    with ExitStack() as st:
        def sb(name, shape, dt):
            return st.enter_context(nc.sbuf_tensor(name, shape, dt))

        v_sb = sb("vc", [128, NKB * 768], BF16)
        kst_sb = sb("kst", [128, 2 * S], BF16)
        xs = [sb(f"x{p}", [128, NCH * T], F32) for p in range(2)]
        h_sb = sb("h", [128, NCH * T], BF16)
        q_sb = sb("q", [128, NCH * T], BF16)
        cats = [sb(f"cat{p}", [128, NCH * T], BF16) for p in range(2)]
        big_sb = sb("big", [128, 5632], F32)
        sbt_sb = sb("sbt", [128, 3840], F32)
        ring_sb = sb("ring", [128, NSLOT * PIECE * 128], BF16)
        wg_sb = sb("wg", [128, 14 * 128], BF16)
        mk_sb = sb("mk", [128, 2 * 2 * 256], BF16)
        mvp_sb = sb("mvp", [128, 2 * 2 * 4 * 128], BF16)
        cst_sb = sb("cstb", [128, 8 * 128], BF16)
        gvec_sb = sb("gvec_s", [128, 64], F32)
        fac_sb = sb("fac_s", [128, 64], F32)
        halo_sb = sb("halo", [128, 6 * 16], F32)
        rkv_sb = sb("rkv", [128, T], F32)
        Fs = [sb(f"fs{k}", [128, 528], F32) for k in range(4)]
        Bs = [sb(f"bs{k}", [128, 512], BF16) for k in range(4)]
        ps = [st.enter_context(nc.psum_tensor(f"ps{k}", [128, 512], F32)) for k in range(8)]

        hid_ap = big_sb[:].bitcast(BF16)
        big_bf = hid_ap
        pooled_ap = big_sb[:, 3168:3168 + 1536].bitcast(BF16)
        sbt_bf = sbt_sb[:].bitcast(BF16)

        X = [[Tile(f"x{p}_{c}") for c in range(NCH)] for p in range(2)]
        CAT = [[Tile(f"cat{p}_{c}") for c in range(NCH)] for p in range(2)]
        H = [Tile(f"h{c}") for c in range(NCH)]
        Q = [Tile(f"q{c}") for c in range(NCH)]
        HID = [Tile(f"hid{f}") for f in range(NF)]
        U = [Tile(f"u{c}") for c in range(6)]
        PL = [Tile(f"pl{c}") for c in range(6)]
        HQ = [Tile(f"hq{c}") for c in range(NCH)]
        OST = [Tile(f"ost{c}") for c in range(NCH)]
        RING = [Tile(f"ring{k}") for k in range(NSLOT)]
        FT = [Tile(f"F{k}") for k in range(4)]
        BT = [Tile(f"B{k}") for k in range(4)]
        PS = [Tile(f"PS{k}") for k in range(8)]
        KS = [Tile(f"ks{s}") for s in range(2)]
        KD = [Tile(f"kd{hc}") for hc in range(6)]
        VT = [Tile(f"v{kb}") for kb in range(NKB)]
        WG = Tile("wg"); MK = Tile("mk"); MVP = Tile("mvp"); CST = Tile("cst"); GV = Tile("gvec"); FAC = Tile("fac")
        RKV = Tile("rkv")
        WSC = [Tile(f"wsc{p}") for p in range(NPIECE)]
        OUTT = Tile("outT")
        HALO = [Tile(f"halo{c}") for c in range(6)]
        SE = [Tile(f"sbE{j}") for j in range(3)]
        SW = [Tile(f"sbW{j}") for j in range(2)]
        SS = [Tile(f"sbS{j}") for j in range(3)]
        SWT = [Tile(f"sbT{j}") for j in range(2)]

        BIGREG = []
        for c in range(6):
            BIGREG.append((U[c], c * 2112, (c + 1) * 2112, "u"))
            BIGREG.append((PL[c], 12672 + c * 1024, 12672 + (c + 1) * 1024, "u"))
        for f in range(NF):
            BIGREG.append((HID[f], f * 1024, (f + 1) * 1024, "hid"))
        for c in range(NCH):
            BIGREG.append((HQ[c], c * 1024, (c + 1) * 1024, "hq"))
            BIGREG.append((OST[c], c * 2048, (c + 1) * 2048, "out"))

        def big_ov(tile):
            for (t, lo, hi, fam) in BIGREG:
                if t is tile:
                    break
            return [t2 for (t2, lo2, hi2, fam2) in BIGREG if fam2 != fam and lo < hi2 and lo2 < hi]

        def sE(j, lo, hi):
            return sbt_sb[:, j * 512 + lo:j * 512 + hi]

        def sW(j, lo, hi):
            return sbt_sb[:, 1536 + j * 512 + lo:1536 + j * 512 + hi]

        def sS(j, lo, hi):
            return sbt_bf[:, 5120 + j * 512 + lo:5120 + j * 512 + hi]

        def sT(j, lo, hi):
            return sbt_bf[:, 6656 + j * 512 + lo:6656 + j * 512 + hi]

        def xa(p, c, lo=0, hi=T):
            return xs[p][:, c * T + lo:c * T + hi]

        def cata(p, c, lo=0, hi=T, pr=slice(0, 128)):
            return cats[p][pr, c * T + lo:c * T + hi]

        def ha(c, lo=0, hi=T):
            return h_sb[:, c * T + lo:c * T + hi]

        def hqa(c):
            return big_bf[:, c * T:(c + 1) * T]

        def qa(c, lo=0, hi=T, pr=slice(0, 128)):
            return q_sb[pr, c * T + lo:c * T + hi]

        def hida(f):
            return hid_ap[:, f * T:(f + 1) * T]

        def ua(c, lo, hi, pr=slice(0, 128)):
            return big_sb[pr, c * 528 + lo:c * 528 + hi]

        def pla(c, lo=0, hi=T, pr=slice(0, 128)):
            return pooled_ap[pr, c * T + lo:c * T + hi]

        def csta(k):
            return cst_sb[:, k * 128:(k + 1) * 128]

        IDENT, ONESD, NEGINCL, NEGREST, MASKB, ONESP0, ONESP1, ZERO = [csta(k) for k in range(8)]

        def gcol(k, c):
            return gvec_sb[:, 8 * k + c:8 * k + c + 1]

        def mm(out_ap, lhsT, rhs, start, stop):
            return lambda e: e.matmul(out_ap, lhsT=lhsT, rhs=rhs, start=start, stop=stop)

        d_cst = P.dsem("cst"); d_gv = P.dsem("gv"); d_fac = P.dsem("fac"); d_wg = P.dsem("wg")
        d_wm = [P.dsem("wm0"), P.dsem("wm1")]
        d_x = [P.dsem("x0"), P.dsem("x1")]
        d_out = P.dsem("out")
        d_ring = [P.dsem(f"ring{k}") for k in range(NSLOT)]
        d_ring2 = [P.dsem(f"ringb{k}") for k in range(NSLOT)]
        d_wst = [P.dsem(f"wst{k}") for k in range(NSLOT)]
        d_ks = [P.dsem("ks0"), P.dsem("ks1")]
        d_kd = [P.dsem(f"kd{hc}") for hc in range(6)]

        class Banks:
            def __init__(self, banks, statb):
                self.banks = banks
                self.statb = statb
                self.n = 0

            def get(self):
                k = self.banks[self.n % len(self.banks)]
                self.n += 1
                return k

        BK_MAIN = Banks((0, 1, 2, 3), 7)
        BK_A = Banks((0, 1, 2), 3)

        def sec_pieces(sec):
            return list(range(sec[0] // PIECE, sec[1] // PIECE))

        seq = []
        sec_start = {}
        sec_start[("A", 0)] = len(seq); seq += sec_pieces(SEC_A)
        for i in range(NT):
            sec_start[("KVQ", i)] = len(seq); seq += sec_pieces(SEC_KVQ)
            if i + 1 < NT:
                sec_start[("A", i + 1)] = len(seq); seq += sec_pieces(SEC_A)
            sec_start[("B", i)] = len(seq); seq += sec_pieces(SEC_B)
        SECS = {"A": SEC_A, "KVQ": SEC_KVQ, "B": SEC_B}
        wstate = {"issued": 0, "cur": -1}

        seen_piece = set()

        def issue_piece(g):
            p = seq[g]
            s = g % NSLOT
            slot = ring_sb[:, s * PIECE * 128:(s + 1) * PIECE * 128]
            if p not in seen_piece:
                seen_piece.add(p)
                P.dma("pool", d_ring[s], slot, wall_d[:, p * PIECE * 128:(p + 1) * PIECE * 128], reads=[], writes=[RING[s]])
                P.dma("sp", d_wst[s], wsc_d[:, p * PIECE * 128:(p + 1) * PIECE * 128], slot, reads=[RING[s]], writes=[WSC[p]])
            else:
                P.dma("sp", d_ring2[s], slot, wsc_d[:, p * PIECE * 128:(p + 1) * PIECE * 128], reads=[WSC[p]], writes=[RING[s]])

        def wblk(kind, tile, j, n=1):
            sec = SECS[kind]
            assert sec[0] <= j < sec[1] and (j % PIECE) + n <= PIECE
            g = sec_start[(kind, tile)] + (j - sec[0]) // PIECE
            assert g >= wstate["cur"], (kind, tile, j, g, wstate["cur"])
            if g > wstate["cur"]:
                wstate["cur"] = g
                while wstate["issued"] < min(g + NSLOT, len(seq)):
                    issue_piece(wstate["issued"])
                    wstate["issued"] += 1
            s = g % NSLOT
            off = s * PIECE * 128 + (j % PIECE) * 128
            return ring_sb[:, off:off + n * 128], RING[s]

        def drain(gen):
            if gen is None:
                return
            for _ in gen:
                pass

        P.dma("pool", d_cst, cst_sb[:], cst_d, writes=[CST])
        P.dma("sp", d_gv, gvec_sb[:], gvec_d, writes=[GV])
        P.dma("sp", d_fac, fac_sb[:], fac_d, writes=[FAC])
        P.dma("pool", d_wg, wg_sb[:], wgrp_d, writes=[WG])
        for L in range(2):
            P.dma("pool", d_wm[L], ring_sb[:, L * 4096:(L + 1) * 4096], wmem_d[:, L * 4096:(L + 1) * 4096],
                  writes=[RING[2 * L], RING[2 * L + 1]])
        P.dma("sp", d_x[0], xs[0][:, 0:2048].rearrange("p (c m) -> p c m", c=8),
              memT_d.rearrange("(c p) m -> p c m", p=128), writes=X[0][0:4])
        P.op("dve", lambda e: e.memset(mvp_sb[:], 0.0), writes=[MVP])
        P.op("dve", lambda e: e.memset(halo_sb[:], 0.0), writes=HALO)

        def memx(c):
            return xs[0][:, c * 256:(c + 1) * 256]

        def memxt(c):
            return X[0][c // 2]

        msp = BK_MAIN.get()
        for c in range(NCH):
            b = c % 2
            eng = "pool" if c % 2 == 0 else "dve"
            P.op(eng, lambda e, c=c, b=b: e.tensor_tensor(out=Bs[b][:, 0:256], in0=memx(c), in1=memx(c), op=ALU.mult),
                 reads=[memxt(c)], writes=[BT[b]])
            P.op("pe", mm(ps[msp][:, 0:256], ONESD, Bs[b][:, 0:256], c == 0, c == NCH - 1), reads=[CST, BT[b]], writes=[PS[msp]])
        P.op("act", lambda e: e.activation(out=Fs[3][:, 0:256], in_=ps[msp][:, 0:256], func=AF.Ln, bias=EPS),
             reads=[PS[msp]], writes=[FT[3]])
        P.op("act", lambda e: e.activation(out=Fs[2][:, 0:256], in_=Fs[3][:, 0:256], func=AF.Exp, scale=-0.5),
             reads=[FT[3]], writes=[FT[2]])
        for c in range(NCH):
            P.op("dve", lambda e, c=c: e.scalar_tensor_tensor(out=ha(c, 0, 256), in0=memx(c), scalar=gcol(G_MEM, c),
                                                                in1=Fs[2][:, 0:256], op0=ALU.mult, op1=ALU.mult),
                 reads=[memxt(c), GV, FT[2]], writes=[H[c]])
        for L in range(2):
            RL = [RING[2 * L], RING[2 * L + 1]]

            def wm(c, j, n=1, L=L):
                off = L * 4096 + c * 512 + j * 128
                return ring_sb[:, off:off + n * 128]
            for fc in range(2):
                k = BK_MAIN.get()
                for c in range(NCH):
                    P.op("pe", mm(ps[k][:, 0:256], wm(c, fc), ha(c, 0, 256), c == 0, c == NCH - 1),
                         reads=RL + [H[c]], writes=[PS[k]], nlhs=2)
                P.op("act", lambda e, k=k, L=L, fc=fc: e.activation(
                    out=mk_sb[:, (L * 2 + fc) * 256:(L * 2 + fc + 1) * 256], in_=ps[k][:, 0:256], func=AF.Copy),
                    reads=[PS[k]], writes=[MK])
            for mb in range(2):
                k = BK_MAIN.get()
                for c in range(NCH):
                    P.op("pe", mm(ps[k][:, 0:256], ha(c, mb * 128, mb * 128 + 128), wm(c, 2, 2), c == 0, c == NCH - 1),
                         reads=[H[c]] + RL, writes=[PS[k]])
                for hh in range(4):
                    base = ((L * 2 + mb) * 4 + hh) * 128 + (hh % 2) * 64
                    P.op("dve", lambda e, k=k, base=base, hh=hh: e.tensor_copy(
                        out=mvp_sb[:, base:base + 64], in_=ps[k][:, hh * 64:(hh + 1) * 64]),
                        reads=[PS[k]], writes=[MVP])

        def mka(L, hh, mb):
            hc, half = divmod(hh, 2)
            base = (L * 2 + hc) * 256 + mb * 128
            return mk_sb[half * 64:half * 64 + 64, base:base + 128]

        def mvpa(L, mb, hh):
            base = ((L * 2 + mb) * 4 + hh) * 128
            return mvp_sb[:, base:base + 128]

        pending = []

        def flush_pending():
            while pending:
                pending.pop(0)()

        sqrot = [0]

        def stat_chunk(c, k, src_ap, src_tile, eng=None, defer=False):
            b = sqrot[0] % 2
            sqrot[0] += 1
            if eng is None:
                eng = "pool" if c % 2 == 0 else "dve"
            P.op(eng, lambda e, c=c, b=b: e.tensor_tensor(out=Bs[b][:, 0:T], in0=src_ap(c), in1=src_ap(c), op=ALU.mult),
                 reads=[src_tile(c)], writes=[BT[b]])

            def do_mm():
                P.op("pe", mm(ps[k][:, 0:T], ONESD, Bs[b][:, 0:T], c == 0, c == NCH - 1),
                     reads=[CST, BT[b]], writes=[PS[k]])
            if defer:
                pending.append(do_mm)
            else:
                do_mm()

        def rstd_from(k, dst_ap=None, dst_tile=None):
            flush_pending()
            if dst_ap is None:
                dst_ap, dst_tile = Fs[2][:, 0:T], FT[2]
            P.op("act", lambda e: e.activation(out=Fs[3][:, 0:T], in_=ps[k][:, 0:T], func=AF.Ln, bias=EPS),
                 reads=[PS[k]], writes=[FT[3]])
            P.op("act", lambda e: e.activation(out=dst_ap, in_=Fs[3][:, 0:T], func=AF.Exp, scale=-0.5),
                 reads=[FT[3]], writes=[dst_tile])

        def norm_apply(p, gk, rap=None, rtile=None):
            if rap is None:
                rap, rtile = Fs[2][:, 0:T], FT[2]
            for c in range(NCH):
                P.op("dve", lambda e, c=c: e.scalar_tensor_tensor(out=ha(c), in0=xa(p, c), scalar=gcol(gk, c), in1=rap,
                                                                    op0=ALU.mult, op1=ALU.mult),
                     reads=[X[p][c], GV, rtile], writes=[H[c]])

        def proj(bk, kind, tile, o_base, oi, evac, sap, stl):
            k = bk.get()
            for c in range(NCH):
                wap, wt = wblk(kind, tile, o_base + oi * NCH + c)
                P.op("pe", mm(ps[k][:, :], wap, sap(c), c == 0, c == NCH - 1), reads=[wt, stl[c]], writes=[PS[k]])
                yield
            evac(k)

        def resid_add(p, oc, statb):
            def ev(k):
                P.op("dve", lambda e: e.tensor_tensor(out=xa(p, oc), in0=xa(p, oc), in1=ps[k][:, :], op=ALU.add),
                     reads=[X[p][oc], PS[k]], writes=[X[p][oc]])
                flush_pending()
                stat_chunk(oc, statb, lambda c: xa(p, c), lambda c: X[p][c], eng="pool", defer=True)
            return ev

        def mem_attention(L, p, OB, ZB, zbanks):
            for hc in range(2):
                pts = []
                for half in range(2):
                    hh = hc * 2 + half
                    pr = slice(half * 64, half * 64 + 64)
                    for mb in range(2):
                        zk = zbanks[len(pts) % len(zbanks)]
                        bslot = (hc * 4 + len(pts)) % 4
                        P.op("pe", mm(ps[zk][:, :], mka(L, hh, mb), qa(6 + hc, pr=pr), True, True),
                             reads=[MK, Q[6 + hc]], writes=[PS[zk]])
                        yield
                        P.op("act", lambda e, zk=zk, bslot=bslot: e.activation(out=Bs[bslot][:, :], in_=ps[zk][:, :], func=AF.Exp),
                             reads=[PS[zk]], writes=[BT[bslot]])
                        pts.append((bslot, hh, mb, half))
                for n, (bslot, hh, mb, half) in enumerate(pts):
                    P.op("pe", mm(ps[OB][:, :], mvpa(L, mb, hh), Bs[bslot][:, :], n == 0, n == 3),
                         reads=[MVP, BT[bslot]], writes=[PS[OB]])
                    yield
                for n, (bslot, hh, mb, half) in enumerate(pts):
                    P.op("pe", mm(ps[ZB][:, :], ONESP0 if half == 0 else ONESP1, Bs[bslot][:, :], n == 0, n == 3),
                         reads=[CST, BT[bslot]], writes=[PS[ZB]])
                    yield
                P.op("act", lambda e: e.activation(out=Fs[3][:, 0:T], in_=ps[ZB][:, :], func=AF.Ln), reads=[PS[ZB]], writes=[FT[3]])
                P.op("act", lambda e: e.activation(out=Fs[3][:, 0:T], in_=Fs[3][:, 0:T], func=AF.Exp, scale=-1.0), reads=[FT[3]], writes=[FT[3]])
                P.op("dve", lambda e, hc=hc: e.tensor_tensor(out=cata(p, 6 + hc), in0=ps[OB][:, :], in1=Fs[3][:, 0:T], op=ALU.mult),
                     reads=[PS[OB], FT[3]], writes=[CAT[p][6 + hc]])

        def ffn(bk, kind, tile, p, gk, o_gu, o_d, after_down=None):
            rstd_from(bk.statb)
            norm_apply(p, gk)
            yield ("wait", WAIT_NORM)
            for f in range(NF):
                kg = bk.get()
                ku = bk.get()
                for c in range(NCH):
                    wap, wt = wblk(kind, tile, o_gu + f * 16 + c)
                    P.op("pe", mm(ps[kg][:, :], wap, ha(c), c == 0, c == NCH - 1), reads=[wt, H[c]], writes=[PS[kg]])
                    yield
                for c in range(NCH):
                    wap, wt = wblk(kind, tile, o_gu + f * 16 + 8 + c)
                    P.op("pe", mm(ps[ku][:, :], wap, ha(c), c == 0, c == NCH - 1), reads=[wt, H[c]], writes=[PS[ku]])
                    yield
                fs = f % 2
                P.op("act", lambda e, kg=kg, fs=fs: e.activation(out=Fs[fs][:, 0:T], in_=ps[kg][:, :], func=AF.Exp, scale=-1.0),
                     reads=[PS[kg]], writes=[FT[fs]])
                P.op("act", lambda e, fs=fs: e.activation(out=Fs[fs][:, 0:T], in_=Fs[fs][:, 0:T], func=AF.Ln, bias=1.0),
                     reads=[FT[fs]], writes=[FT[fs]])
                P.op("act", lambda e, fs=fs: e.activation(out=Fs[fs][:, 0:T], in_=Fs[fs][:, 0:T], func=AF.Exp, scale=-1.0),
                     reads=[FT[fs]], writes=[FT[fs]])
                P.op("dve", lambda e, kg=kg, fs=fs: e.tensor_tensor(out=Fs[fs][:, 0:T], in0=Fs[fs][:, 0:T], in1=ps[kg][:, :], op=ALU.mult),
                     reads=[FT[fs], PS[kg]], writes=[FT[fs]])
                P.op("dve", lambda e, ku=ku, fs=fs, f=f: e.tensor_tensor(out=hida(f), in0=Fs[fs][:, 0:T], in1=ps[ku][:, :], op=ALU.mult),
                     reads=[FT[fs], PS[ku]], writes=[HID[f]] + big_ov(HID[f]))
            for oc in range(NCH):
                k = bk.get()
                for f in range(NF):
                    wap, wt = wblk(kind, tile, o_d + oc * NF + f)
                    P.op("pe", mm(ps[k][:, :], wap, hida(f), f == 0, f == NF - 1), reads=[wt, HID[f]], writes=[PS[k]])
                    yield
                resid_add(p, oc, bk.statb)(k)

        POOLCFG = [(0, 0, 128, 2), (1, 0, 128, 4), (2, 0, 128, 8), (3, 0, 128, 16),
                   (4, 0, 64, 2), (4, 64, 128, 4), (5, 0, 64, 8), (5, 64, 128, 16)]

        def pool_chunk(i, c, eng, fa, fb):
            cfgs = [cf for cf in POOLCFG if cf[0] == c]
            wmax = max(cf[3] for cf in cfgs)
            src_ap = lambda lo, hi: ua(c, lo, hi)
            src_t = U[c]
            slots = [fa, fb]
            sums = {}
            step = 1
            n = 0
            while step < wmax:
                dst = slots[n % 2]
                lo = 2 * step - 1
                P.op(eng, lambda e, dst=dst, lo=lo, step=step, src_ap=src_ap: e.tensor_tensor(
                    out=Fs[dst][:, lo:528], in0=src_ap(lo, 528), in1=src_ap(lo - step, 528 - step), op=ALU.add),
                    reads=[src_t], writes=[FT[dst]])
                step *= 2
                sums[step] = dst
                src_ap = (lambda lo, hi, dst=dst: Fs[dst][:, lo:hi])
                src_t = FT[dst]
                n += 1
            for (_, p0, p1, w) in cfgs:
                s = sums[w]
                pr = slice(p0, p1)
                widx = {2: 0, 4: 1, 8: 2, 16: 3}[w]
                if i == 0:
                    P.op(eng, lambda e, s=s, pr=pr, widx=widx: e.tensor_tensor(
                        out=Fs[s][pr, 16:32], in0=Fs[s][pr, 16:32], in1=fac_sb[pr, widx * 16:(widx + 1) * 16], op=ALU.mult),
                        reads=[FT[s], FAC], writes=[FT[s]])
                P.op("dve", lambda e, s=s, pr=pr, w=w: e.scalar_tensor_tensor(
                    out=pla(c, pr=pr), in0=Fs[s][pr, 16:528], scalar=1.0 / w, in1=ua(c, 16, 528, pr=pr),
                    op0=ALU.mult, op1=ALU.subtract),
                    reads=[FT[s], U[c]], writes=[PL[c]] + big_ov(PL[c]))
            P.op(eng, lambda e: e.tensor_copy(out=halo_sb[:, c * 16:(c + 1) * 16], in_=ua(c, 512, 528)), reads=[U[c]], writes=[HALO[c]])

        GROUP_PLAN = [(0, [(0, 0), (1, 4)]), (1, [(2, 1), (3, 4)]), (2, [(4, 2), (5, 5)]), (3, [(6, 3), (7, 5)]),
                      (4, [(8, 0), (9, 1), (10, 4)]), (5, [(11, 2), (12, 3), (13, 5)])]

        def layer_a(i):
            p = i % 2
            bk = BK_A
            t0 = i * T
            P.dma("sp", d_x[p], xs[p][:].rearrange("p (c t) -> p c t", c=NCH),
                  xT_d.rearrange("(c p) s -> p c s", p=128)[:, :, t0:t0 + T], reads=[], writes=X[p])
            k = bk.get()
            for c in range(NCH):
                stat_chunk(c, k, lambda c: xa(p, c), lambda c: X[p][c])
                yield
            rstd_from(k)
            norm_apply(p, G_AMIX)
            yield ("wait", WAIT_NORM)
            for oi, oc in enumerate(WIN_ORDER):
                if oc < 6:
                    def ev(k, oc=oc):
                        P.op("dve", lambda e: e.tensor_copy(out=ua(oc, 16, 528), in_=ps[k][:, :]),
                             reads=[PS[k]], writes=[U[oc]] + big_ov(U[oc]))
                        P.op("pool", lambda e: e.tensor_copy(out=ua(oc, 0, 16), in_=halo_sb[:, oc * 16:(oc + 1) * 16]),
                             reads=[HALO[oc]], writes=[U[oc]])
                else:
                    def ev(k, oc=oc):
                        P.op("dve", lambda e: e.tensor_scalar_mul(out=qa(oc), in0=ps[k][:, :], scalar1=0.125),
                             reads=[PS[k]], writes=[Q[oc]])
                yield from proj(bk, "A", i, O_WIN, oi, ev, ha, H)
            for c in (4, 5):
                pool_chunk(i, c, "pool", 2, 3)
            for c in range(4):
                pool_chunk(i, c, "dve", 0, 1)
            yield from mem_attention(0, p, 0, 1, (2,))
            bk.n = 0
            yield ("wait", WAIT_POOL)
            for (oc, plan) in GROUP_PLAN:
                k = bk.get()
                for n, (bi, pc) in enumerate(plan):
                    P.op("pe", mm(ps[k][:, :], wg_sb[:, bi * 128:(bi + 1) * 128], pla(pc), n == 0, n == len(plan) - 1),
                         reads=[WG, PL[pc]], writes=[PS[k]])
                    yield
                P.op("dve", lambda e, k=k, oc=oc: e.tensor_scalar_mul(out=cata(p, oc), in0=ps[k][:, :],
                                                                       scalar1=gvec_sb[:, G_SCALE + oc:G_SCALE + oc + 1]),
                     reads=[PS[k], GV], writes=[CAT[p][oc]])
            for oc in range(NCH):
                yield from proj(bk, "A", i, O_WOUTA, oc, resid_add(p, oc, bk.statb), lambda c: cata(p, c), CAT[p])
            yield from ffn(bk, "A", i, p, G_AFFN, O_WGUA, O_WDA)
            rstd_from(bk.statb, rkv_sb[:, :], RKV)
            yield

        def kvq(i):
            p = i % 2
            bk = BK_MAIN
            t0 = i * T
            norm_apply(p, G_KV, rkv_sb[:, :], RKV)
            for c in range(NCH):
                P.op("dve", lambda e, c=c: e.scalar_tensor_tensor(out=hqa(c), in0=xa(p, c), scalar=gcol(G_BMIX, c), in1=rkv_sb[:, :],
                                                                    op0=ALU.mult, op1=ALU.mult),
                     reads=[X[p][c], GV, RKV], writes=[HQ[c]] + big_ov(HQ[c]))
            for hc in range(6):
                def ev(k, hc=hc):
                    eng = "dve" if hc % 2 == 0 else "act"
                    if eng == "dve":
                        P.op("dve", lambda e: e.tensor_copy(out=cata(p, hc), in_=ps[k][:, :]), reads=[PS[k]], writes=[CAT[p][hc]])
                    else:
                        P.op("act", lambda e: e.activation(out=cata(p, hc), in_=ps[k][:, :], func=AF.Copy), reads=[PS[k]], writes=[CAT[p][hc]])
                    P.dma("sp", d_kd[hc], kTs_d[hc * 128:(hc + 1) * 128, t0:t0 + T], cata(p, hc), reads=[CAT[p][hc]], writes=[KD[hc]])
                yield from proj(bk, "KVQ", i, O_WK, hc, ev, ha, H)
            kbanks = [bk.get() for _ in range(4)]
            for c in range(NCH):
                wap, wt = wblk("KVQ", i, O_WV + c * 4, 4)
                for tb in range(4):
                    P.op("pe", mm(ps[kbanks[tb]][:, :], ha(c, tb * 128, tb * 128 + 128), wap, c == 0, c == NCH - 1),
                         reads=[H[c], wt], writes=[PS[kbanks[tb]]])
                    yield
            for tb in range(4):
                kb = 4 * i + tb
                if tb % 2 == 0:
                    P.op("dve", lambda e, tb=tb, kb=kb: e.tensor_copy(out=v_sb[:, kb * 768:kb * 768 + 512], in_=ps[kbanks[tb]][:, :]),
                         reads=[PS[kbanks[tb]]], writes=[VT[kb]])
                else:
                    P.op("act", lambda e, tb=tb, kb=kb: e.activation(out=v_sb[:, kb * 768:kb * 768 + 512], in_=ps[kbanks[tb]][:, :], func=AF.Copy),
                         reads=[PS[kbanks[tb]]], writes=[VT[kb]])
            k2 = [bk.get() for _ in range(4)]
            for c in range(NCH):
                wap, wt = wblk("KVQ", i, O_WV + 32 + c * 2, 2)
                for tb in range(4):
                    kk = k2[tb]
                    col = 0
                    P.op("pe", mm(ps[kk][:, col:col + 256], ha(c, tb * 128, tb * 128 + 128), wap, c == 0, c == NCH - 1),
                         reads=[H[c], wt], writes=[PS[kk]])
                    yield
            for tb in range(4):
                kb = 4 * i + tb
                kk = k2[tb]
                col = 0
                P.op("act", lambda e, kk=kk, col=col, kb=kb: e.activation(out=v_sb[:, kb * 768 + 512:kb * 768 + 768], in_=ps[kk][:, col:col + 256], func=AF.Copy),
                     reads=[PS[kk]], writes=[VT[kb]])
            for oc in range(NCH):
                def ev(k, oc=oc):
                    if oc % 2 == 0:
                        P.op("act", lambda e: e.activation(out=qa(oc), in_=ps[k][:, :], func=AF.Copy, scale=0.125),
                             reads=[PS[k]], writes=[Q[oc]])
                    else:
                        P.op("dve", lambda e: e.tensor_scalar_mul(out=qa(oc), in0=ps[k][:, :], scalar1=0.125),
                             reads=[PS[k]], writes=[Q[oc]])
                yield from proj(bk, "KVQ", i, O_WQ, oc, ev, hqa, HQ)

        def sb_tile(i, gen):
            p = i % 2
            nkb = 4 * (i + 1)
            npre = 4 * i * 128
            steps = [(hh, kb) for hh in range(NHEAD) for kb in range(nkb - 1, -1, -1)]
            N = len(steps)
            info = {}
            ACC, OB = 4, 5
            live = {"gen": gen}

            live["skip"] = 0

            def pull():
                g = live["gen"]
                if g is None or live["skip"] > 0:
                    return False
                try:
                    tok = next(g)
                    if isinstance(tok, tuple):
                        live["skip"] = tok[1]
                    return True
                except StopIteration:
                    live["gen"] = None
                    return False

            def load_kstage(hc):
                s = hc % 2
                if npre > 0:
                    P.dma("sp", d_ks[s], kst_sb[:, s * S:s * S + npre], kTs_d[hc * 128:(hc + 1) * 128, 0:npre],
                          reads=[KD[hc]], writes=[KS[s]])
                P.dma("sp", d_ks[s], kst_sb[:, s * S + npre:s * S + npre + T], cata(p, hc), reads=[CAT[p][hc]], writes=[KS[s]])

            load_kstage(0)
            load_kstage(1)

            def hparams(hh):
                hc, half = divmod(hh, 2)
                pr = slice(half * 64, half * 64 + 64)
                return hc, pr

            def S1(n):
                hh, kb = steps[n]
                hc, pr = hparams(hh)
                if hh % 2 == 0 and kb == nkb - 1 and 1 <= hc and hc + 1 < 6:
                    load_kstage(hc + 1)
                j = kb - 4 * i
                c0 = 128 * j if j > 0 else 0
                zk = 6 + n % 2
                es = n % 3
                bs = n % 3
                s = hc % 2
                ka = kst_sb[pr, s * S + kb * 128:s * S + kb * 128 + 128]
                if j >= 0:
                    P.op("pe", mm(ps[zk][:, c0:c0 + 128], IDENT, MASKB, True, False), reads=[CST], writes=[PS[zk]])
                    P.op("pe", mm(ps[zk][:, c0:c0 + 128], ka, qa(hc, c0, c0 + 128, pr=pr), False, True),
                         reads=[KS[s], Q[hc]], writes=[PS[zk]])
                    if c0 + 128 < T:
                        P.op("pe", mm(ps[zk][:, c0 + 128:T], ka, qa(hc, c0 + 128, T, pr=pr), True, True),
                             reads=[KS[s], Q[hc]], writes=[PS[zk]])
                else:
                    P.op("pe", mm(ps[zk][:, :], ka, qa(hc, pr=pr), True, True), reads=[KS[s], Q[hc]], writes=[PS[zk]])
                P.op("act", lambda e: e.activation(out=sE(es, c0, T), in_=ps[zk][:, c0:T], func=AF.Exp),
                     reads=[PS[zk]], writes=[SE[es]])
                P.op("act", lambda e: e.activation(out=sS(bs, c0, T), in_=sE(es, c0, T), func=AF.Ln, bias=1.0),
                     reads=[SE[es]], writes=[SS[bs]])
                info[n] = (c0, es, bs)

            def S2(n):
                hh, kb = steps[n]
                hc, pr = hparams(hh)
                c0, es, bs = info[n]
                ws = n % 2
                wb = n % 2
                if kb == nkb - 1:
                    P.op("pe", mm(ps[ACC][:, :], ZERO, qa(0), True, False), reads=[CST, Q[0]], writes=[PS[ACC]])
                    P.op("pe", mm(ps[OB][:, :], ZERO, qa(0), True, False), reads=[CST, Q[0]], writes=[PS[OB]])
                P.op("pe", mm(ps[ACC][:, c0:T], NEGINCL, sS(bs, c0, T), False, True), reads=[CST, SS[bs]], writes=[PS[ACC]])
                P.op("act", lambda e: e.activation(out=sW(ws, c0, T), in_=ps[ACC][:, c0:T], func=AF.Exp),
                     reads=[PS[ACC]], writes=[SW[ws]])
                P.op("dve", lambda e: e.tensor_tensor(out=sT(wb, c0, T), in0=sE(es, c0, T), in1=sW(ws, c0, T), op=ALU.mult),
                     reads=[SE[es], SW[ws]], writes=[SWT[wb]])

            def S3a(n):
                hh, kb = steps[n]
                c0, es, bs = info[n]
                if kb > 0:
                    P.op("pe", mm(ps[ACC][:, c0:T], NEGREST, sS(bs, c0, T), False, True), reads=[CST, SS[bs]], writes=[PS[ACC]])

            def S3b(n):
                hh, kb = steps[n]
                hc, pr = hparams(hh)
                c0, es, bs = info[n]
                wb = n % 2
                P.op("pe", mm(ps[OB][:, c0:T], v_sb[:, kb * 768 + hc * 128:kb * 768 + hc * 128 + 128], sT(wb, c0, T), False, kb == 0),
                     reads=[VT[kb], SWT[wb]], writes=[PS[OB]])
                if kb == 0:
                    P.op("dve", lambda e: e.tensor_copy(out=cata(p, hc, pr=pr), in_=ps[OB][pr, :]), reads=[PS[OB]], writes=[CAT[p][hc]])

            def filler():
                P.op("pe", mm(ps[OB][:, 0:FILL_N], ZERO, qa(0, 0, FILL_N), False, False), reads=[CST, Q[0]], writes=[PS[OB]])

            def gap_work(npull, nfill):
                got = 0
                for _ in range(npull):
                    if pull():
                        got += 1
                if live["gen"] is None or live["skip"] > 0:
                    for _ in range(max(0, nfill - got)):
                        filler()

            for k in range(-2, N):
                if live["skip"] > 0:
                    live["skip"] -= 1
                if 0 <= k + 2 < N:
                    S1(k + 2)
                if 0 <= k < N:
                    gap_work(PULL_A, 2)
                    S3a(k)
                head_start = (0 <= k + 1 < N) and steps[k + 1][1] == nkb - 1
                if head_start and 0 <= k < N:
                    gap_work(RPULL - PULL_A, 1)
                    S3b(k)
                    S2(k + 1)
                else:
                    if 0 <= k + 1 < N:
                        S2(k + 1)
                    if 0 <= k < N:
                        gap_work(RPULL - PULL_A, 1)
                        S3b(k)
            return live["gen"]

        drain(layer_a(0))
        for i in range(NT):
            p = i % 2
            t0 = i * T
            drain(kvq(i))
            drain(mem_attention(1, p, 4, 5, (6, 7)))
            gen = layer_a(i + 1) if i + 1 < NT else None
            gen = sb_tile(i, gen)
            drain(gen)
            for oc in range(NCH):
                drain(proj(BK_MAIN, "B", i, O_WOUTB, oc, resid_add(p, oc, BK_MAIN.statb), lambda c: cata(p, c), CAT[p]))
            drain(ffn(BK_MAIN, "B", i, p, G_BFFN, O_WGUB, O_WDB))
            rstd_from(BK_MAIN.statb)
            for c in range(NCH):
                P.op("dve", lambda e, c=c, p=p: e.scalar_tensor_tensor(out=big_sb[:, c * T:(c + 1) * T], in0=xa(p, c), scalar=gcol(G_FINAL, c),
                                                                    in1=Fs[2][:, 0:T], op0=ALU.mult, op1=ALU.mult),
                     reads=[X[p][c], GV, FT[2]], writes=[OST[c]] + big_ov(OST[c]))
            P.dma("sp", d_out, outT_d.rearrange("(c p) s -> p c s", p=128)[:, :, t0:t0 + T],
                  big_sb[:, 0:NCH * T].rearrange("p (c t) -> p c t", c=NCH), reads=OST, writes=[OUTT])
        P.wait_all("sp", [OUTT])
        P.emit(nc)
    return nc


def _blk(W, kc, mc):
    return W[kc * 128:(kc + 1) * 128, mc * 128:(mc + 1) * 128]


def _pool_perm():
    perm = []
    for g in range(4):
        perm.extend(range(192 * g, 192 * g + 128))
    for g in range(4):
        perm.extend(range(192 * g + 128, 192 * g + 192))
    return np.array(perm, dtype=np.int64)


def pack_weights(inp):
    f32 = np.float32
    perm = _pool_perm()
    full_perm = np.concatenate([perm, np.arange(768, 1024)])
    blocks = []
    w_in = np.asarray(inp["a_w_in"][0], f32)[:, full_perm]
    for oc in WIN_ORDER:
        for c in range(8):
            blocks.append(_blk(w_in, c, oc))
    w_out_a = np.asarray(inp["a_w_out"][0], f32)[full_perm, :]
    for oc in range(8):
        for c in range(8):
            blocks.append(_blk(w_out_a, c, oc))

    def ffn_blocks(w_gu, w_d):
        for f in range(NF):
            for c in range(8):
                blocks.append(_blk(w_gu, c, f))
            for c in range(8):
                blocks.append(_blk(w_gu, c, NF + f))
        for oc in range(8):
            for f in range(NF):
                blocks.append(_blk(w_d, f, oc))

    ffn_blocks(np.asarray(inp["a_w_gu"][0], f32), np.asarray(inp["a_w_down"][0], f32))
    w_kv = np.asarray(inp["w_kv"], f32)
    for hc in range(6):
        for c in range(8):
            blocks.append(_blk(w_kv, c, hc))
    for c in range(8):
        for j in range(4):
            blocks.append(_blk(w_kv, c, 6 + j))
    for c in range(8):
        for j in range(2):
            blocks.append(_blk(w_kv, c, 10 + j))
    w_q = np.asarray(inp["b_w_q"][0], f32)
    for oc in range(8):
        for c in range(8):
            blocks.append(_blk(w_q, c, oc))
    w_out_b = np.asarray(inp["b_w_out"][0], f32)
    for oc in range(8):
        for c in range(8):
            blocks.append(_blk(w_out_b, c, oc))
    ffn_blocks(np.asarray(inp["b_w_gu"][0], f32), np.asarray(inp["b_w_down"][0], f32))
    assert len(blocks) == NBLK
    wall = np.ascontiguousarray(np.stack(blocks, axis=1).reshape(128, NBLK * 128))

    wgp = np.asarray(inp["a_w_group"][0], f32)
    z = np.zeros((128, 128), f32)
    gb = []
    for g in range(4):
        a = wgp[g][0:128, 0:128]
        b = z.copy()
        r0 = (g % 2) * 64
        b[r0:r0 + 64, :] = wgp[g][128:192, 0:128]
        gb.extend([a, b])
    for pair in range(2):
        g0, g1 = 2 * pair, 2 * pair + 1
        c0 = z.copy(); c0[:, 0:64] = wgp[g0][0:128, 128:192]
        c1 = z.copy(); c1[:, 64:128] = wgp[g1][0:128, 128:192]
        dd = z.copy(); dd[0:64, 0:64] = wgp[g0][128:192, 128:192]; dd[64:128, 64:128] = wgp[g1][128:192, 128:192]
        gb.extend([c0, c1, dd])
    wgrp = np.ascontiguousarray(np.stack(gb, axis=1).reshape(128, 14 * 128))

    wm = []
    for key in ("a_w_mem_kv", "b_w_mem_kv"):
        w = np.asarray(inp[key][0], f32)
        wm.append(w.reshape(8, 128, 512).transpose(1, 0, 2).reshape(128, 8 * 512))
    wmem = np.ascontiguousarray(np.concatenate(wm, axis=1))

    idx = np.arange(128)
    ident = np.eye(128, dtype=f32)
    onesd = np.full((128, 128), 1.0 / D, f32)
    negincl = np.where(idx[:, None] >= idx[None, :], -1.0, 0.0).astype(f32)
    negrest = np.where(idx[:, None] < idx[None, :], -1.0, 0.0).astype(f32)
    maskb = np.where(idx[:, None] < idx[None, :], 0.0, -30000.0).astype(f32)
    onesp0 = np.zeros((128, 128), f32); onesp0[:, 0:64] = 1.0
    onesp1 = np.zeros((128, 128), f32); onesp1[:, 64:128] = 1.0
    cst = np.ascontiguousarray(np.concatenate([ident, onesd, negincl, negrest, maskb, onesp0, onesp1, z], axis=1))

    gvec = np.zeros((128, 64), f32)
    for k, g in enumerate([inp["a_norm_mix"][0], inp["a_norm_ffn"][0], inp["kv_norm"], inp["b_norm_mix"][0],
                           inp["b_norm_ffn"][0], inp["final_norm"], inp["mem_norm"]]):
        gvec[:, 8 * k:8 * k + 8] = np.asarray(g, f32).reshape(8, 128).T
    gvec[:, G_SCALE:G_SCALE + 6] = np.asarray(inp["a_scale"][0], f32)[perm].reshape(6, 128).T
    fac = np.zeros((128, 64), f32)
    for widx, w in enumerate((2, 4, 8, 16)):
        t = np.arange(16)
        fac[:, widx * 16:(widx + 1) * 16] = (w / np.minimum(t + 1, w)).astype(f32)[None, :]
    return dict(wall=wall, wgrp=wgrp, wmem=wmem, cst=cst, gvec=gvec, fac=fac)


def kernel(**inputs):
    x = np.asarray(inputs["x"], np.float32)
    mem = np.asarray(inputs["mem"], np.float32)
    B, S, _ = x.shape
    NT = S // T
    shared = pack_weights(inputs)
    nc = build_nc(NT)
    in_maps = []
    for b in range(B):
        m = dict(shared)
        m["xT"] = np.ascontiguousarray(x[b].T)
        m["memT"] = np.ascontiguousarray(mem[b].T)
        in_maps.append(m)
    res = run_bass_kernel_spmd(nc, in_maps, core_ids=list(range(B)))
    out = np.stack([np.asarray(r["outT"], np.float32).T for r in res.results], axis=0)
    return np.ascontiguousarray(out)
```

```python
from contextlib import ExitStack

import numpy as np
import concourse.bass as bass
import concourse.mybir as mybir
from concourse.bass_utils import run_bass_kernel_spmd

F32 = mybir.dt.float32
BF16 = mybir.dt.bfloat16
AF = mybir.ActivationFunctionType
ALU = mybir.AluOpType

D = 1024
NCH = 8
T = 512
DFF = 2816
NF = 22
NHEAD = 12
EPS = 1e-6
FILL = 2
FILL_N = 256
RPULL = 4
PULL_A = 2
NSLOT = 4
PIECE = 16
CONVB = 32
NBLK = 1408
NPIECE = NBLK // PIECE
NCONV = NBLK // CONVB
SEC_A = (0, 656)
SEC_KVQ = (656, 816)
SEC_B = (816, 1408)

O_WIN = 0
O_WOUTA = 64
O_WGUA = 128
O_WDA = 480
O_WK = 656
O_WV = 704
O_WQ = 752
O_WOUTB = 816
O_WGUB = 880
O_WDB = 1232

WIN_ORDER = (6, 7, 4, 5, 0, 1, 2, 3)

G_AMIX, G_AFFN, G_KV, G_BMIX, G_BFFN, G_FINAL, G_MEM = range(7)
G_SCALE = 56


class Tile:
    __slots__ = ("name", "w", "readers")

    def __init__(self, name):
        self.name = name
        self.w = None
        self.readers = []


class DmaSem:
    __slots__ = ("name", "count", "handle")

    def __init__(self, name):
        self.name = name
        self.count = 0
        self.handle = None


class _Instr:
    __slots__ = ("fn", "waits", "late", "signal", "dsem", "idx")

    def __init__(self, fn):
        self.fn = fn
        self.waits = []
        self.late = []
        self.signal = False
        self.dsem = None
        self.idx = 0


ENGS = ("pe", "act", "dve", "pool", "sp")
LATE_DEFAULT = 1


class Prog:
    def __init__(self):
        self.ins = {e: [] for e in ENGS}
        self.clock = {e: {} for e in ENGS}
        self.dsems = []

    def dsem(self, name):
        d = DmaSem(name)
        self.dsems.append(d)
        return d

    def _record(self, eng, fn, reads, writes, dsem=None, update=True, nlhs=None):
        ins = _Instr(fn)
        lst = self.ins[eng]
        lst.append(ins)
        ins.idx = len(lst)
        clock = self.clock[eng]
        need = []
        early_keys = set()
        for n, t in enumerate(reads):
            if t.w is not None:
                need.append(t.w)
                if nlhs is not None and n < nlhs:
                    early_keys.add(t.w[0])
        for t in writes:
            if t.w is not None:
                need.append(t.w)
            need.extend(t.readers)
        best = {}
        for ev in need:
            key, val, evclock = ev
            if key == "pe" and eng == "pe":
                continue
            if clock.get(key, 0) >= val:
                continue
            if best.get(key, (0, None))[0] < val:
                best[key] = (val, evclock)
        for key, (val, evclock) in best.items():
            if clock.get(key, 0) >= val:
                continue
            if nlhs is not None and key not in early_keys:
                ins.late.append((key, val))
            else:
                ins.waits.append((key, val))
            if isinstance(key, str):
                self.ins[key][val - 1].signal = True
            for k2, v2 in evclock.items():
                if clock.get(k2, 0) < v2:
                    clock[k2] = v2
            clock[key] = max(clock.get(key, 0), val)
        if dsem is not None:
            dsem.count += 16
            ins.dsem = dsem
            evclock = dict(clock)
            evclock[dsem] = dsem.count
            ev = (dsem, dsem.count, evclock)
        else:
            evclock = dict(clock)
            evclock[eng] = ins.idx
            ev = (eng, ins.idx, evclock)
        if update:
            for t in writes:
                t.w = ev
                t.readers = []
            for t in reads:
                t.readers.append(ev)
        return ins

    def op(self, eng, fn, reads=(), writes=(), nlhs=None):
        if eng == "pe" and nlhs is None:
            nlhs = LATE_DEFAULT
        if eng != "pe":
            nlhs = None
        return self._record(eng, fn, list(reads), list(writes), nlhs=nlhs)

    def dma(self, eng, dsem, out_ap, in_ap, reads=(), writes=()):
        def fn(e, out_ap=out_ap, in_ap=in_ap):
            return e.dma_start(out=out_ap, in_=in_ap)

        return self._record(eng, fn, list(reads), list(writes), dsem=dsem)

    def wait_all(self, eng, tiles):
        return self._record(eng, None, list(tiles), list(tiles), update=False)

    def emit(self, nc):
        with ExitStack() as st:
            sems = {}
            for e in ENGS:
                sems[e] = st.enter_context(nc.semaphore("s_" + e))
            for d in self.dsems:
                d.handle = st.enter_context(nc.semaphore("d_" + d.name))
            block = st.enter_context(nc.Block())
            rank = {}
            for e in ENGS:
                r = 0
                rk = []
                for ins in self.ins[e]:
                    if ins.signal:
                        r += 1
                    rk.append(r)
                rank[e] = rk

            def semval(key, val):
                if isinstance(key, str):
                    return sems[key], rank[key][val - 1]
                return key.handle, val

            def run(e, eh):
                for ins in self.ins[e]:
                    late = list(ins.late)
                    late.sort(key=lambda kv: 1 if isinstance(kv[0], str) and kv[0] in ("dve", "act") else 0)
                    attach = late.pop() if (late and ins.fn is not None) else None
                    for key, val in list(ins.waits) + late:
                        sh, sv = semval(key, val)
                        eh.wait_ge(sh, sv)
                    if ins.fn is None:
                        continue
                    bi = ins.fn(eh)
                    if attach is not None:
                        sh, sv = semval(*attach)
                        bi._wait_ge(sh, sv)
                    if ins.dsem is not None:
                        bi.then_inc(ins.dsem.handle, 16)
                    elif ins.signal:
                        bi.then_inc(sems[e], 1)

            @block.tensor
            def _(eh):
                run("pe", eh)

            @block.scalar
            def _(eh):
                run("act", eh)

            @block.vector
            def _(eh):
                run("dve", eh)

            @block.gpsimd
            def _(eh):
                run("pool", eh)

            @block.sync
            def _(eh):
                run("sp", eh)


def build_nc(NT):
    S = NT * T
    NKB = S // 128
    nc = bass.Bass("TRN2", target_bir_lowering=False)
    xT_d = nc.dram_tensor("xT", [D, S], F32, kind="ExternalInput").ap()
    memT_d = nc.dram_tensor("memT", [D, 256], F32, kind="ExternalInput").ap()
    wall_d = nc.dram_tensor("wall", [128, NBLK * 128], F32, kind="ExternalInput").ap()
    wgrp_d = nc.dram_tensor("wgrp", [128, 14 * 128], F32, kind="ExternalInput").ap()
    wmem_d = nc.dram_tensor("wmem", [128, 2 * 8 * 512], F32, kind="ExternalInput").ap()
    cst_d = nc.dram_tensor("cst", [128, 8 * 128], F32, kind="ExternalInput").ap()
    gvec_d = nc.dram_tensor("gvec", [128, 64], F32, kind="ExternalInput").ap()
    fac_d = nc.dram_tensor("fac", [128, 64], F32, kind="ExternalInput").ap()
    outT_d = nc.dram_tensor("outT", [D, S], F32, kind="ExternalOutput").ap()
    wsc_d = nc.dram_tensor("wsc", [128, NBLK * 128], BF16).ap()
    kTs_d = nc.dram_tensor("kTs", [6 * 128, S], BF16).ap()

    P = Prog()
    with ExitStack() as st:
        def sb(name, shape, dt):
            return st.enter_context(nc.sbuf_tensor(name, shape, dt))

        v_sb = sb("vc", [128, NKB * 768], BF16)
        kst_sb = sb("kst", [128, 2 * S], BF16)
        xs = [sb(f"x{p}", [128, NCH * T], F32) for p in range(2)]
        h_sb = sb("h", [128, NCH * T], BF16)
        q_sb = sb("q", [128, NCH * T], BF16)
        cats = [sb(f"cat{p}", [128, NCH * T], BF16) for p in range(2)]
        big_sb = sb("big", [128, 5632], F32)
        sbt_sb = sb("sbt", [128, 3840], F32)
        ring_sb = sb("ring", [128, NSLOT * PIECE * 128], BF16)
        wg_sb = sb("wg", [128, 14 * 128], BF16)
        mk_sb = sb("mk", [128, 2 * 2 * 256], BF16)
        mvp_sb = sb("mvp", [128, 2 * 2 * 4 * 128], BF16)
        cst_sb = sb("cstb", [128, 8 * 128], BF16)
        gvec_sb = sb("gvec_s", [128, 64], F32)
        fac_sb = sb("fac_s", [128, 64], F32)
        halo_sb = sb("halo", [128, 6 * 16], F32)
        rkv_sb = sb("rkv", [128, T], F32)
        Fs = [sb(f"fs{k}", [128, 528], F32) for k in range(4)]
        Bs = [sb(f"bs{k}", [128, 512], BF16) for k in range(4)]
        ps = [st.enter_context(nc.psum_tensor(f"ps{k}", [128, 512], F32)) for k in range(8)]

        hid_ap = big_sb[:].bitcast(BF16)
        big_bf = hid_ap
        pooled_ap = big_sb[:, 3168:3168 + 1536].bitcast(BF16)
        sbt_bf = sbt_sb[:].bitcast(BF16)

        X = [[Tile(f"x{p}_{c}") for c in range(NCH)] for p in range(2)]
        CAT = [[Tile(f"cat{p}_{c}") for c in range(NCH)] for p in range(2)]
        H = [Tile(f"h{c}") for c in range(NCH)]
        Q = [Tile(f"q{c}") for c in range(NCH)]
        HID = [Tile(f"hid{f}") for f in range(NF)]
        U = [Tile(f"u{c}") for c in range(6)]
        PL = [Tile(f"pl{c}") for c in range(6)]
        HQ = [Tile(f"hq{c}") for c in range(NCH)]
        OST = [Tile(f"ost{c}") for c in range(NCH)]
        RING = [Tile(f"ring{k}") for k in range(NSLOT)]
        FT = [Tile(f"F{k}") for k in range(4)]
        BT = [Tile(f"B{k}") for k in range(4)]
        PS = [Tile(f"PS{k}") for k in range(8)]
        KS = [Tile(f"ks{s}") for s in range(2)]
        KD = [Tile(f"kd{hc}") for hc in range(6)]
        VT = [Tile(f"v{kb}") for kb in range(NKB)]
        WG = Tile("wg"); MK = Tile("mk"); MVP = Tile("mvp"); CST = Tile("cst"); GV = Tile("gvec"); FAC = Tile("fac")
        RKV = Tile("rkv")
        WSC = [Tile(f"wsc{p}") for p in range(NPIECE)]
        OUTT = Tile("outT")
        HALO = [Tile(f"halo{c}") for c in range(6)]
        SE = [Tile(f"sbE{j}") for j in range(3)]
        SW = [Tile(f"sbW{j}") for j in range(2)]
        SS = [Tile(f"sbS{j}") for j in range(3)]
        SWT = [Tile(f"sbT{j}") for j in range(2)]

        BIGREG = []
        for c in range(6):
            BIGREG.append((U[c], c * 2112, (c + 1) * 2112, "u"))
            BIGREG.append((PL[c], 12672 + c * 1024, 12672 + (c + 1) * 1024, "u"))
        for f in range(NF):
            BIGREG.append((HID[f], f * 1024, (f + 1) * 1024, "hid"))
        for c in range(NCH):
            BIGREG.append((HQ[c], c * 1024, (c + 1) * 1024, "hq"))
            BIGREG.append((OST[c], c * 2048, (c + 1) * 2048, "out"))

        def big_ov(tile):
            for (t, lo, hi, fam) in BIGREG:
                if t is tile:
                    break
            return [t2 for (t2, lo2, hi2, fam2) in BIGREG if fam2 != fam and lo < hi2 and lo2 < hi]

        def sE(j, lo, hi):
            return sbt_sb[:, j * 512 + lo:j * 512 + hi]

        def sW(j, lo, hi):
            return sbt_sb[:, 1536 + j * 512 + lo:1536 + j * 512 + hi]

        def sS(j, lo, hi):
            return sbt_bf[:, 5120 + j * 512 + lo:5120 + j * 512 + hi]

        def sT(j, lo, hi):
            return sbt_bf[:, 6656 + j * 512 + lo:6656 + j * 512 + hi]

        def xa(p, c, lo=0, hi=T):
            return xs[p][:, c * T + lo:c * T + hi]

        def cata(p, c, lo=0, hi=T, pr=slice(0, 128)):
            return cats[p][pr, c * T + lo:c * T + hi]

        def ha(c, lo=0, hi=T):
            return h_sb[:, c * T + lo:c * T + hi]

        def hqa(c):
            return big_bf[:, c * T:(c + 1) * T]

        def qa(c, lo=0, hi=T, pr=slice(0, 128)):
            return q_sb[pr, c * T + lo:c * T + hi]

        def hida(f):
            return hid_ap[:, f * T:(f + 1) * T]

        def ua(c, lo, hi, pr=slice(0, 128)):
            return big_sb[pr, c * 528 + lo:c * 528 + hi]

        def pla(c, lo=0, hi=T, pr=slice(0, 128)):
            return pooled_ap[pr, c * T + lo:c * T + hi]

        def csta(k):
            return cst_sb[:, k * 128:(k + 1) * 128]

        IDENT, ONESD, NEGINCL, NEGREST, MASKB, ONESP0, ONESP1, ZERO = [csta(k) for k in range(8)]

        def gcol(k, c):
            return gvec_sb[:, 8 * k + c:8 * k + c + 1]

        def mm(out_ap, lhsT, rhs, start, stop):
            return lambda e: e.matmul(out_ap, lhsT=lhsT, rhs=rhs, start=start, stop=stop)

        d_cst = P.dsem("cst"); d_gv = P.dsem("gv"); d_fac = P.dsem("fac"); d_wg = P.dsem("wg")
        d_wm = [P.dsem("wm0"), P.dsem("wm1")]
        d_x = [P.dsem("x0"), P.dsem("x1")]
        d_out = P.dsem("out")
        d_ring = [P.dsem(f"ring{k}") for k in range(NSLOT)]
        d_ring2 = [P.dsem(f"ringb{k}") for k in range(NSLOT)]
        d_wst = [P.dsem(f"wst{k}") for k in range(NSLOT)]
        d_ks = [P.dsem("ks0"), P.dsem("ks1")]
        d_kd = [P.dsem(f"kd{hc}") for hc in range(6)]

        class Banks:
            def __init__(self, banks, statb):
                self.banks = banks
                self.statb = statb
                self.n = 0

            def get(self):
                k = self.banks[self.n % len(self.banks)]
                self.n += 1
                return k

        BK_MAIN = Banks((0, 1, 2, 3), 7)
        BK_A = Banks((0, 1, 2), 3)

        def sec_pieces(sec):
            return list(range(sec[0] // PIECE, sec[1] // PIECE))

        seq = []
        sec_start = {}
        sec_start[("A", 0)] = len(seq); seq += sec_pieces(SEC_A)
        for i in range(NT):
            sec_start[("KVQ", i)] = len(seq); seq += sec_pieces(SEC_KVQ)
            if i + 1 < NT:
                sec_start[("A", i + 1)] = len(seq); seq += sec_pieces(SEC_A)
            sec_start[("B", i)] = len(seq); seq += sec_pieces(SEC_B)
        SECS = {"A": SEC_A, "KVQ": SEC_KVQ, "B": SEC_B}
        wstate = {"issued": 0, "cur": -1}

        seen_piece = set()

        def issue_piece(g):
            p = seq[g]
            s = g % NSLOT
            slot = ring_sb[:, s * PIECE * 128:(s + 1) * PIECE * 128]
            if p not in seen_piece:
                seen_piece.add(p)
                P.dma("pool", d_ring[s], slot, wall_d[:, p * PIECE * 128:(p + 1) * PIECE * 128], reads=[], writes=[RING[s]])
                P.dma("sp", d_wst[s], wsc_d[:, p * PIECE * 128:(p + 1) * PIECE * 128], slot, reads=[RING[s]], writes=[WSC[p]])
            else:
                P.dma("sp", d_ring2[s], slot, wsc_d[:, p * PIECE * 128:(p + 1) * PIECE * 128], reads=[WSC[p]], writes=[RING[s]])

        def wblk(kind, tile, j, n=1):
            sec = SECS[kind]
            assert sec[0] <= j < sec[1] and (j % PIECE) + n <= PIECE
            g = sec_start[(kind, tile)] + (j - sec[0]) // PIECE
            assert g >= wstate["cur"], (kind, tile, j, g, wstate["cur"])
            if g > wstate["cur"]:
                wstate["cur"] = g
                while wstate["issued"] < min(g + NSLOT, len(seq)):
                    issue_piece(wstate["issued"])
                    wstate["issued"] += 1
            s = g % NSLOT
            off = s * PIECE * 128 + (j % PIECE) * 128
            return ring_sb[:, off:off + n * 128], RING[s]

        def drain(gen):
            if gen is None:
                return
            for _ in gen:
                pass

        P.dma("pool", d_cst, cst_sb[:], cst_d, writes=[CST])
        P.dma("sp", d_gv, gvec_sb[:], gvec_d, writes=[GV])
        P.dma("sp", d_fac, fac_sb[:], fac_d, writes=[FAC])
        P.dma("pool", d_wg, wg_sb[:], wgrp_d, writes=[WG])
        for L in range(2):
            P.dma("pool", d_wm[L], ring_sb[:, L * 4096:(L + 1) * 4096], wmem_d[:, L * 4096:(L + 1) * 4096],
                  writes=[RING[2 * L], RING[2 * L + 1]])
        P.dma("sp", d_x[0], xs[0][:, 0:2048].rearrange("p (c m) -> p c m", c=8),
              memT_d.rearrange("(c p) m -> p c m", p=128), writes=X[0][0:4])
        P.op("dve", lambda e: e.memset(mvp_sb[:], 0.0), writes=[MVP])
        P.op("dve", lambda e: e.memset(halo_sb[:], 0.0), writes=HALO)

        def memx(c):
            return xs[0][:, c * 256:(c + 1) * 256]

        def memxt(c):
            return X[0][c // 2]

        msp = BK_MAIN.get()
        for c in range(NCH):
            b = c % 2
            eng = "pool" if c % 2 == 0 else "dve"
            P.op(eng, lambda e, c=c, b=b: e.tensor_tensor(out=Bs[b][:, 0:256], in0=memx(c), in1=memx(c), op=ALU.mult),
                 reads=[memxt(c)], writes=[BT[b]])
            P.op("pe", mm(ps[msp][:, 0:256], ONESD, Bs[b][:, 0:256], c == 0, c == NCH - 1), reads=[CST, BT[b]], writes=[PS[msp]])
        P.op("act", lambda e: e.activation(out=Fs[3][:, 0:256], in_=ps[msp][:, 0:256], func=AF.Ln, bias=EPS),
             reads=[PS[msp]], writes=[FT[3]])
        P.op("act", lambda e: e.activation(out=Fs[2][:, 0:256], in_=Fs[3][:, 0:256], func=AF.Exp, scale=-0.5),
             reads=[FT[3]], writes=[FT[2]])
        for c in range(NCH):
            P.op("dve", lambda e, c=c: e.scalar_tensor_tensor(out=ha(c, 0, 256), in0=memx(c), scalar=gcol(G_MEM, c),
                                                                in1=Fs[2][:, 0:256], op0=ALU.mult, op1=ALU.mult),
                 reads=[memxt(c), GV, FT[2]], writes=[H[c]])
        for L in range(2):
            RL = [RING[2 * L], RING[2 * L + 1]]

            def wm(c, j, n=1, L=L):
                off = L * 4096 + c * 512 + j * 128
                return ring_sb[:, off:off + n * 128]
            for fc in range(2):
                k = BK_MAIN.get()
                for c in range(NCH):
                    P.op("pe", mm(ps[k][:, 0:256], wm(c, fc), ha(c, 0, 256), c == 0, c == NCH - 1),
                         reads=RL + [H[c]], writes=[PS[k]], nlhs=2)
                P.op("act", lambda e, k=k, L=L, fc=fc: e.activation(
                    out=mk_sb[:, (L * 2 + fc) * 256:(L * 2 + fc + 1) * 256], in_=ps[k][:, 0:256], func=AF.Copy),
                    reads=[PS[k]], writes=[MK])
            for mb in range(2):
                k = BK_MAIN.get()
                for c in range(NCH):
                    P.op("pe", mm(ps[k][:, 0:256], ha(c, mb * 128, mb * 128 + 128), wm(c, 2, 2), c == 0, c == NCH - 1),
                         reads=[H[c]] + RL, writes=[PS[k]])
                for hh in range(4):
                    base = ((L * 2 + mb) * 4 + hh) * 128 + (hh % 2) * 64
                    P.op("dve", lambda e, k=k, base=base, hh=hh: e.tensor_copy(
                        out=mvp_sb[:, base:base + 64], in_=ps[k][:, hh * 64:(hh + 1) * 64]),
                        reads=[PS[k]], writes=[MVP])

        def mka(L, hh, mb):
            hc, half = divmod(hh, 2)
            base = (L * 2 + hc) * 256 + mb * 128
            return mk_sb[half * 64:half * 64 + 64, base:base + 128]

        def mvpa(L, mb, hh):
            base = ((L * 2 + mb) * 4 + hh) * 128
            return mvp_sb[:, base:base + 128]

        pending = []

        def flush_pending():
            while pending:
                pending.pop(0)()

        sqrot = [0]

        def stat_chunk(c, k, src_ap, src_tile, eng=None, defer=False):
            b = sqrot[0] % 2
            sqrot[0] += 1
            if eng is None:
                eng = "pool" if c % 2 == 0 else "dve"
            P.op(eng, lambda e, c=c, b=b: e.tensor_tensor(out=Bs[b][:, 0:T], in0=src_ap(c), in1=src_ap(c), op=ALU.mult),
                 reads=[src_tile(c)], writes=[BT[b]])

            def do_mm():
                P.op("pe", mm(ps[k][:, 0:T], ONESD, Bs[b][:, 0:T], c == 0, c == NCH - 1),
                     reads=[CST, BT[b]], writes=[PS[k]])
            if defer:
                pending.append(do_mm)
            else:
                do_mm()

        def rstd_from(k, dst_ap=None, dst_tile=None):
            flush_pending()
            if dst_ap is None:
                dst_ap, dst_tile = Fs[2][:, 0:T], FT[2]
            P.op("act", lambda e: e.activation(out=Fs[3][:, 0:T], in_=ps[k][:, 0:T], func=AF.Ln, bias=EPS),
                 reads=[PS[k]], writes=[FT[3]])
            P.op("act", lambda e: e.activation(out=dst_ap, in_=Fs[3][:, 0:T], func=AF.Exp, scale=-0.5),
                 reads=[FT[3]], writes=[dst_tile])

        def norm_apply(p, gk, rap=None, rtile=None):
            if rap is None:
                rap, rtile = Fs[2][:, 0:T], FT[2]
            for c in range(NCH):
                P.op("dve", lambda e, c=c: e.scalar_tensor_tensor(out=ha(c), in0=xa(p, c), scalar=gcol(gk, c), in1=rap,
                                                                    op0=ALU.mult, op1=ALU.mult),
                     reads=[X[p][c], GV, rtile], writes=[H[c]])

        def proj(bk, kind, tile, o_base, oi, evac, sap, stl):
            k = bk.get()
            for c in range(NCH):
                wap, wt = wblk(kind, tile, o_base + oi * NCH + c)
                P.op("pe", mm(ps[k][:, :], wap, sap(c), c == 0, c == NCH - 1), reads=[wt, stl[c]], writes=[PS[k]])
                yield
            evac(k)

        def resid_add(p, oc, statb):
            def ev(k):
                P.op("dve", lambda e: e.tensor_tensor(out=xa(p, oc), in0=xa(p, oc), in1=ps[k][:, :], op=ALU.add),
                     reads=[X[p][oc], PS[k]], writes=[X[p][oc]])
                flush_pending()
                stat_chunk(oc, statb, lambda c: xa(p, c), lambda c: X[p][c], eng="pool", defer=True)
            return ev

        def mem_attention(L, p, OB, ZB, zbanks):
            for hc in range(2):
                pts = []
                for half in range(2):
                    hh = hc * 2 + half
                    pr = slice(half * 64, half * 64 + 64)
                    for mb in range(2):
                        zk = zbanks[len(pts) % len(zbanks)]
                        bslot = (hc * 4 + len(pts)) % 4
                        P.op("pe", mm(ps[zk][:, :], mka(L, hh, mb), qa(6 + hc, pr=pr), True, True),
                             reads=[MK, Q[6 + hc]], writes=[PS[zk]])
                        yield
                        P.op("act", lambda e, zk=zk, bslot=bslot: e.activation(out=Bs[bslot][:, :], in_=ps[zk][:, :], func=AF.Exp),
                             reads=[PS[zk]], writes=[BT[bslot]])
                        pts.append((bslot, hh, mb, half))
                for n, (bslot, hh, mb, half) in enumerate(pts):
                    P.op("pe", mm(ps[OB][:, :], mvpa(L, mb, hh), Bs[bslot][:, :], n == 0, n == 3),
                         reads=[MVP, BT[bslot]], writes=[PS[OB]])
                    yield
                for n, (bslot, hh, mb, half) in enumerate(pts):
                    P.op("pe", mm(ps[ZB][:, :], ONESP0 if half == 0 else ONESP1, Bs[bslot][:, :], n == 0, n == 3),
                         reads=[CST, BT[bslot]], writes=[PS[ZB]])
                    yield
                P.op("act", lambda e: e.activation(out=Fs[3][:, 0:T], in_=ps[ZB][:, :], func=AF.Ln), reads=[PS[ZB]], writes=[FT[3]])
                P.op("act", lambda e: e.activation(out=Fs[3][:, 0:T], in_=Fs[3][:, 0:T], func=AF.Exp, scale=-1.0), reads=[FT[3]], writes=[FT[3]])
                P.op("dve", lambda e, hc=hc: e.tensor_tensor(out=cata(p, 6 + hc), in0=ps[OB][:, :], in1=Fs[3][:, 0:T], op=ALU.mult),
                     reads=[PS[OB], FT[3]], writes=[CAT[p][6 + hc]])

        def ffn(bk, kind, tile, p, gk, o_gu, o_d, after_down=None):
            rstd_from(bk.statb)
            norm_apply(p, gk)
            for f in range(NF):
                kg = bk.get()
                ku = bk.get()
                for c in range(NCH):
                    wap, wt = wblk(kind, tile, o_gu + f * 16 + c)
                    P.op("pe", mm(ps[kg][:, :], wap, ha(c), c == 0, c == NCH - 1), reads=[wt, H[c]], writes=[PS[kg]])
                    yield
                for c in range(NCH):
                    wap, wt = wblk(kind, tile, o_gu + f * 16 + 8 + c)
                    P.op("pe", mm(ps[ku][:, :], wap, ha(c), c == 0, c == NCH - 1), reads=[wt, H[c]], writes=[PS[ku]])
                    yield
                fs = f % 2
                P.op("act", lambda e, kg=kg, fs=fs: e.activation(out=Fs[fs][:, 0:T], in_=ps[kg][:, :], func=AF.Exp, scale=-1.0),
                     reads=[PS[kg]], writes=[FT[fs]])
                P.op("act", lambda e, fs=fs: e.activation(out=Fs[fs][:, 0:T], in_=Fs[fs][:, 0:T], func=AF.Ln, bias=1.0),
                     reads=[FT[fs]], writes=[FT[fs]])
                P.op("act", lambda e, fs=fs: e.activation(out=Fs[fs][:, 0:T], in_=Fs[fs][:, 0:T], func=AF.Exp, scale=-1.0),
                     reads=[FT[fs]], writes=[FT[fs]])
                P.op("dve", lambda e, kg=kg, fs=fs: e.tensor_tensor(out=Fs[fs][:, 0:T], in0=Fs[fs][:, 0:T], in1=ps[kg][:, :], op=ALU.mult),
                     reads=[FT[fs], PS[kg]], writes=[FT[fs]])
                P.op("dve", lambda e, ku=ku, fs=fs, f=f: e.tensor_tensor(out=hida(f), in0=Fs[fs][:, 0:T], in1=ps[ku][:, :], op=ALU.mult),
                     reads=[FT[fs], PS[ku]], writes=[HID[f]] + big_ov(HID[f]))
            for oc in range(NCH):
                k = bk.get()
                for f in range(NF):
                    wap, wt = wblk(kind, tile, o_d + oc * NF + f)
                    P.op("pe", mm(ps[k][:, :], wap, hida(f), f == 0, f == NF - 1), reads=[wt, HID[f]], writes=[PS[k]])
                    yield
                resid_add(p, oc, bk.statb)(k)

        POOLCFG = [(0, 0, 128, 2), (1, 0, 128, 4), (2, 0, 128, 8), (3, 0, 128, 16),
                   (4, 0, 64, 2), (4, 64, 128, 4), (5, 0, 64, 8), (5, 64, 128, 16)]

        def pool_chunk(i, c, eng, fa, fb):
            cfgs = [cf for cf in POOLCFG if cf[0] == c]
            wmax = max(cf[3] for cf in cfgs)
            src_ap = lambda lo, hi: ua(c, lo, hi)
            src_t = U[c]
            slots = [fa, fb]
            sums = {}
            step = 1
            n = 0
            while step < wmax:
                dst = slots[n % 2]
                lo = 2 * step - 1
                P.op(eng, lambda e, dst=dst, lo=lo, step=step, src_ap=src_ap: e.tensor_tensor(
                    out=Fs[dst][:, lo:528], in0=src_ap(lo, 528), in1=src_ap(lo - step, 528 - step), op=ALU.add),
                    reads=[src_t], writes=[FT[dst]])
                step *= 2
                sums[step] = dst
                src_ap = (lambda lo, hi, dst=dst: Fs[dst][:, lo:hi])
                src_t = FT[dst]
                n += 1
            for (_, p0, p1, w) in cfgs:
                s = sums[w]
                pr = slice(p0, p1)
                widx = {2: 0, 4: 1, 8: 2, 16: 3}[w]
                if i == 0:
                    P.op(eng, lambda e, s=s, pr=pr, widx=widx: e.tensor_tensor(
                        out=Fs[s][pr, 16:32], in0=Fs[s][pr, 16:32], in1=fac_sb[pr, widx * 16:(widx + 1) * 16], op=ALU.mult),
                        reads=[FT[s], FAC], writes=[FT[s]])
                P.op("dve", lambda e, s=s, pr=pr, w=w: e.scalar_tensor_tensor(
                    out=pla(c, pr=pr), in0=Fs[s][pr, 16:528], scalar=1.0 / w, in1=ua(c, 16, 528, pr=pr),
                    op0=ALU.mult, op1=ALU.subtract),
                    reads=[FT[s], U[c]], writes=[PL[c]] + big_ov(PL[c]))
            P.op(eng, lambda e: e.tensor_copy(out=halo_sb[:, c * 16:(c + 1) * 16], in_=ua(c, 512, 528)), reads=[U[c]], writes=[HALO[c]])

        GROUP_PLAN = [(0, [(0, 0), (1, 4)]), (1, [(2, 1), (3, 4)]), (2, [(4, 2), (5, 5)]), (3, [(6, 3), (7, 5)]),
                      (4, [(8, 0), (9, 1), (10, 4)]), (5, [(11, 2), (12, 3), (13, 5)])]

        def layer_a(i):
            p = i % 2
            bk = BK_A
            t0 = i * T
            P.dma("sp", d_x[p], xs[p][:].rearrange("p (c t) -> p c t", c=NCH),
                  xT_d.rearrange("(c p) s -> p c s", p=128)[:, :, t0:t0 + T], reads=[], writes=X[p])
            k = bk.get()
            for c in range(NCH):
                stat_chunk(c, k, lambda c: xa(p, c), lambda c: X[p][c])
                yield
            rstd_from(k)
            norm_apply(p, G_AMIX)
            for oi, oc in enumerate(WIN_ORDER):
                if oc < 6:
                    def ev(k, oc=oc):
                        P.op("dve", lambda e: e.tensor_copy(out=ua(oc, 16, 528), in_=ps[k][:, :]),
                             reads=[PS[k]], writes=[U[oc]] + big_ov(U[oc]))
                        P.op("pool", lambda e: e.tensor_copy(out=ua(oc, 0, 16), in_=halo_sb[:, oc * 16:(oc + 1) * 16]),
                             reads=[HALO[oc]], writes=[U[oc]])
                else:
                    def ev(k, oc=oc):
                        P.op("dve", lambda e: e.tensor_scalar_mul(out=qa(oc), in0=ps[k][:, :], scalar1=0.125),
                             reads=[PS[k]], writes=[Q[oc]])
                yield from proj(bk, "A", i, O_WIN, oi, ev, ha, H)
            for c in (4, 5):
                pool_chunk(i, c, "pool", 2, 3)
            for c in range(4):
                pool_chunk(i, c, "dve", 0, 1)
            yield from mem_attention(0, p, 0, 1, (2,))
            bk.n = 0
            for (oc, plan) in GROUP_PLAN:
                k = bk.get()
                for n, (bi, pc) in enumerate(plan):
                    P.op("pe", mm(ps[k][:, :], wg_sb[:, bi * 128:(bi + 1) * 128], pla(pc), n == 0, n == len(plan) - 1),
                         reads=[WG, PL[pc]], writes=[PS[k]])
                    yield
                P.op("dve", lambda e, k=k, oc=oc: e.tensor_scalar_mul(out=cata(p, oc), in0=ps[k][:, :],
                                                                       scalar1=gvec_sb[:, G_SCALE + oc:G_SCALE + oc + 1]),
                     reads=[PS[k], GV], writes=[CAT[p][oc]])
            for oc in range(NCH):
                yield from proj(bk, "A", i, O_WOUTA, oc, resid_add(p, oc, bk.statb), lambda c: cata(p, c), CAT[p])
            yield from ffn(bk, "A", i, p, G_AFFN, O_WGUA, O_WDA)
            rstd_from(bk.statb, rkv_sb[:, :], RKV)
            yield

        def kvq(i):
            p = i % 2
            bk = BK_MAIN
            t0 = i * T
            norm_apply(p, G_KV, rkv_sb[:, :], RKV)
            for c in range(NCH):
                P.op("dve", lambda e, c=c: e.scalar_tensor_tensor(out=hqa(c), in0=xa(p, c), scalar=gcol(G_BMIX, c), in1=rkv_sb[:, :],
                                                                    op0=ALU.mult, op1=ALU.mult),
                     reads=[X[p][c], GV, RKV], writes=[HQ[c]] + big_ov(HQ[c]))
            for hc in range(6):
                def ev(k, hc=hc):
                    eng = "dve" if hc % 2 == 0 else "act"
                    if eng == "dve":
                        P.op("dve", lambda e: e.tensor_copy(out=cata(p, hc), in_=ps[k][:, :]), reads=[PS[k]], writes=[CAT[p][hc]])
                    else:
                        P.op("act", lambda e: e.activation(out=cata(p, hc), in_=ps[k][:, :], func=AF.Copy), reads=[PS[k]], writes=[CAT[p][hc]])
                    P.dma("sp", d_kd[hc], kTs_d[hc * 128:(hc + 1) * 128, t0:t0 + T], cata(p, hc), reads=[CAT[p][hc]], writes=[KD[hc]])
                yield from proj(bk, "KVQ", i, O_WK, hc, ev, ha, H)
            kbanks = [bk.get() for _ in range(4)]
            for c in range(NCH):
                wap, wt = wblk("KVQ", i, O_WV + c * 4, 4)
                for tb in range(4):
                    P.op("pe", mm(ps[kbanks[tb]][:, :], ha(c, tb * 128, tb * 128 + 128), wap, c == 0, c == NCH - 1),
                         reads=[H[c], wt], writes=[PS[kbanks[tb]]])
                    yield
            for tb in range(4):
                kb = 4 * i + tb
                if tb % 2 == 0:
                    P.op("dve", lambda e, tb=tb, kb=kb: e.tensor_copy(out=v_sb[:, kb * 768:kb * 768 + 512], in_=ps[kbanks[tb]][:, :]),
                         reads=[PS[kbanks[tb]]], writes=[VT[kb]])
                else:
                    P.op("act", lambda e, tb=tb, kb=kb: e.activation(out=v_sb[:, kb * 768:kb * 768 + 512], in_=ps[kbanks[tb]][:, :], func=AF.Copy),
                         reads=[PS[kbanks[tb]]], writes=[VT[kb]])
            k2 = [bk.get() for _ in range(4)]
            for c in range(NCH):
                wap, wt = wblk("KVQ", i, O_WV + 32 + c * 2, 2)
                for tb in range(4):
                    kk = k2[tb]
                    col = 0
                    P.op("pe", mm(ps[kk][:, col:col + 256], ha(c, tb * 128, tb * 128 + 128), wap, c == 0, c == NCH - 1),
                         reads=[H[c], wt], writes=[PS[kk]])
                    yield
            for tb in range(4):
                kb = 4 * i + tb
                kk = k2[tb]
                col = 0
                P.op("act", lambda e, kk=kk, col=col, kb=kb: e.activation(out=v_sb[:, kb * 768 + 512:kb * 768 + 768], in_=ps[kk][:, col:col + 256], func=AF.Copy),
                     reads=[PS[kk]], writes=[VT[kb]])
            for oc in range(NCH):
                def ev(k, oc=oc):
                    if oc % 2 == 0:
                        P.op("act", lambda e: e.activation(out=qa(oc), in_=ps[k][:, :], func=AF.Copy, scale=0.125),
                             reads=[PS[k]], writes=[Q[oc]])
                    else:
                        P.op("dve", lambda e: e.tensor_scalar_mul(out=qa(oc), in0=ps[k][:, :], scalar1=0.125),
                             reads=[PS[k]], writes=[Q[oc]])
                yield from proj(bk, "KVQ", i, O_WQ, oc, ev, hqa, HQ)

        def sb_tile(i, gen):
            p = i % 2
            nkb = 4 * (i + 1)
            npre = 4 * i * 128
            steps = [(hh, kb) for hh in range(NHEAD) for kb in range(nkb - 1, -1, -1)]
            N = len(steps)
            info = {}
            ACC, OB = 4, 5
            live = {"gen": gen}

            def pull():
                g = live["gen"]
                if g is None:
                    return False
                try:
                    next(g)
                    return True
                except StopIteration:
                    live["gen"] = None
                    return False

            def load_kstage(hc):
                s = hc % 2
                if npre > 0:
                    P.dma("sp", d_ks[s], kst_sb[:, s * S:s * S + npre], kTs_d[hc * 128:(hc + 1) * 128, 0:npre],
                          reads=[KD[hc]], writes=[KS[s]])
                P.dma("sp", d_ks[s], kst_sb[:, s * S + npre:s * S + npre + T], cata(p, hc), reads=[CAT[p][hc]], writes=[KS[s]])

            load_kstage(0)
            load_kstage(1)

            def hparams(hh):
                hc, half = divmod(hh, 2)
                pr = slice(half * 64, half * 64 + 64)
                return hc, pr

            def S1(n):
                hh, kb = steps[n]
                hc, pr = hparams(hh)
                if hh % 2 == 0 and kb == nkb - 1 and 1 <= hc and hc + 1 < 6:
                    load_kstage(hc + 1)
                j = kb - 4 * i
                c0 = 128 * j if j > 0 else 0
                zk = 6 + n % 2
                es = n % 3
                bs = n % 3
                s = hc % 2
                ka = kst_sb[pr, s * S + kb * 128:s * S + kb * 128 + 128]
                if j >= 0:
                    P.op("pe", mm(ps[zk][:, c0:c0 + 128], IDENT, MASKB, True, False), reads=[CST], writes=[PS[zk]])
                    P.op("pe", mm(ps[zk][:, c0:c0 + 128], ka, qa(hc, c0, c0 + 128, pr=pr), False, True),
                         reads=[KS[s], Q[hc]], writes=[PS[zk]])
                    if c0 + 128 < T:
                        P.op("pe", mm(ps[zk][:, c0 + 128:T], ka, qa(hc, c0 + 128, T, pr=pr), True, True),
                             reads=[KS[s], Q[hc]], writes=[PS[zk]])
                else:
                    P.op("pe", mm(ps[zk][:, :], ka, qa(hc, pr=pr), True, True), reads=[KS[s], Q[hc]], writes=[PS[zk]])
                P.op("act", lambda e: e.activation(out=sE(es, c0, T), in_=ps[zk][:, c0:T], func=AF.Exp),
                     reads=[PS[zk]], writes=[SE[es]])
                P.op("act", lambda e: e.activation(out=sS(bs, c0, T), in_=sE(es, c0, T), func=AF.Ln, bias=1.0),
                     reads=[SE[es]], writes=[SS[bs]])
                info[n] = (c0, es, bs)

            def S2(n):
                hh, kb = steps[n]
                hc, pr = hparams(hh)
                c0, es, bs = info[n]
                ws = n % 2
                wb = n % 2
                if kb == nkb - 1:
                    P.op("pe", mm(ps[ACC][:, :], ZERO, qa(0), True, False), reads=[CST, Q[0]], writes=[PS[ACC]])
                    P.op("pe", mm(ps[OB][:, :], ZERO, qa(0), True, False), reads=[CST, Q[0]], writes=[PS[OB]])
                P.op("pe", mm(ps[ACC][:, c0:T], NEGINCL, sS(bs, c0, T), False, True), reads=[CST, SS[bs]], writes=[PS[ACC]])
                P.op("act", lambda e: e.activation(out=sW(ws, c0, T), in_=ps[ACC][:, c0:T], func=AF.Exp),
                     reads=[PS[ACC]], writes=[SW[ws]])
                P.op("dve", lambda e: e.tensor_tensor(out=sT(wb, c0, T), in0=sE(es, c0, T), in1=sW(ws, c0, T), op=ALU.mult),
                     reads=[SE[es], SW[ws]], writes=[SWT[wb]])

            def S3a(n):
                hh, kb = steps[n]
                c0, es, bs = info[n]
                if kb > 0:
                    P.op("pe", mm(ps[ACC][:, c0:T], NEGREST, sS(bs, c0, T), False, True), reads=[CST, SS[bs]], writes=[PS[ACC]])

            def S3b(n):
                hh, kb = steps[n]
                hc, pr = hparams(hh)
                c0, es, bs = info[n]
                wb = n % 2
                P.op("pe", mm(ps[OB][:, c0:T], v_sb[:, kb * 768 + hc * 128:kb * 768 + hc * 128 + 128], sT(wb, c0, T), False, kb == 0),
                     reads=[VT[kb], SWT[wb]], writes=[PS[OB]])
                if kb == 0:
                    P.op("dve", lambda e: e.tensor_copy(out=cata(p, hc, pr=pr), in_=ps[OB][pr, :]), reads=[PS[OB]], writes=[CAT[p][hc]])

            def filler():
                P.op("pe", mm(ps[OB][:, 0:FILL_N], ZERO, qa(0, 0, FILL_N), False, False), reads=[CST, Q[0]], writes=[PS[OB]])

            def gap_work(npull, nfill):
                got = 0
                for _ in range(npull):
                    if pull():
                        got += 1
                if live["gen"] is None:
                    for _ in range(max(0, nfill - got)):
                        filler()

            for k in range(-2, N):
                if 0 <= k + 2 < N:
                    S1(k + 2)
                if 0 <= k < N:
                    gap_work(PULL_A, 2)
                    S3a(k)
                head_start = (0 <= k + 1 < N) and steps[k + 1][1] == nkb - 1
                if head_start and 0 <= k < N:
                    gap_work(RPULL - PULL_A, 1)
                    S3b(k)
                    S2(k + 1)
                else:
                    if 0 <= k + 1 < N:
                        S2(k + 1)
                    if 0 <= k < N:
                        gap_work(RPULL - PULL_A, 1)
                        S3b(k)
            return live["gen"]

        drain(layer_a(0))
        for i in range(NT):
            p = i % 2
            t0 = i * T
            drain(kvq(i))
            def sb_companion(i=i, p=p):
                yield from mem_attention(1, p, 0, 1, (2,))
                BK_A.n = 0
                if i + 1 < NT:
                    yield from layer_a(i + 1)
            gen = sb_companion()
            gen = sb_tile(i, gen)
            drain(gen)
            for oc in range(NCH):
                drain(proj(BK_MAIN, "B", i, O_WOUTB, oc, resid_add(p, oc, BK_MAIN.statb), lambda c: cata(p, c), CAT[p]))
            drain(ffn(BK_MAIN, "B", i, p, G_BFFN, O_WGUB, O_WDB))
            rstd_from(BK_MAIN.statb)
            for c in range(NCH):
                P.op("dve", lambda e, c=c, p=p: e.scalar_tensor_tensor(out=big_sb[:, c * T:(c + 1) * T], in0=xa(p, c), scalar=gcol(G_FINAL, c),
                                                                    in1=Fs[2][:, 0:T], op0=ALU.mult, op1=ALU.mult),
                     reads=[X[p][c], GV, FT[2]], writes=[OST[c]] + big_ov(OST[c]))
            P.dma("sp", d_out, outT_d.rearrange("(c p) s -> p c s", p=128)[:, :, t0:t0 + T],
                  big_sb[:, 0:NCH * T].rearrange("p (c t) -> p c t", c=NCH), reads=OST, writes=[OUTT])
        P.wait_all("sp", [OUTT])
        P.emit(nc)
    return nc


def _blk(W, kc, mc):
    return W[kc * 128:(kc + 1) * 128, mc * 128:(mc + 1) * 128]


def _pool_perm():
    perm = []
    for g in range(4):
        perm.extend(range(192 * g, 192 * g + 128))
    for g in range(4):
        perm.extend(range(192 * g + 128, 192 * g + 192))
    return np.array(perm, dtype=np.int64)


def pack_weights(inp):
    f32 = np.float32
    perm = _pool_perm()
    full_perm = np.concatenate([perm, np.arange(768, 1024)])
    blocks = []
    w_in = np.asarray(inp["a_w_in"][0], f32)[:, full_perm]
    for oc in WIN_ORDER:
        for c in range(8):
            blocks.append(_blk(w_in, c, oc))
    w_out_a = np.asarray(inp["a_w_out"][0], f32)[full_perm, :]
    for oc in range(8):
        for c in range(8):
            blocks.append(_blk(w_out_a, c, oc))

    def ffn_blocks(w_gu, w_d):
        for f in range(NF):
            for c in range(8):
                blocks.append(_blk(w_gu, c, f))
            for c in range(8):
                blocks.append(_blk(w_gu, c, NF + f))
        for oc in range(8):
            for f in range(NF):
                blocks.append(_blk(w_d, f, oc))

    ffn_blocks(np.asarray(inp["a_w_gu"][0], f32), np.asarray(inp["a_w_down"][0], f32))
    w_kv = np.asarray(inp["w_kv"], f32)
    for hc in range(6):
        for c in range(8):
            blocks.append(_blk(w_kv, c, hc))
    for c in range(8):
        for j in range(4):
            blocks.append(_blk(w_kv, c, 6 + j))
    for c in range(8):
        for j in range(2):
            blocks.append(_blk(w_kv, c, 10 + j))
    w_q = np.asarray(inp["b_w_q"][0], f32)
    for oc in range(8):
        for c in range(8):
            blocks.append(_blk(w_q, c, oc))
    w_out_b = np.asarray(inp["b_w_out"][0], f32)
    for oc in range(8):
        for c in range(8):
            blocks.append(_blk(w_out_b, c, oc))
    ffn_blocks(np.asarray(inp["b_w_gu"][0], f32), np.asarray(inp["b_w_down"][0], f32))
    assert len(blocks) == NBLK
    wall = np.ascontiguousarray(np.stack(blocks, axis=1).reshape(128, NBLK * 128))

    wgp = np.asarray(inp["a_w_group"][0], f32)
    z = np.zeros((128, 128), f32)
    gb = []
    for g in range(4):
        a = wgp[g][0:128, 0:128]
        b = z.copy()
        r0 = (g % 2) * 64
        b[r0:r0 + 64, :] = wgp[g][128:192, 0:128]
        gb.extend([a, b])
    for pair in range(2):
        g0, g1 = 2 * pair, 2 * pair + 1
        c0 = z.copy(); c0[:, 0:64] = wgp[g0][0:128, 128:192]
        c1 = z.copy(); c1[:, 64:128] = wgp[g1][0:128, 128:192]
        dd = z.copy(); dd[0:64, 0:64] = wgp[g0][128:192, 128:192]; dd[64:128, 64:128] = wgp[g1][128:192, 128:192]
        gb.extend([c0, c1, dd])
    wgrp = np.ascontiguousarray(np.stack(gb, axis=1).reshape(128, 14 * 128))

    wm = []
    for key in ("a_w_mem_kv", "b_w_mem_kv"):
        w = np.asarray(inp[key][0], f32)
        wm.append(w.reshape(8, 128, 512).transpose(1, 0, 2).reshape(128, 8 * 512))
    wmem = np.ascontiguousarray(np.concatenate(wm, axis=1))

    idx = np.arange(128)
    ident = np.eye(128, dtype=f32)
    onesd = np.full((128, 128), 1.0 / D, f32)
    negincl = np.where(idx[:, None] >= idx[None, :], -1.0, 0.0).astype(f32)
    negrest = np.where(idx[:, None] < idx[None, :], -1.0, 0.0).astype(f32)
    maskb = np.where(idx[:, None] < idx[None, :], 0.0, -30000.0).astype(f32)
    onesp0 = np.zeros((128, 128), f32); onesp0[:, 0:64] = 1.0
    onesp1 = np.zeros((128, 128), f32); onesp1[:, 64:128] = 1.0
    cst = np.ascontiguousarray(np.concatenate([ident, onesd, negincl, negrest, maskb, onesp0, onesp1, z], axis=1))

    gvec = np.zeros((128, 64), f32)
    for k, g in enumerate([inp["a_norm_mix"][0], inp["a_norm_ffn"][0], inp["kv_norm"], inp["b_norm_mix"][0],
                           inp["b_norm_ffn"][0], inp["final_norm"], inp["mem_norm"]]):
        gvec[:, 8 * k:8 * k + 8] = np.asarray(g, f32).reshape(8, 128).T
    gvec[:, G_SCALE:G_SCALE + 6] = np.asarray(inp["a_scale"][0], f32)[perm].reshape(6, 128).T
    fac = np.zeros((128, 64), f32)
    for widx, w in enumerate((2, 4, 8, 16)):
        t = np.arange(16)
        fac[:, widx * 16:(widx + 1) * 16] = (w / np.minimum(t + 1, w)).astype(f32)[None, :]
    return dict(wall=wall, wgrp=wgrp, wmem=wmem, cst=cst, gvec=gvec, fac=fac)


def kernel(**inputs):
    x = np.asarray(inputs["x"], np.float32)
    mem = np.asarray(inputs["mem"], np.float32)
    B, S, _ = x.shape
    NT = S // T
    shared = pack_weights(inputs)
    nc = build_nc(NT)
    in_maps = []
    for b in range(B):
        m = dict(shared)
        m["xT"] = np.ascontiguousarray(x[b].T)
        m["memT"] = np.ascontiguousarray(mem[b].T)
        in_maps.append(m)
    res = run_bass_kernel_spmd(nc, in_maps, core_ids=list(range(B)))
    out = np.stack([np.asarray(r["outT"], np.float32).T for r in res.results], axis=0)
    return np.ascontiguousarray(out)
```

```python
from contextlib import ExitStack

import numpy as np
import concourse.bass as bass
import concourse.mybir as mybir
from concourse.bass_utils import run_bass_kernel_spmd

F32 = mybir.dt.float32
BF16 = mybir.dt.bfloat16
AF = mybir.ActivationFunctionType
ALU = mybir.AluOpType

D = 1024
NCH = 8
T = 512
DFF = 2816
NF = 22
NHEAD = 12
EPS = 1e-6
FILL = 2
FILL_N = 256
RPULL = 4
PULL_A = 2
NSLOT = 4
PIECE = 16
CONVB = 32
NBLK = 1408
NPIECE = NBLK // PIECE
NCONV = NBLK // CONVB
SEC_A = (0, 656)
SEC_KVQ = (656, 816)
SEC_B = (816, 1408)

O_WIN = 0
O_WOUTA = 64
O_WGUA = 128
O_WDA = 480
O_WK = 656
O_WV = 704
O_WQ = 752
O_WOUTB = 816
O_WGUB = 880
O_WDB = 1232

WIN_ORDER = (6, 7, 4, 5, 0, 1, 2, 3)

G_AMIX, G_AFFN, G_KV, G_BMIX, G_BFFN, G_FINAL, G_MEM = range(7)
G_SCALE = 56


class Tile:
    __slots__ = ("name", "w", "readers")

    def __init__(self, name):
        self.name = name
        self.w = None
        self.readers = []


class DmaSem:
    __slots__ = ("name", "count", "handle")

    def __init__(self, name):
        self.name = name
        self.count = 0
        self.handle = None


class _Instr:
    __slots__ = ("fn", "waits", "late", "signal", "dsem", "idx")

    def __init__(self, fn):
        self.fn = fn
        self.waits = []
        self.late = []
        self.signal = False
        self.dsem = None
        self.idx = 0


ENGS = ("pe", "act", "dve", "pool", "sp")
LATE_DEFAULT = 1


class Prog:
    def __init__(self):
        self.ins = {e: [] for e in ENGS}
        self.clock = {e: {} for e in ENGS}
        self.dsems = []

    def dsem(self, name):
        d = DmaSem(name)
        self.dsems.append(d)
        return d

    def _record(self, eng, fn, reads, writes, dsem=None, update=True, nlhs=None):
        ins = _Instr(fn)
        lst = self.ins[eng]
        lst.append(ins)
        ins.idx = len(lst)
        clock = self.clock[eng]
        need = []
        early_keys = set()
        for n, t in enumerate(reads):
            if t.w is not None:
                need.append(t.w)
                if nlhs is not None and n < nlhs:
                    early_keys.add(t.w[0])
        for t in writes:
            if t.w is not None:
                need.append(t.w)
            need.extend(t.readers)
        best = {}
        for ev in need:
            key, val, evclock = ev
            if key == "pe" and eng == "pe":
                continue
            if clock.get(key, 0) >= val:
                continue
            if best.get(key, (0, None))[0] < val:
                best[key] = (val, evclock)
        for key, (val, evclock) in best.items():
            if clock.get(key, 0) >= val:
                continue
            if nlhs is not None and key not in early_keys:
                ins.late.append((key, val))
            else:
                ins.waits.append((key, val))
            if isinstance(key, str):
                self.ins[key][val - 1].signal = True
            for k2, v2 in evclock.items():
                if clock.get(k2, 0) < v2:
                    clock[k2] = v2
            clock[key] = max(clock.get(key, 0), val)
        if dsem is not None:
            dsem.count += 16
            ins.dsem = dsem
            evclock = dict(clock)
            evclock[dsem] = dsem.count
            ev = (dsem, dsem.count, evclock)
        else:
            evclock = dict(clock)
            evclock[eng] = ins.idx
            ev = (eng, ins.idx, evclock)
        if update:
            for t in writes:
                t.w = ev
                t.readers = []
            for t in reads:
                t.readers.append(ev)
        return ins

    def op(self, eng, fn, reads=(), writes=(), nlhs=None):
        if eng == "pe" and nlhs is None:
            nlhs = LATE_DEFAULT
        if eng != "pe":
            nlhs = None
        return self._record(eng, fn, list(reads), list(writes), nlhs=nlhs)

    def dma(self, eng, dsem, out_ap, in_ap, reads=(), writes=()):
        def fn(e, out_ap=out_ap, in_ap=in_ap):
            return e.dma_start(out=out_ap, in_=in_ap)

        return self._record(eng, fn, list(reads), list(writes), dsem=dsem)

    def wait_all(self, eng, tiles):
        return self._record(eng, None, list(tiles), list(tiles), update=False)

    def emit(self, nc):
        with ExitStack() as st:
            sems = {}
            for e in ENGS:
                sems[e] = st.enter_context(nc.semaphore("s_" + e))
            for d in self.dsems:
                d.handle = st.enter_context(nc.semaphore("d_" + d.name))
            block = st.enter_context(nc.Block())
            rank = {}
            for e in ENGS:
                r = 0
                rk = []
                for ins in self.ins[e]:
                    if ins.signal:
                        r += 1
                    rk.append(r)
                rank[e] = rk

            def semval(key, val):
                if isinstance(key, str):
                    return sems[key], rank[key][val - 1]
                return key.handle, val

            def run(e, eh):
                for ins in self.ins[e]:
                    late = list(ins.late)
                    late.sort(key=lambda kv: 1 if isinstance(kv[0], str) and kv[0] in ("dve", "act") else 0)
                    attach = late.pop() if (late and ins.fn is not None) else None
                    for key, val in list(ins.waits) + late:
                        sh, sv = semval(key, val)
                        eh.wait_ge(sh, sv)
                    if ins.fn is None:
                        continue
                    bi = ins.fn(eh)
                    if attach is not None:
                        sh, sv = semval(*attach)
                        bi._wait_ge(sh, sv)
                    if ins.dsem is not None:
                        bi.then_inc(ins.dsem.handle, 16)
                    elif ins.signal:
                        bi.then_inc(sems[e], 1)

            @block.tensor
            def _(eh):
                run("pe", eh)

            @block.scalar
            def _(eh):
                run("act", eh)

            @block.vector
            def _(eh):
                run("dve", eh)

            @block.gpsimd
            def _(eh):
                run("pool", eh)

            @block.sync
            def _(eh):
                run("sp", eh)


def build_nc(NT):
    S = NT * T
    NKB = S // 128
    nc = bass.Bass("TRN2", target_bir_lowering=False)
    xT_d = nc.dram_tensor("xT", [D, S], F32, kind="ExternalInput").ap()
    memT_d = nc.dram_tensor("memT", [D, 256], F32, kind="ExternalInput").ap()
    wall_d = nc.dram_tensor("wall", [128, NBLK * 128], F32, kind="ExternalInput").ap()
    wgrp_d = nc.dram_tensor("wgrp", [128, 14 * 128], F32, kind="ExternalInput").ap()
    wmem_d = nc.dram_tensor("wmem", [128, 2 * 8 * 512], F32, kind="ExternalInput").ap()
    cst_d = nc.dram_tensor("cst", [128, 8 * 128], F32, kind="ExternalInput").ap()
    gvec_d = nc.dram_tensor("gvec", [128, 64], F32, kind="ExternalInput").ap()
    fac_d = nc.dram_tensor("fac", [128, 64], F32, kind="ExternalInput").ap()
    outT_d = nc.dram_tensor("outT", [D, S], F32, kind="ExternalOutput").ap()
    wsc_d = nc.dram_tensor("wsc", [128, NBLK * 128], BF16).ap()
    kTs_d = nc.dram_tensor("kTs", [6 * 128, S], BF16).ap()

    P = Prog()
    with ExitStack() as st:
        def sb(name, shape, dt):
            return st.enter_context(nc.sbuf_tensor(name, shape, dt))

        v_sb = sb("vc", [128, NKB * 768], BF16)
        kst_sb = sb("kst", [128, 2 * S], BF16)
        xs = [sb(f"x{p}", [128, NCH * T], F32) for p in range(2)]
        h_sb = sb("h", [128, NCH * T], BF16)
        q_sb = sb("q", [128, NCH * T], BF16)
        cats = [sb(f"cat{p}", [128, NCH * T], BF16) for p in range(2)]
        big_sb = sb("big", [128, 5632], F32)
        sbt_sb = sb("sbt", [128, 3840], F32)
        ring_sb = sb("ring", [128, NSLOT * PIECE * 128], BF16)
        wg_sb = sb("wg", [128, 14 * 128], BF16)
        mk_sb = sb("mk", [128, 2 * 2 * 256], BF16)
        mvp_sb = sb("mvp", [128, 2 * 2 * 4 * 128], BF16)
        cst_sb = sb("cstb", [128, 8 * 128], BF16)
        gvec_sb = sb("gvec_s", [128, 64], F32)
        fac_sb = sb("fac_s", [128, 64], F32)
        halo_sb = sb("halo", [128, 6 * 16], F32)
        rkv_sb = sb("rkv", [128, T], F32)
        Fs = [sb(f"fs{k}", [128, 528], F32) for k in range(4)]
        Bs = [sb(f"bs{k}", [128, 512], BF16) for k in range(4)]
        ps = [st.enter_context(nc.psum_tensor(f"ps{k}", [128, 512], F32)) for k in range(8)]

        hid_ap = big_sb[:].bitcast(BF16)
        big_bf = hid_ap
        pooled_ap = big_sb[:, 3168:3168 + 1536].bitcast(BF16)
        sbt_bf = sbt_sb[:].bitcast(BF16)

        X = [[Tile(f"x{p}_{c}") for c in range(NCH)] for p in range(2)]
        CAT = [[Tile(f"cat{p}_{c}") for c in range(NCH)] for p in range(2)]
        H = [Tile(f"h{c}") for c in range(NCH)]
        Q = [Tile(f"q{c}") for c in range(NCH)]
        HID = [Tile(f"hid{f}") for f in range(NF)]
        U = [Tile(f"u{c}") for c in range(6)]
        PL = [Tile(f"pl{c}") for c in range(6)]
        HQ = [Tile(f"hq{c}") for c in range(NCH)]
        OST = [Tile(f"ost{c}") for c in range(NCH)]
        RING = [Tile(f"ring{k}") for k in range(NSLOT)]
        FT = [Tile(f"F{k}") for k in range(4)]
        BT = [Tile(f"B{k}") for k in range(4)]
        PS = [Tile(f"PS{k}") for k in range(8)]
        KS = [Tile(f"ks{s}") for s in range(2)]
        KD = [Tile(f"kd{hc}") for hc in range(6)]
        VT = [Tile(f"v{kb}") for kb in range(NKB)]
        WG = Tile("wg"); MK = Tile("mk"); MVP = Tile("mvp"); CST = Tile("cst"); GV = Tile("gvec"); FAC = Tile("fac")
        RKV = Tile("rkv")
        WSC = [Tile(f"wsc{p}") for p in range(NPIECE)]
        OUTT = Tile("outT")
        HALO = [Tile(f"halo{c}") for c in range(6)]
        SE = [Tile(f"sbE{j}") for j in range(3)]
        SW = [Tile(f"sbW{j}") for j in range(2)]
        SS = [Tile(f"sbS{j}") for j in range(3)]
        SWT = [Tile(f"sbT{j}") for j in range(2)]

        BIGREG = []
        for c in range(6):
            BIGREG.append((U[c], c * 2112, (c + 1) * 2112, "u"))
            BIGREG.append((PL[c], 12672 + c * 1024, 12672 + (c + 1) * 1024, "u"))
        for f in range(NF):
            BIGREG.append((HID[f], f * 1024, (f + 1) * 1024, "hid"))
        for c in range(NCH):
            BIGREG.append((HQ[c], c * 1024, (c + 1) * 1024, "hq"))
            BIGREG.append((OST[c], c * 2048, (c + 1) * 2048, "out"))

        def big_ov(tile):
            for (t, lo, hi, fam) in BIGREG:
                if t is tile:
                    break
            return [t2 for (t2, lo2, hi2, fam2) in BIGREG if fam2 != fam and lo < hi2 and lo2 < hi]

        def sE(j, lo, hi):
            return sbt_sb[:, j * 512 + lo:j * 512 + hi]

        def sW(j, lo, hi):
            return sbt_sb[:, 1536 + j * 512 + lo:1536 + j * 512 + hi]

        def sS(j, lo, hi):
            return sbt_bf[:, 5120 + j * 512 + lo:5120 + j * 512 + hi]

        def sT(j, lo, hi):
            return sbt_bf[:, 6656 + j * 512 + lo:6656 + j * 512 + hi]

        def xa(p, c, lo=0, hi=T):
            return xs[p][:, c * T + lo:c * T + hi]

        def cata(p, c, lo=0, hi=T, pr=slice(0, 128)):
            return cats[p][pr, c * T + lo:c * T + hi]

        def ha(c, lo=0, hi=T):
            return h_sb[:, c * T + lo:c * T + hi]

        def hqa(c):
            return big_bf[:, c * T:(c + 1) * T]

        def qa(c, lo=0, hi=T, pr=slice(0, 128)):
            return q_sb[pr, c * T + lo:c * T + hi]

        def hida(f):
            return hid_ap[:, f * T:(f + 1) * T]

        def ua(c, lo, hi, pr=slice(0, 128)):
            return big_sb[pr, c * 528 + lo:c * 528 + hi]

        def pla(c, lo=0, hi=T, pr=slice(0, 128)):
            return pooled_ap[pr, c * T + lo:c * T + hi]

        def csta(k):
            return cst_sb[:, k * 128:(k + 1) * 128]

        IDENT, ONESD, NEGINCL, NEGREST, MASKB, ONESP0, ONESP1, ZERO = [csta(k) for k in range(8)]

        def gcol(k, c):
            return gvec_sb[:, 8 * k + c:8 * k + c + 1]

        def mm(out_ap, lhsT, rhs, start, stop, sgc=False):
            if sgc:
                return lambda e: e.matmul(out_ap, lhsT=lhsT, rhs=rhs, start=start, stop=stop, skip_group_check=True)
            return lambda e: e.matmul(out_ap, lhsT=lhsT, rhs=rhs, start=start, stop=stop)

        d_cst = P.dsem("cst"); d_gv = P.dsem("gv"); d_fac = P.dsem("fac"); d_wg = P.dsem("wg")
        d_wm = [P.dsem("wm0"), P.dsem("wm1")]
        d_x = [P.dsem("x0"), P.dsem("x1")]
        d_out = P.dsem("out")
        d_ring = [P.dsem(f"ring{k}") for k in range(NSLOT)]
        d_ring2 = [P.dsem(f"ringb{k}") for k in range(NSLOT)]
        d_wst = [P.dsem(f"wst{k}") for k in range(NSLOT)]
        d_ks = [P.dsem("ks0"), P.dsem("ks1")]
        d_kd = [P.dsem(f"kd{hc}") for hc in range(6)]

        class Banks:
            def __init__(self, banks, statb):
                self.banks = banks
                self.statb = statb
                self.n = 0

            def get(self):
                k = self.banks[self.n % len(self.banks)]
                self.n += 1
                return k

        BK_MAIN = Banks((0, 1, 2, 3), 7)
        BK_A = Banks((0, 1, 2), 3)

        def sec_pieces(sec):
            return list(range(sec[0] // PIECE, sec[1] // PIECE))

        seq = []
        sec_start = {}
        sec_start[("A", 0)] = len(seq); seq += sec_pieces(SEC_A)
        for i in range(NT):
            sec_start[("KVQ", i)] = len(seq); seq += sec_pieces(SEC_KVQ)
            if i + 1 < NT:
                sec_start[("A", i + 1)] = len(seq); seq += sec_pieces(SEC_A)
            sec_start[("B", i)] = len(seq); seq += sec_pieces(SEC_B)
        SECS = {"A": SEC_A, "KVQ": SEC_KVQ, "B": SEC_B}
        wstate = {"issued": 0, "cur": -1}

        seen_piece = set()

        def issue_piece(g):
            p = seq[g]
            s = g % NSLOT
            slot = ring_sb[:, s * PIECE * 128:(s + 1) * PIECE * 128]
            if p not in seen_piece:
                seen_piece.add(p)
                P.dma("pool", d_ring[s], slot, wall_d[:, p * PIECE * 128:(p + 1) * PIECE * 128], reads=[], writes=[RING[s]])
                P.dma("sp", d_wst[s], wsc_d[:, p * PIECE * 128:(p + 1) * PIECE * 128], slot, reads=[RING[s]], writes=[WSC[p]])
            else:
                P.dma("sp", d_ring2[s], slot, wsc_d[:, p * PIECE * 128:(p + 1) * PIECE * 128], reads=[WSC[p]], writes=[RING[s]])

        def wblk(kind, tile, j, n=1):
            sec = SECS[kind]
            assert sec[0] <= j < sec[1] and (j % PIECE) + n <= PIECE
            g = sec_start[(kind, tile)] + (j - sec[0]) // PIECE
            assert g >= wstate["cur"], (kind, tile, j, g, wstate["cur"])
            if g > wstate["cur"]:
                wstate["cur"] = g
                while wstate["issued"] < min(g + NSLOT, len(seq)):
                    issue_piece(wstate["issued"])
                    wstate["issued"] += 1
            s = g % NSLOT
            off = s * PIECE * 128 + (j % PIECE) * 128
            return ring_sb[:, off:off + n * 128], RING[s]

        def drain(gen):
            if gen is None:
                return
            for _ in gen:
                pass

        P.dma("pool", d_cst, cst_sb[:], cst_d, writes=[CST])
        P.dma("sp", d_gv, gvec_sb[:], gvec_d, writes=[GV])
        P.dma("sp", d_fac, fac_sb[:], fac_d, writes=[FAC])
        P.dma("pool", d_wg, wg_sb[:], wgrp_d, writes=[WG])
        for L in range(2):
            P.dma("pool", d_wm[L], ring_sb[:, L * 4096:(L + 1) * 4096], wmem_d[:, L * 4096:(L + 1) * 4096],
                  writes=[RING[2 * L], RING[2 * L + 1]])
        P.dma("sp", d_x[0], xs[0][:, 0:2048].rearrange("p (c m) -> p c m", c=8),
              memT_d.rearrange("(c p) m -> p c m", p=128), writes=X[0][0:4])
        P.op("dve", lambda e: e.memset(mvp_sb[:], 0.0), writes=[MVP])
        P.op("dve", lambda e: e.memset(halo_sb[:], 0.0), writes=HALO)

        def memx(c):
            return xs[0][:, c * 256:(c + 1) * 256]

        def memxt(c):
            return X[0][c // 2]

        msp = BK_MAIN.get()
        for c in range(NCH):
            b = c % 2
            eng = "pool" if c % 2 == 0 else "dve"
            P.op(eng, lambda e, c=c, b=b: e.tensor_tensor(out=Bs[b][:, 0:256], in0=memx(c), in1=memx(c), op=ALU.mult),
                 reads=[memxt(c)], writes=[BT[b]])
            P.op("pe", mm(ps[msp][:, 0:256], ONESD, Bs[b][:, 0:256], c == 0, c == NCH - 1), reads=[CST, BT[b]], writes=[PS[msp]])
        P.op("act", lambda e: e.activation(out=Fs[3][:, 0:256], in_=ps[msp][:, 0:256], func=AF.Ln, bias=EPS),
             reads=[PS[msp]], writes=[FT[3]])
        P.op("act", lambda e: e.activation(out=Fs[2][:, 0:256], in_=Fs[3][:, 0:256], func=AF.Exp, scale=-0.5),
             reads=[FT[3]], writes=[FT[2]])
        for c in range(NCH):
            P.op("dve", lambda e, c=c: e.scalar_tensor_tensor(out=ha(c, 0, 256), in0=memx(c), scalar=gcol(G_MEM, c),
                                                                in1=Fs[2][:, 0:256], op0=ALU.mult, op1=ALU.mult),
                 reads=[memxt(c), GV, FT[2]], writes=[H[c]])
        for L in range(2):
            RL = [RING[2 * L], RING[2 * L + 1]]

            def wm(c, j, n=1, L=L):
                off = L * 4096 + c * 512 + j * 128
                return ring_sb[:, off:off + n * 128]
            for fc in range(2):
                k = BK_MAIN.get()
                for c in range(NCH):
                    P.op("pe", mm(ps[k][:, 0:256], wm(c, fc), ha(c, 0, 256), c == 0, c == NCH - 1),
                         reads=RL + [H[c]], writes=[PS[k]], nlhs=2)
                P.op("act", lambda e, k=k, L=L, fc=fc: e.activation(
                    out=mk_sb[:, (L * 2 + fc) * 256:(L * 2 + fc + 1) * 256], in_=ps[k][:, 0:256], func=AF.Copy),
                    reads=[PS[k]], writes=[MK])
            for mb in range(2):
                k = BK_MAIN.get()
                for c in range(NCH):
                    P.op("pe", mm(ps[k][:, 0:256], ha(c, mb * 128, mb * 128 + 128), wm(c, 2, 2), c == 0, c == NCH - 1),
                         reads=[H[c]] + RL, writes=[PS[k]])
                for hh in range(4):
                    base = ((L * 2 + mb) * 4 + hh) * 128 + (hh % 2) * 64
                    P.op("dve", lambda e, k=k, base=base, hh=hh: e.tensor_copy(
                        out=mvp_sb[:, base:base + 64], in_=ps[k][:, hh * 64:(hh + 1) * 64]),
                        reads=[PS[k]], writes=[MVP])

        def mka(L, hh, mb):
            hc, half = divmod(hh, 2)
            base = (L * 2 + hc) * 256 + mb * 128
            return mk_sb[half * 64:half * 64 + 64, base:base + 128]

        def mvpa(L, mb, hh):
            base = ((L * 2 + mb) * 4 + hh) * 128
            return mvp_sb[:, base:base + 128]

        pending = []

        def flush_pending():
            while pending:
                pending.pop(0)()

        sqrot = [0]

        def stat_chunk(c, k, src_ap, src_tile, eng=None, defer=False):
            b = sqrot[0] % 2
            sqrot[0] += 1
            if eng is None:
                eng = "pool" if c % 2 == 0 else "dve"
            P.op(eng, lambda e, c=c, b=b: e.tensor_tensor(out=Bs[b][:, 0:T], in0=src_ap(c), in1=src_ap(c), op=ALU.mult),
                 reads=[src_tile(c)], writes=[BT[b]])

            def do_mm():
                P.op("pe", mm(ps[k][:, 0:T], ONESD, Bs[b][:, 0:T], c == 0, c == NCH - 1),
                     reads=[CST, BT[b]], writes=[PS[k]])
            if defer:
                pending.append(do_mm)
            else:
                do_mm()

        def rstd_from(k, dst_ap=None, dst_tile=None):
            flush_pending()
            if dst_ap is None:
                dst_ap, dst_tile = Fs[2][:, 0:T], FT[2]
            P.op("act", lambda e: e.activation(out=Fs[3][:, 0:T], in_=ps[k][:, 0:T], func=AF.Ln, bias=EPS),
                 reads=[PS[k]], writes=[FT[3]])
            P.op("act", lambda e: e.activation(out=dst_ap, in_=Fs[3][:, 0:T], func=AF.Exp, scale=-0.5),
                 reads=[FT[3]], writes=[dst_tile])

        def norm_apply(p, gk, rap=None, rtile=None):
            if rap is None:
                rap, rtile = Fs[2][:, 0:T], FT[2]
            for c in range(NCH):
                P.op("dve", lambda e, c=c: e.scalar_tensor_tensor(out=ha(c), in0=xa(p, c), scalar=gcol(gk, c), in1=rap,
                                                                    op0=ALU.mult, op1=ALU.mult),
                     reads=[X[p][c], GV, rtile], writes=[H[c]])

        def proj(bk, kind, tile, o_base, oi, evac, sap, stl):
            k = bk.get()
            for c in range(NCH):
                wap, wt = wblk(kind, tile, o_base + oi * NCH + c)
                P.op("pe", mm(ps[k][:, :], wap, sap(c), c == 0, c == NCH - 1), reads=[wt, stl[c]], writes=[PS[k]])
                yield
            evac(k)

        def resid_add(p, oc, statb):
            def ev(k):
                P.op("dve", lambda e: e.tensor_tensor(out=xa(p, oc), in0=xa(p, oc), in1=ps[k][:, :], op=ALU.add),
                     reads=[X[p][oc], PS[k]], writes=[X[p][oc]])
                flush_pending()
                stat_chunk(oc, statb, lambda c: xa(p, c), lambda c: X[p][c], eng="pool", defer=True)
            return ev

        def mem_attention(L, p, OB, ZB, zbanks):
            for hc in range(2):
                pts = []
                for half in range(2):
                    hh = hc * 2 + half
                    pr = slice(half * 64, half * 64 + 64)
                    for mb in range(2):
                        zk = zbanks[len(pts) % len(zbanks)]
                        bslot = (hc * 4 + len(pts)) % 4
                        P.op("pe", mm(ps[zk][:, :], mka(L, hh, mb), qa(6 + hc, pr=pr), True, True),
                             reads=[MK, Q[6 + hc]], writes=[PS[zk]])
                        yield
                        P.op("act", lambda e, zk=zk, bslot=bslot: e.activation(out=Bs[bslot][:, :], in_=ps[zk][:, :], func=AF.Exp),
                             reads=[PS[zk]], writes=[BT[bslot]])
                        pts.append((bslot, hh, mb, half))
                for n, (bslot, hh, mb, half) in enumerate(pts):
                    P.op("pe", mm(ps[OB][:, :], mvpa(L, mb, hh), Bs[bslot][:, :], n == 0, n == 3),
                         reads=[MVP, BT[bslot]], writes=[PS[OB]])
                    yield
                for n, (bslot, hh, mb, half) in enumerate(pts):
                    P.op("pe", mm(ps[ZB][:, :], ONESP0 if half == 0 else ONESP1, Bs[bslot][:, :], n == 0, n == 3),
                         reads=[CST, BT[bslot]], writes=[PS[ZB]])
                    yield
                P.op("act", lambda e: e.activation(out=Fs[3][:, 0:T], in_=ps[ZB][:, :], func=AF.Ln), reads=[PS[ZB]], writes=[FT[3]])
                P.op("act", lambda e: e.activation(out=Fs[3][:, 0:T], in_=Fs[3][:, 0:T], func=AF.Exp, scale=-1.0), reads=[FT[3]], writes=[FT[3]])
                P.op("dve", lambda e, hc=hc: e.tensor_tensor(out=cata(p, 6 + hc), in0=ps[OB][:, :], in1=Fs[3][:, 0:T], op=ALU.mult),
                     reads=[PS[OB], FT[3]], writes=[CAT[p][6 + hc]])

        def ffn(bk, kind, tile, p, gk, o_gu, o_d, after_down=None):
            rstd_from(bk.statb)
            norm_apply(p, gk)
            for f in range(NF):
                kg = bk.get()
                ku = bk.get()
                for c in range(NCH):
                    wap, wt = wblk(kind, tile, o_gu + f * 16 + c)
                    P.op("pe", mm(ps[kg][:, :], wap, ha(c), c == 0, c == NCH - 1), reads=[wt, H[c]], writes=[PS[kg]])
                    yield
                for c in range(NCH):
                    wap, wt = wblk(kind, tile, o_gu + f * 16 + 8 + c)
                    P.op("pe", mm(ps[ku][:, :], wap, ha(c), c == 0, c == NCH - 1), reads=[wt, H[c]], writes=[PS[ku]])
                    yield
                fs = f % 2
                P.op("act", lambda e, kg=kg, fs=fs: e.activation(out=Fs[fs][:, 0:T], in_=ps[kg][:, :], func=AF.Exp, scale=-1.0),
                     reads=[PS[kg]], writes=[FT[fs]])
                P.op("act", lambda e, fs=fs: e.activation(out=Fs[fs][:, 0:T], in_=Fs[fs][:, 0:T], func=AF.Ln, bias=1.0),
                     reads=[FT[fs]], writes=[FT[fs]])
                P.op("act", lambda e, fs=fs: e.activation(out=Fs[fs][:, 0:T], in_=Fs[fs][:, 0:T], func=AF.Exp, scale=-1.0),
                     reads=[FT[fs]], writes=[FT[fs]])
                P.op("dve", lambda e, kg=kg, fs=fs: e.tensor_tensor(out=Fs[fs][:, 0:T], in0=Fs[fs][:, 0:T], in1=ps[kg][:, :], op=ALU.mult),
                     reads=[FT[fs], PS[kg]], writes=[FT[fs]])
                P.op("dve", lambda e, ku=ku, fs=fs, f=f: e.tensor_tensor(out=hida(f), in0=Fs[fs][:, 0:T], in1=ps[ku][:, :], op=ALU.mult),
                     reads=[FT[fs], PS[ku]], writes=[HID[f]] + big_ov(HID[f]))
            for oc in range(NCH):
                k = bk.get()
                for f in range(NF):
                    wap, wt = wblk(kind, tile, o_d + oc * NF + f)
                    P.op("pe", mm(ps[k][:, :], wap, hida(f), f == 0, f == NF - 1), reads=[wt, HID[f]], writes=[PS[k]])
                    yield
                resid_add(p, oc, bk.statb)(k)

        POOLCFG = [(0, 0, 128, 2), (1, 0, 128, 4), (2, 0, 128, 8), (3, 0, 128, 16),
                   (4, 0, 64, 2), (4, 64, 128, 4), (5, 0, 64, 8), (5, 64, 128, 16)]

        def pool_chunk(i, c, eng, fa, fb):
            cfgs = [cf for cf in POOLCFG if cf[0] == c]
            wmax = max(cf[3] for cf in cfgs)
            src_ap = lambda lo, hi: ua(c, lo, hi)
            src_t = U[c]
            slots = [fa, fb]
            sums = {}
            step = 1
            n = 0
            while step < wmax:
                dst = slots[n % 2]
                lo = 2 * step - 1
                P.op(eng, lambda e, dst=dst, lo=lo, step=step, src_ap=src_ap: e.tensor_tensor(
                    out=Fs[dst][:, lo:528], in0=src_ap(lo, 528), in1=src_ap(lo - step, 528 - step), op=ALU.add),
                    reads=[src_t], writes=[FT[dst]])
                step *= 2
                sums[step] = dst
                src_ap = (lambda lo, hi, dst=dst: Fs[dst][:, lo:hi])
                src_t = FT[dst]
                n += 1
            for (_, p0, p1, w) in cfgs:
                s = sums[w]
                pr = slice(p0, p1)
                widx = {2: 0, 4: 1, 8: 2, 16: 3}[w]
                if i == 0:
                    P.op(eng, lambda e, s=s, pr=pr, widx=widx: e.tensor_tensor(
                        out=Fs[s][pr, 16:32], in0=Fs[s][pr, 16:32], in1=fac_sb[pr, widx * 16:(widx + 1) * 16], op=ALU.mult),
                        reads=[FT[s], FAC], writes=[FT[s]])
                P.op("dve", lambda e, s=s, pr=pr, w=w: e.scalar_tensor_tensor(
                    out=pla(c, pr=pr), in0=Fs[s][pr, 16:528], scalar=1.0 / w, in1=ua(c, 16, 528, pr=pr),
                    op0=ALU.mult, op1=ALU.subtract),
                    reads=[FT[s], U[c]], writes=[PL[c]] + big_ov(PL[c]))
            P.op(eng, lambda e: e.tensor_copy(out=halo_sb[:, c * 16:(c + 1) * 16], in_=ua(c, 512, 528)), reads=[U[c]], writes=[HALO[c]])

        GROUP_PLAN = [(0, [(0, 0), (1, 4)]), (1, [(2, 1), (3, 4)]), (2, [(4, 2), (5, 5)]), (3, [(6, 3), (7, 5)]),
                      (4, [(8, 0), (9, 1), (10, 4)]), (5, [(11, 2), (12, 3), (13, 5)])]

        def layer_a(i):
            p = i % 2
            bk = BK_A
            t0 = i * T
            P.dma("sp", d_x[p], xs[p][:].rearrange("p (c t) -> p c t", c=NCH),
                  xT_d.rearrange("(c p) s -> p c s", p=128)[:, :, t0:t0 + T], reads=[], writes=X[p])
            k = bk.get()
            for c in range(NCH):
                stat_chunk(c, k, lambda c: xa(p, c), lambda c: X[p][c])
                yield
            rstd_from(k)
            norm_apply(p, G_AMIX)
            for oi, oc in enumerate(WIN_ORDER):
                if oc < 6:
                    def ev(k, oc=oc):
                        P.op("dve", lambda e: e.tensor_copy(out=ua(oc, 16, 528), in_=ps[k][:, :]),
                             reads=[PS[k]], writes=[U[oc]] + big_ov(U[oc]))
                        P.op("pool", lambda e: e.tensor_copy(out=ua(oc, 0, 16), in_=halo_sb[:, oc * 16:(oc + 1) * 16]),
                             reads=[HALO[oc]], writes=[U[oc]])
                else:
                    def ev(k, oc=oc):
                        P.op("dve", lambda e: e.tensor_scalar_mul(out=qa(oc), in0=ps[k][:, :], scalar1=0.125),
                             reads=[PS[k]], writes=[Q[oc]])
                yield from proj(bk, "A", i, O_WIN, oi, ev, ha, H)
            for c in (4, 5):
                pool_chunk(i, c, "pool", 2, 3)
            for c in range(4):
                pool_chunk(i, c, "dve", 0, 1)
            yield from mem_attention(0, p, 0, 1, (2,))
            bk.n = 0
            for (oc, plan) in GROUP_PLAN:
                k = bk.get()
                for n, (bi, pc) in enumerate(plan):
                    P.op("pe", mm(ps[k][:, :], wg_sb[:, bi * 128:(bi + 1) * 128], pla(pc), n == 0, n == len(plan) - 1),
                         reads=[WG, PL[pc]], writes=[PS[k]])
                    yield
                P.op("dve", lambda e, k=k, oc=oc: e.tensor_scalar_mul(out=cata(p, oc), in0=ps[k][:, :],
                                                                       scalar1=gvec_sb[:, G_SCALE + oc:G_SCALE + oc + 1]),
                     reads=[PS[k], GV], writes=[CAT[p][oc]])
            for oc in range(NCH):
                yield from proj(bk, "A", i, O_WOUTA, oc, resid_add(p, oc, bk.statb), lambda c: cata(p, c), CAT[p])
            yield from ffn(bk, "A", i, p, G_AFFN, O_WGUA, O_WDA)
            rstd_from(bk.statb, rkv_sb[:, :], RKV)
            yield

        def kvq(i):
            p = i % 2
            bk = BK_MAIN
            t0 = i * T
            norm_apply(p, G_KV, rkv_sb[:, :], RKV)
            for c in range(NCH):
                P.op("dve", lambda e, c=c: e.scalar_tensor_tensor(out=hqa(c), in0=xa(p, c), scalar=gcol(G_BMIX, c), in1=rkv_sb[:, :],
                                                                    op0=ALU.mult, op1=ALU.mult),
                     reads=[X[p][c], GV, RKV], writes=[HQ[c]] + big_ov(HQ[c]))
            for hc in range(6):
                def ev(k, hc=hc):
                    eng = "dve" if hc % 2 == 0 else "act"
                    if eng == "dve":
                        P.op("dve", lambda e: e.tensor_copy(out=cata(p, hc), in_=ps[k][:, :]), reads=[PS[k]], writes=[CAT[p][hc]])
                    else:
                        P.op("act", lambda e: e.activation(out=cata(p, hc), in_=ps[k][:, :], func=AF.Copy), reads=[PS[k]], writes=[CAT[p][hc]])
                    P.dma("sp", d_kd[hc], kTs_d[hc * 128:(hc + 1) * 128, t0:t0 + T], cata(p, hc), reads=[CAT[p][hc]], writes=[KD[hc]])
                yield from proj(bk, "KVQ", i, O_WK, hc, ev, ha, H)
            kbanks = [bk.get() for _ in range(4)]
            for c in range(NCH):
                wap, wt = wblk("KVQ", i, O_WV + c * 4, 4)
                for tb in range(4):
                    P.op("pe", mm(ps[kbanks[tb]][:, :], ha(c, tb * 128, tb * 128 + 128), wap, c == 0, c == NCH - 1),
                         reads=[H[c], wt], writes=[PS[kbanks[tb]]])
                    yield
            for tb in range(4):
                kb = 4 * i + tb
                if tb % 2 == 0:
                    P.op("dve", lambda e, tb=tb, kb=kb: e.tensor_copy(out=v_sb[:, kb * 768:kb * 768 + 512], in_=ps[kbanks[tb]][:, :]),
                         reads=[PS[kbanks[tb]]], writes=[VT[kb]])
                else:
                    P.op("act", lambda e, tb=tb, kb=kb: e.activation(out=v_sb[:, kb * 768:kb * 768 + 512], in_=ps[kbanks[tb]][:, :], func=AF.Copy),
                         reads=[PS[kbanks[tb]]], writes=[VT[kb]])
            k2 = [bk.get() for _ in range(4)]
            for c in range(NCH):
                wap, wt = wblk("KVQ", i, O_WV + 32 + c * 2, 2)
                for tb in range(4):
                    kk = k2[tb]
                    col = 0
                    P.op("pe", mm(ps[kk][:, col:col + 256], ha(c, tb * 128, tb * 128 + 128), wap, c == 0, c == NCH - 1),
                         reads=[H[c], wt], writes=[PS[kk]])
                    yield
            for tb in range(4):
                kb = 4 * i + tb
                kk = k2[tb]
                col = 0
                P.op("act", lambda e, kk=kk, col=col, kb=kb: e.activation(out=v_sb[:, kb * 768 + 512:kb * 768 + 768], in_=ps[kk][:, col:col + 256], func=AF.Copy),
                     reads=[PS[kk]], writes=[VT[kb]])
            for oc in range(NCH):
                def ev(k, oc=oc):
                    if oc % 2 == 0:
                        P.op("act", lambda e: e.activation(out=qa(oc), in_=ps[k][:, :], func=AF.Copy, scale=0.125),
                             reads=[PS[k]], writes=[Q[oc]])
                    else:
                        P.op("dve", lambda e: e.tensor_scalar_mul(out=qa(oc), in0=ps[k][:, :], scalar1=0.125),
                             reads=[PS[k]], writes=[Q[oc]])
                yield from proj(bk, "KVQ", i, O_WQ, oc, ev, hqa, HQ)

        def sb_tile(i, gen):
            p = i % 2
            nkb = 4 * (i + 1)
            npre = 4 * i * 128
            steps = [(hh, kb) for hh in range(NHEAD) for kb in range(nkb - 1, -1, -1)]
            N = len(steps)
            info = {}
            ACC, OB = 4, 5
            live = {"gen": gen}

            def pull():
                g = live["gen"]
                if g is None:
                    return False
                try:
                    next(g)
                    return True
                except StopIteration:
                    live["gen"] = None
                    return False

            def load_kstage(hc):
                s = hc % 2
                if npre > 0:
                    P.dma("sp", d_ks[s], kst_sb[:, s * S:s * S + npre], kTs_d[hc * 128:(hc + 1) * 128, 0:npre],
                          reads=[KD[hc]], writes=[KS[s]])
                P.dma("sp", d_ks[s], kst_sb[:, s * S + npre:s * S + npre + T], cata(p, hc), reads=[CAT[p][hc]], writes=[KS[s]])

            load_kstage(0)
            load_kstage(1)

            def hparams(hh):
                hc, half = divmod(hh, 2)
                pr = slice(half * 64, half * 64 + 64)
                return hc, pr

            def S1(n):
                hh, kb = steps[n]
                hc, pr = hparams(hh)
                if hh % 2 == 0 and kb == nkb - 1 and 1 <= hc and hc + 1 < 6:
                    load_kstage(hc + 1)
                j = kb - 4 * i
                c0 = 128 * j if j > 0 else 0
                zk = 6 + n % 2
                es = n % 3
                bs = n % 3
                s = hc % 2
                ka = kst_sb[pr, s * S + kb * 128:s * S + kb * 128 + 128]
                if j >= 0:
                    P.op("pe", mm(ps[zk][:, c0:c0 + 128], IDENT, MASKB, True, False), reads=[CST], writes=[PS[zk]])
                    P.op("pe", mm(ps[zk][:, c0:c0 + 128], ka, qa(hc, c0, c0 + 128, pr=pr), False, True),
                         reads=[KS[s], Q[hc]], writes=[PS[zk]])
                    if c0 + 128 < T:
                        P.op("pe", mm(ps[zk][:, c0 + 128:T], ka, qa(hc, c0 + 128, T, pr=pr), True, True),
                             reads=[KS[s], Q[hc]], writes=[PS[zk]])
                else:
                    P.op("pe", mm(ps[zk][:, :], ka, qa(hc, pr=pr), True, True), reads=[KS[s], Q[hc]], writes=[PS[zk]])
                P.op("act", lambda e: e.activation(out=sE(es, c0, T), in_=ps[zk][:, c0:T], func=AF.Exp),
                     reads=[PS[zk]], writes=[SE[es]])
                P.op("act", lambda e: e.activation(out=sS(bs, c0, T), in_=sE(es, c0, T), func=AF.Ln, bias=1.0),
                     reads=[SE[es]], writes=[SS[bs]])
                info[n] = (c0, es, bs)

            def S2(n):
                hh, kb = steps[n]
                hc, pr = hparams(hh)
                c0, es, bs = info[n]
                ws = n % 2
                wb = n % 2
                if kb == nkb - 1:
                    P.op("pe", mm(ps[ACC][:, :], ZERO, qa(0), True, False, sgc=True), reads=[CST, Q[0]], writes=[PS[ACC]])
                    P.op("pe", mm(ps[OB][:, :], ZERO, qa(0), True, False, sgc=True), reads=[CST, Q[0]], writes=[PS[OB]])
                P.op("pe", mm(ps[ACC][:, c0:T], NEGINCL, sS(bs, c0, T), False, True, sgc=True), reads=[CST, SS[bs]], writes=[PS[ACC]])
                P.op("act", lambda e: e.activation(out=sW(ws, c0, T), in_=ps[ACC][:, c0:T], func=AF.Exp),
                     reads=[PS[ACC]], writes=[SW[ws]])
                P.op("dve", lambda e: e.tensor_tensor(out=sT(wb, c0, T), in0=sE(es, c0, T), in1=sW(ws, c0, T), op=ALU.mult),
                     reads=[SE[es], SW[ws]], writes=[SWT[wb]])

            def S3a(n):
                hh, kb = steps[n]
                c0, es, bs = info[n]
                if kb > 0:
                    P.op("pe", mm(ps[ACC][:, c0:T], NEGREST, sS(bs, c0, T), False, True, sgc=True), reads=[CST, SS[bs]], writes=[PS[ACC]])

            def S3b(n):
                hh, kb = steps[n]
                hc, pr = hparams(hh)
                c0, es, bs = info[n]
                wb = n % 2
                P.op("pe", mm(ps[OB][:, c0:T], v_sb[:, kb * 768 + hc * 128:kb * 768 + hc * 128 + 128], sT(wb, c0, T), False, kb == 0, sgc=True),
                     reads=[VT[kb], SWT[wb]], writes=[PS[OB]])
                if kb == 0:
                    P.op("dve", lambda e: e.tensor_copy(out=cata(p, hc, pr=pr), in_=ps[OB][pr, :]), reads=[PS[OB]], writes=[CAT[p][hc]])

            def filler():
                P.op("pe", mm(ps[OB][:, 0:FILL_N], ZERO, qa(0, 0, FILL_N), False, False, sgc=True), reads=[CST, Q[0]], writes=[PS[OB]])

            def gap_work(npull, nfill):
                got = 0
                for _ in range(npull):
                    if pull():
                        got += 1
                if live["gen"] is None:
                    for _ in range(max(0, nfill - got)):
                        filler()

            for k in range(-2, N):
                if 0 <= k + 2 < N:
                    S1(k + 2)
                if 0 <= k < N:
                    gap_work(PULL_A, 2)
                    S3a(k)
                head_start = (0 <= k + 1 < N) and steps[k + 1][1] == nkb - 1
                if head_start and 0 <= k < N:
                    gap_work(RPULL - PULL_A, 1)
                    S3b(k)
                    S2(k + 1)
                else:
                    if 0 <= k + 1 < N:
                        S2(k + 1)
                    if 0 <= k < N:
                        gap_work(RPULL - PULL_A, 1)
                        S3b(k)
            return live["gen"]

        drain(layer_a(0))
        for i in range(NT):
            p = i % 2
            t0 = i * T
            drain(kvq(i))
            drain(mem_attention(1, p, 4, 5, (6, 7)))
            gen = layer_a(i + 1) if i + 1 < NT else None
            gen = sb_tile(i, gen)
            drain(gen)
            for oc in range(NCH):
                drain(proj(BK_MAIN, "B", i, O_WOUTB, oc, resid_add(p, oc, BK_MAIN.statb), lambda c: cata(p, c), CAT[p]))
            drain(ffn(BK_MAIN, "B", i, p, G_BFFN, O_WGUB, O_WDB))
            rstd_from(BK_MAIN.statb)
            for c in range(NCH):
                P.op("dve", lambda e, c=c, p=p: e.scalar_tensor_tensor(out=big_sb[:, c * T:(c + 1) * T], in0=xa(p, c), scalar=gcol(G_FINAL, c),
                                                                    in1=Fs[2][:, 0:T], op0=ALU.mult, op1=ALU.mult),
                     reads=[X[p][c], GV, FT[2]], writes=[OST[c]] + big_ov(OST[c]))
            P.dma("sp", d_out, outT_d.rearrange("(c p) s -> p c s", p=128)[:, :, t0:t0 + T],
                  big_sb[:, 0:NCH * T].rearrange("p (c t) -> p c t", c=NCH), reads=OST, writes=[OUTT])
        P.wait_all("sp", [OUTT])
        P.emit(nc)
    return nc


def _blk(W, kc, mc):
    return W[kc * 128:(kc + 1) * 128, mc * 128:(mc + 1) * 128]


def _pool_perm():
    perm = []
    for g in range(4):
        perm.extend(range(192 * g, 192 * g + 128))
    for g in range(4):
        perm.extend(range(192 * g + 128, 192 * g + 192))
    return np.array(perm, dtype=np.int64)


def pack_weights(inp):
    f32 = np.float32
    perm = _pool_perm()
    full_perm = np.concatenate([perm, np.arange(768, 1024)])
    blocks = []
    w_in = np.asarray(inp["a_w_in"][0], f32)[:, full_perm]
    for oc in WIN_ORDER:
        for c in range(8):
            blocks.append(_blk(w_in, c, oc))
    w_out_a = np.asarray(inp["a_w_out"][0], f32)[full_perm, :]
    for oc in range(8):
        for c in range(8):
            blocks.append(_blk(w_out_a, c, oc))

    def ffn_blocks(w_gu, w_d):
        for f in range(NF):
            for c in range(8):
                blocks.append(_blk(w_gu, c, f))
            for c in range(8):
                blocks.append(_blk(w_gu, c, NF + f))
        for oc in range(8):
            for f in range(NF):
                blocks.append(_blk(w_d, f, oc))

    ffn_blocks(np.asarray(inp["a_w_gu"][0], f32), np.asarray(inp["a_w_down"][0], f32))
    w_kv = np.asarray(inp["w_kv"], f32)
    for hc in range(6):
        for c in range(8):
            blocks.append(_blk(w_kv, c, hc))
    for c in range(8):
        for j in range(4):
            blocks.append(_blk(w_kv, c, 6 + j))
    for c in range(8):
        for j in range(2):
            blocks.append(_blk(w_kv, c, 10 + j))
    w_q = np.asarray(inp["b_w_q"][0], f32)
    for oc in range(8):
        for c in range(8):
            blocks.append(_blk(w_q, c, oc))
    w_out_b = np.asarray(inp["b_w_out"][0], f32)
    for oc in range(8):
        for c in range(8):
            blocks.append(_blk(w_out_b, c, oc))
    ffn_blocks(np.asarray(inp["b_w_gu"][0], f32), np.asarray(inp["b_w_down"][0], f32))
    assert len(blocks) == NBLK
    wall = np.ascontiguousarray(np.stack(blocks, axis=1).reshape(128, NBLK * 128))

    wgp = np.asarray(inp["a_w_group"][0], f32)
    z = np.zeros((128, 128), f32)
    gb = []
    for g in range(4):
        a = wgp[g][0:128, 0:128]
        b = z.copy()
        r0 = (g % 2) * 64
        b[r0:r0 + 64, :] = wgp[g][128:192, 0:128]
        gb.extend([a, b])
    for pair in range(2):
        g0, g1 = 2 * pair, 2 * pair + 1
        c0 = z.copy(); c0[:, 0:64] = wgp[g0][0:128, 128:192]
        c1 = z.copy(); c1[:, 64:128] = wgp[g1][0:128, 128:192]
        dd = z.copy(); dd[0:64, 0:64] = wgp[g0][128:192, 128:192]; dd[64:128, 64:128] = wgp[g1][128:192, 128:192]
        gb.extend([c0, c1, dd])
    wgrp = np.ascontiguousarray(np.stack(gb, axis=1).reshape(128, 14 * 128))

    wm = []
    for key in ("a_w_mem_kv", "b_w_mem_kv"):
        w = np.asarray(inp[key][0], f32)
        wm.append(w.reshape(8, 128, 512).transpose(1, 0, 2).reshape(128, 8 * 512))
    wmem = np.ascontiguousarray(np.concatenate(wm, axis=1))

    idx = np.arange(128)
    ident = np.eye(128, dtype=f32)
    onesd = np.full((128, 128), 1.0 / D, f32)
    negincl = np.where(idx[:, None] >= idx[None, :], -1.0, 0.0).astype(f32)
    negrest = np.where(idx[:, None] < idx[None, :], -1.0, 0.0).astype(f32)
    maskb = np.where(idx[:, None] < idx[None, :], 0.0, -30000.0).astype(f32)
    onesp0 = np.zeros((128, 128), f32); onesp0[:, 0:64] = 1.0
    onesp1 = np.zeros((128, 128), f32); onesp1[:, 64:128] = 1.0
    cst = np.ascontiguousarray(np.concatenate([ident, onesd, negincl, negrest, maskb, onesp0, onesp1, z], axis=1))

    gvec = np.zeros((128, 64), f32)
    for k, g in enumerate([inp["a_norm_mix"][0], inp["a_norm_ffn"][0], inp["kv_norm"], inp["b_norm_mix"][0],
                           inp["b_norm_ffn"][0], inp["final_norm"], inp["mem_norm"]]):
        gvec[:, 8 * k:8 * k + 8] = np.asarray(g, f32).reshape(8, 128).T
    gvec[:, G_SCALE:G_SCALE + 6] = np.asarray(inp["a_scale"][0], f32)[perm].reshape(6, 128).T
    fac = np.zeros((128, 64), f32)
    for widx, w in enumerate((2, 4, 8, 16)):
        t = np.arange(16)
        fac[:, widx * 16:(widx + 1) * 16] = (w / np.minimum(t + 1, w)).astype(f32)[None, :]
    return dict(wall=wall, wgrp=wgrp, wmem=wmem, cst=cst, gvec=gvec, fac=fac)


def kernel(**inputs):
    x = np.asarray(inputs["x"], np.float32)
    mem = np.asarray(inputs["mem"], np.float32)
    B, S, _ = x.shape
    NT = S // T
    shared = pack_weights(inputs)
    nc = build_nc(NT)
    in_maps = []
    for b in range(B):
        m = dict(shared)
        m["xT"] = np.ascontiguousarray(x[b].T)
        m["memT"] = np.ascontiguousarray(mem[b].T)
        in_maps.append(m)
    res = run_bass_kernel_spmd(nc, in_maps, core_ids=list(range(B)))
    out = np.stack([np.asarray(r["outT"], np.float32).T for r in res.results], axis=0)
    return np.ascontiguousarray(out)
```

```python
from contextlib import ExitStack

import numpy as np
import concourse.bass as bass
import concourse.mybir as mybir
from concourse.bass_utils import run_bass_kernel_spmd

F32 = mybir.dt.float32
BF16 = mybir.dt.bfloat16
AF = mybir.ActivationFunctionType
ALU = mybir.AluOpType

D = 1024
NCH = 8
T = 512
DFF = 2816
NF = 22
NHEAD = 12
EPS = 1e-6
FILL = 2
FILL_N = 256
RPULL = 4
PULL_A = 2
NSLOT = 4
PIECE = 16
CONVB = 32
NBLK = 1408
NPIECE = NBLK // PIECE
NCONV = NBLK // CONVB
SEC_A = (0, 656)
SEC_KVQ = (656, 816)
SEC_B = (816, 1408)

O_WIN = 0
O_WOUTA = 64
O_WGUA = 128
O_WDA = 480
O_WK = 656
O_WV = 704
O_WQ = 752
O_WOUTB = 816
O_WGUB = 880
O_WDB = 1232

WIN_ORDER = (6, 7, 4, 5, 0, 1, 2, 3)

G_AMIX, G_AFFN, G_KV, G_BMIX, G_BFFN, G_FINAL, G_MEM = range(7)
G_SCALE = 56


class Tile:
    __slots__ = ("name", "w", "readers")

    def __init__(self, name):
        self.name = name
        self.w = None
        self.readers = []


class DmaSem:
    __slots__ = ("name", "count", "handle")

    def __init__(self, name):
        self.name = name
        self.count = 0
        self.handle = None


class _Instr:
    __slots__ = ("fn", "waits", "late", "signal", "dsem", "idx")

    def __init__(self, fn):
        self.fn = fn
        self.waits = []
        self.late = []
        self.signal = False
        self.dsem = None
        self.idx = 0


ENGS = ("pe", "act", "dve", "pool", "sp")
LATE_DEFAULT = 1


class Prog:
    def __init__(self):
        self.ins = {e: [] for e in ENGS}
        self.clock = {e: {} for e in ENGS}
        self.dsems = []

    def dsem(self, name):
        d = DmaSem(name)
        self.dsems.append(d)
        return d

    def _record(self, eng, fn, reads, writes, dsem=None, update=True, nlhs=None):
        ins = _Instr(fn)
        lst = self.ins[eng]
        lst.append(ins)
        ins.idx = len(lst)
        clock = self.clock[eng]
        need = []
        early_keys = set()
        for n, t in enumerate(reads):
            if t.w is not None:
                need.append(t.w)
                if nlhs is not None and n < nlhs:
                    early_keys.add(t.w[0])
        for t in writes:
            if t.w is not None:
                need.append(t.w)
            need.extend(t.readers)
        best = {}
        for ev in need:
            key, val, evclock = ev
            if key == "pe" and eng == "pe":
                continue
            if clock.get(key, 0) >= val:
                continue
            if best.get(key, (0, None))[0] < val:
                best[key] = (val, evclock)
        for key, (val, evclock) in best.items():
            if clock.get(key, 0) >= val:
                continue
            if nlhs is not None and key not in early_keys:
                ins.late.append((key, val))
            else:
                ins.waits.append((key, val))
            if isinstance(key, str):
                self.ins[key][val - 1].signal = True
            for k2, v2 in evclock.items():
                if clock.get(k2, 0) < v2:
                    clock[k2] = v2
            clock[key] = max(clock.get(key, 0), val)
        if dsem is not None:
            dsem.count += 16
            ins.dsem = dsem
            evclock = dict(clock)
            evclock[dsem] = dsem.count
            ev = (dsem, dsem.count, evclock)
        else:
            evclock = dict(clock)
            evclock[eng] = ins.idx
            ev = (eng, ins.idx, evclock)
        if update:
            for t in writes:
                t.w = ev
                t.readers = []
            for t in reads:
                t.readers.append(ev)
        return ins

    def op(self, eng, fn, reads=(), writes=(), nlhs=None):
        if eng == "pe" and nlhs is None:
            nlhs = LATE_DEFAULT
        if eng != "pe":
            nlhs = None
        return self._record(eng, fn, list(reads), list(writes), nlhs=nlhs)

    def dma(self, eng, dsem, out_ap, in_ap, reads=(), writes=()):
        def fn(e, out_ap=out_ap, in_ap=in_ap):
            return e.dma_start(out=out_ap, in_=in_ap)

        return self._record(eng, fn, list(reads), list(writes), dsem=dsem)

    def wait_all(self, eng, tiles):
        return self._record(eng, None, list(tiles), list(tiles), update=False)

    def emit(self, nc):
        with ExitStack() as st:
            sems = {}
            for e in ENGS:
                sems[e] = st.enter_context(nc.semaphore("s_" + e))
            for d in self.dsems:
                d.handle = st.enter_context(nc.semaphore("d_" + d.name))
            block = st.enter_context(nc.Block())
            rank = {}
            for e in ENGS:
                r = 0
                rk = []
                for ins in self.ins[e]:
                    if ins.signal:
                        r += 1
                    rk.append(r)
                rank[e] = rk

            def semval(key, val):
                if isinstance(key, str):
                    return sems[key], rank[key][val - 1]
                return key.handle, val

            def run(e, eh):
                for ins in self.ins[e]:
                    late = list(ins.late)
                    late.sort(key=lambda kv: 1 if isinstance(kv[0], str) and kv[0] in ("dve", "act") else 0)
                    attach = late.pop() if (late and ins.fn is not None) else None
                    for key, val in list(ins.waits) + late:
                        sh, sv = semval(key, val)
                        eh.wait_ge(sh, sv)
                    if ins.fn is None:
                        continue
                    bi = ins.fn(eh)
                    if attach is not None:
                        sh, sv = semval(*attach)
                        bi._wait_ge(sh, sv)
                    if ins.dsem is not None:
                        bi.then_inc(ins.dsem.handle, 16)
                    elif ins.signal:
                        bi.then_inc(sems[e], 1)

            @block.tensor
            def _(eh):
                run("pe", eh)

            @block.scalar
            def _(eh):
                run("act", eh)

            @block.vector
            def _(eh):
                run("dve", eh)

            @block.gpsimd
            def _(eh):
                run("pool", eh)

            @block.sync
            def _(eh):
                run("sp", eh)


def build_nc(NT):
    S = NT * T
    NKB = S // 128
    nc = bass.Bass("TRN2", target_bir_lowering=False)
    xT_d = nc.dram_tensor("xT", [D, S], F32, kind="ExternalInput").ap()
    memT_d = nc.dram_tensor("memT", [D, 256], F32, kind="ExternalInput").ap()
    wall_d = nc.dram_tensor("wall", [128, NBLK * 128], F32, kind="ExternalInput").ap()
    wgrp_d = nc.dram_tensor("wgrp", [128, 14 * 128], F32, kind="ExternalInput").ap()
    wmem_d = nc.dram_tensor("wmem", [128, 2 * 8 * 512], F32, kind="ExternalInput").ap()
    cst_d = nc.dram_tensor("cst", [128, 8 * 128], F32, kind="ExternalInput").ap()
    gvec_d = nc.dram_tensor("gvec", [128, 64], F32, kind="ExternalInput").ap()
    fac_d = nc.dram_tensor("fac", [128, 64], F32, kind="ExternalInput").ap()
    outT_d = nc.dram_tensor("outT", [D, S], F32, kind="ExternalOutput").ap()
    wsc_d = nc.dram_tensor("wsc", [128, NBLK * 128], BF16).ap()
    kTs_d = nc.dram_tensor("kTs", [6 * 128, S], BF16).ap()

    P = Prog()
    with ExitStack() as st:
        def sb(name, shape, dt):
            return st.enter_context(nc.sbuf_tensor(name, shape, dt))

        v_sb = sb("vc", [128, NKB * 768], BF16)
        kst_sb = sb("kst", [128, 2 * S], BF16)
        xs = [sb(f"x{p}", [128, NCH * T], F32) for p in range(2)]
        h_sb = sb("h", [128, NCH * T], BF16)
        q_sb = sb("q", [128, NCH * T], BF16)
        cats = [sb(f"cat{p}", [128, NCH * T], BF16) for p in range(2)]
        big_sb = sb("big", [128, 5632], F32)
        sbt_sb = sb("sbt", [128, 3840], F32)
        ring_sb = sb("ring", [128, NSLOT * PIECE * 128], BF16)
        wg_sb = sb("wg", [128, 14 * 128], BF16)
        mk_sb = sb("mk", [128, 2 * 2 * 256], BF16)
        mvp_sb = sb("mvp", [128, 2 * 2 * 4 * 128], BF16)
        cst_sb = sb("cstb", [128, 8 * 128], BF16)
        gvec_sb = sb("gvec_s", [128, 64], F32)
        fac_sb = sb("fac_s", [128, 64], F32)
        halo_sb = sb("halo", [128, 6 * 16], F32)
        rkv_sb = sb("rkv", [128, T], F32)
        Fs = [sb(f"fs{k}", [128, 528], F32) for k in range(4)]
        Bs = [sb(f"bs{k}", [128, 512], BF16) for k in range(4)]
        ps = [st.enter_context(nc.psum_tensor(f"ps{k}", [128, 512], F32)) for k in range(8)]

        hid_ap = big_sb[:].bitcast(BF16)
        big_bf = hid_ap
        pooled_ap = big_sb[:, 3168:3168 + 1536].bitcast(BF16)
        sbt_bf = sbt_sb[:].bitcast(BF16)

        X = [[Tile(f"x{p}_{c}") for c in range(NCH)] for p in range(2)]
        CAT = [[Tile(f"cat{p}_{c}") for c in range(NCH)] for p in range(2)]
        H = [Tile(f"h{c}") for c in range(NCH)]
        Q = [Tile(f"q{c}") for c in range(NCH)]
        HID = [Tile(f"hid{f}") for f in range(NF)]
        U = [Tile(f"u{c}") for c in range(6)]
        PL = [Tile(f"pl{c}") for c in range(6)]
        HQ = [Tile(f"hq{c}") for c in range(NCH)]
        OST = [Tile(f"ost{c}") for c in range(NCH)]
        RING = [Tile(f"ring{k}") for k in range(NSLOT)]
        FT = [Tile(f"F{k}") for k in range(4)]
        BT = [Tile(f"B{k}") for k in range(4)]
        PS = [Tile(f"PS{k}") for k in range(8)]
        KS = [Tile(f"ks{s}") for s in range(2)]
        KD = [Tile(f"kd{hc}") for hc in range(6)]
        VT = [Tile(f"v{kb}") for kb in range(NKB)]
        WG = Tile("wg"); MK = Tile("mk"); MVP = Tile("mvp"); CST = Tile("cst"); GV = Tile("gvec"); FAC = Tile("fac")
        RKV = Tile("rkv")
        WSC = [Tile(f"wsc{p}") for p in range(NPIECE)]
        OUTT = Tile("outT")
        HALO = [Tile(f"halo{c}") for c in range(6)]
        SE = [Tile(f"sbE{j}") for j in range(3)]
        SW = [Tile(f"sbW{j}") for j in range(2)]
        SS = [Tile(f"sbS{j}") for j in range(3)]
        SWT = [Tile(f"sbT{j}") for j in range(2)]

        BIGREG = []
        for c in range(6):
            BIGREG.append((U[c], c * 2112, (c + 1) * 2112, "u"))
            BIGREG.append((PL[c], 12672 + c * 1024, 12672 + (c + 1) * 1024, "u"))
        for f in range(NF):
            BIGREG.append((HID[f], f * 1024, (f + 1) * 1024, "hid"))
        for c in range(NCH):
            BIGREG.append((HQ[c], c * 1024, (c + 1) * 1024, "hq"))
            BIGREG.append((OST[c], c * 2048, (c + 1) * 2048, "out"))

        def big_ov(tile):
            for (t, lo, hi, fam) in BIGREG:
                if t is tile:
                    break
            return [t2 for (t2, lo2, hi2, fam2) in BIGREG if fam2 != fam and lo < hi2 and lo2 < hi]

        def sE(j, lo, hi):
            return sbt_sb[:, j * 512 + lo:j * 512 + hi]

        def sW(j, lo, hi):
            return sbt_sb[:, 1536 + j * 512 + lo:1536 + j * 512 + hi]

        def sS(j, lo, hi):
            return sbt_bf[:, 5120 + j * 512 + lo:5120 + j * 512 + hi]

        def sT(j, lo, hi):
            return sbt_bf[:, 6656 + j * 512 + lo:6656 + j * 512 + hi]

        def xa(p, c, lo=0, hi=T):
            return xs[p][:, c * T + lo:c * T + hi]

        def cata(p, c, lo=0, hi=T, pr=slice(0, 128)):
            return cats[p][pr, c * T + lo:c * T + hi]

        def ha(c, lo=0, hi=T):
            return h_sb[:, c * T + lo:c * T + hi]

        def hqa(c):
            return big_bf[:, c * T:(c + 1) * T]

        def qa(c, lo=0, hi=T, pr=slice(0, 128)):
            return q_sb[pr, c * T + lo:c * T + hi]

        def hida(f):
            return hid_ap[:, f * T:(f + 1) * T]

        def ua(c, lo, hi, pr=slice(0, 128)):
            return big_sb[pr, c * 528 + lo:c * 528 + hi]

        def pla(c, lo=0, hi=T, pr=slice(0, 128)):
            return pooled_ap[pr, c * T + lo:c * T + hi]

        def csta(k):
            return cst_sb[:, k * 128:(k + 1) * 128]

        IDENT, ONESD, NEGINCL, NEGREST, MASKB, ONESP0, ONESP1, ZERO = [csta(k) for k in range(8)]

        def gcol(k, c):
            return gvec_sb[:, 8 * k + c:8 * k + c + 1]

        def mm(out_ap, lhsT, rhs, start, stop, sgc=False):
            if sgc:
                return lambda e: e.matmul(out_ap, lhsT=lhsT, rhs=rhs, start=start, stop=stop, skip_group_check=True)
            return lambda e: e.matmul(out_ap, lhsT=lhsT, rhs=rhs, start=start, stop=stop)

        d_cst = P.dsem("cst"); d_gv = P.dsem("gv"); d_fac = P.dsem("fac"); d_wg = P.dsem("wg")
        d_wm = [P.dsem("wm0"), P.dsem("wm1")]
        d_x = [P.dsem("x0"), P.dsem("x1")]
        d_out = P.dsem("out")
        d_ring = [P.dsem(f"ring{k}") for k in range(NSLOT)]
        d_ring2 = [P.dsem(f"ringb{k}") for k in range(NSLOT)]
        d_wst = [P.dsem(f"wst{k}") for k in range(NSLOT)]
        d_ks = [P.dsem("ks0"), P.dsem("ks1")]
        d_kd = [P.dsem(f"kd{hc}") for hc in range(6)]

        class Banks:
            def __init__(self, banks, statb):
                self.banks = banks
                self.statb = statb
                self.n = 0

            def get(self):
                k = self.banks[self.n % len(self.banks)]
                self.n += 1
                return k

        BK_MAIN = Banks((0, 1, 2, 3, 4, 5, 6), 7)
        BK_A = Banks((0, 1, 2), 3)

        def sec_pieces(sec):
            return list(range(sec[0] // PIECE, sec[1] // PIECE))

        seq = []
        sec_start = {}
        sec_start[("A", 0)] = len(seq); seq += sec_pieces(SEC_A)
        for i in range(NT):
            sec_start[("KVQ", i)] = len(seq); seq += sec_pieces(SEC_KVQ)
            if i + 1 < NT:
                sec_start[("A", i + 1)] = len(seq); seq += sec_pieces(SEC_A)
            sec_start[("B", i)] = len(seq); seq += sec_pieces(SEC_B)
        SECS = {"A": SEC_A, "KVQ": SEC_KVQ, "B": SEC_B}
        wstate = {"issued": 0, "cur": -1}

        seen_piece = set()

        def issue_piece(g):
            p = seq[g]
            s = g % NSLOT
            slot = ring_sb[:, s * PIECE * 128:(s + 1) * PIECE * 128]
            if p not in seen_piece:
                seen_piece.add(p)
                P.dma("pool", d_ring[s], slot, wall_d[:, p * PIECE * 128:(p + 1) * PIECE * 128], reads=[], writes=[RING[s]])
                P.dma("sp", d_wst[s], wsc_d[:, p * PIECE * 128:(p + 1) * PIECE * 128], slot, reads=[RING[s]], writes=[WSC[p]])
            else:
                P.dma("sp", d_ring2[s], slot, wsc_d[:, p * PIECE * 128:(p + 1) * PIECE * 128], reads=[WSC[p]], writes=[RING[s]])

        def wblk(kind, tile, j, n=1):
            sec = SECS[kind]
            assert sec[0] <= j < sec[1] and (j % PIECE) + n <= PIECE
            g = sec_start[(kind, tile)] + (j - sec[0]) // PIECE
            assert g >= wstate["cur"], (kind, tile, j, g, wstate["cur"])
            if g > wstate["cur"]:
                wstate["cur"] = g
                while wstate["issued"] < min(g + NSLOT, len(seq)):
                    issue_piece(wstate["issued"])
                    wstate["issued"] += 1
            s = g % NSLOT
            off = s * PIECE * 128 + (j % PIECE) * 128
            return ring_sb[:, off:off + n * 128], RING[s]

        def drain(gen):
            if gen is None:
                return
            for _ in gen:
                pass

        P.dma("pool", d_cst, cst_sb[:], cst_d, writes=[CST])
        P.dma("sp", d_gv, gvec_sb[:], gvec_d, writes=[GV])
        P.dma("sp", d_fac, fac_sb[:], fac_d, writes=[FAC])
        P.dma("pool", d_wg, wg_sb[:], wgrp_d, writes=[WG])
        for L in range(2):
            P.dma("pool", d_wm[L], ring_sb[:, L * 4096:(L + 1) * 4096], wmem_d[:, L * 4096:(L + 1) * 4096],
                  writes=[RING[2 * L], RING[2 * L + 1]])
        P.dma("sp", d_x[0], xs[0][:, 0:2048].rearrange("p (c m) -> p c m", c=8),
              memT_d.rearrange("(c p) m -> p c m", p=128), writes=X[0][0:4])
        P.op("dve", lambda e: e.memset(mvp_sb[:], 0.0), writes=[MVP])
        P.op("dve", lambda e: e.memset(halo_sb[:], 0.0), writes=HALO)

        def memx(c):
            return xs[0][:, c * 256:(c + 1) * 256]

        def memxt(c):
            return X[0][c // 2]

        msp = BK_MAIN.get()
        for c in range(NCH):
            b = c % 2
            eng = "pool" if c % 2 == 0 else "dve"
            P.op(eng, lambda e, c=c, b=b: e.tensor_tensor(out=Bs[b][:, 0:256], in0=memx(c), in1=memx(c), op=ALU.mult),
                 reads=[memxt(c)], writes=[BT[b]])
            P.op("pe", mm(ps[msp][:, 0:256], ONESD, Bs[b][:, 0:256], c == 0, c == NCH - 1), reads=[CST, BT[b]], writes=[PS[msp]])
        P.op("act", lambda e: e.activation(out=Fs[3][:, 0:256], in_=ps[msp][:, 0:256], func=AF.Ln, bias=EPS),
             reads=[PS[msp]], writes=[FT[3]])
        P.op("act", lambda e: e.activation(out=Fs[2][:, 0:256], in_=Fs[3][:, 0:256], func=AF.Exp, scale=-0.5),
             reads=[FT[3]], writes=[FT[2]])
        for c in range(NCH):
            P.op("dve", lambda e, c=c: e.scalar_tensor_tensor(out=ha(c, 0, 256), in0=memx(c), scalar=gcol(G_MEM, c),
                                                                in1=Fs[2][:, 0:256], op0=ALU.mult, op1=ALU.mult),
                 reads=[memxt(c), GV, FT[2]], writes=[H[c]])
        for L in range(2):
            RL = [RING[2 * L], RING[2 * L + 1]]

            def wm(c, j, n=1, L=L):
                off = L * 4096 + c * 512 + j * 128
                return ring_sb[:, off:off + n * 128]
            for fc in range(2):
                k = BK_MAIN.get()
                for c in range(NCH):
                    P.op("pe", mm(ps[k][:, 0:256], wm(c, fc), ha(c, 0, 256), c == 0, c == NCH - 1),
                         reads=RL + [H[c]], writes=[PS[k]], nlhs=2)
                P.op("act", lambda e, k=k, L=L, fc=fc: e.activation(
                    out=mk_sb[:, (L * 2 + fc) * 256:(L * 2 + fc + 1) * 256], in_=ps[k][:, 0:256], func=AF.Copy),
                    reads=[PS[k]], writes=[MK])
            for mb in range(2):
                k = BK_MAIN.get()
                for c in range(NCH):
                    P.op("pe", mm(ps[k][:, 0:256], ha(c, mb * 128, mb * 128 + 128), wm(c, 2, 2), c == 0, c == NCH - 1),
                         reads=[H[c]] + RL, writes=[PS[k]])
                for hh in range(4):
                    base = ((L * 2 + mb) * 4 + hh) * 128 + (hh % 2) * 64
                    P.op("dve", lambda e, k=k, base=base, hh=hh: e.tensor_copy(
                        out=mvp_sb[:, base:base + 64], in_=ps[k][:, hh * 64:(hh + 1) * 64]),
                        reads=[PS[k]], writes=[MVP])

        def mka(L, hh, mb):
            hc, half = divmod(hh, 2)
            base = (L * 2 + hc) * 256 + mb * 128
            return mk_sb[half * 64:half * 64 + 64, base:base + 128]

        def mvpa(L, mb, hh):
            base = ((L * 2 + mb) * 4 + hh) * 128
            return mvp_sb[:, base:base + 128]

        pending = []

        def flush_pending():
            while pending:
                pending.pop(0)()

        sqrot = [0]

        def stat_chunk(c, k, src_ap, src_tile, eng=None, defer=False):
            b = sqrot[0] % 2
            sqrot[0] += 1
            if eng is None:
                eng = "pool" if c % 2 == 0 else "dve"
            P.op(eng, lambda e, c=c, b=b: e.tensor_tensor(out=Bs[b][:, 0:T], in0=src_ap(c), in1=src_ap(c), op=ALU.mult),
                 reads=[src_tile(c)], writes=[BT[b]])

            def do_mm():
                P.op("pe", mm(ps[k][:, 0:T], ONESD, Bs[b][:, 0:T], c == 0, c == NCH - 1),
                     reads=[CST, BT[b]], writes=[PS[k]])
            if defer:
                pending.append(do_mm)
            else:
                do_mm()

        def rstd_from(k, dst_ap=None, dst_tile=None):
            flush_pending()
            if dst_ap is None:
                dst_ap, dst_tile = Fs[2][:, 0:T], FT[2]
            P.op("act", lambda e: e.activation(out=Fs[3][:, 0:T], in_=ps[k][:, 0:T], func=AF.Ln, bias=EPS),
                 reads=[PS[k]], writes=[FT[3]])
            P.op("act", lambda e: e.activation(out=dst_ap, in_=Fs[3][:, 0:T], func=AF.Exp, scale=-0.5),
                 reads=[FT[3]], writes=[dst_tile])

        def norm_apply(p, gk, rap=None, rtile=None):
            if rap is None:
                rap, rtile = Fs[2][:, 0:T], FT[2]
            for c in range(NCH):
                P.op("dve", lambda e, c=c: e.scalar_tensor_tensor(out=ha(c), in0=xa(p, c), scalar=gcol(gk, c), in1=rap,
                                                                    op0=ALU.mult, op1=ALU.mult),
                     reads=[X[p][c], GV, rtile], writes=[H[c]])

        def proj(bk, kind, tile, o_base, oi, evac, sap, stl):
            k = bk.get()
            for c in range(NCH):
                wap, wt = wblk(kind, tile, o_base + oi * NCH + c)
                P.op("pe", mm(ps[k][:, :], wap, sap(c), c == 0, c == NCH - 1), reads=[wt, stl[c]], writes=[PS[k]])
                yield
            evac(k)

        def resid_add(p, oc, statb):
            def ev(k):
                P.op("dve", lambda e: e.tensor_tensor(out=xa(p, oc), in0=xa(p, oc), in1=ps[k][:, :], op=ALU.add),
                     reads=[X[p][oc], PS[k]], writes=[X[p][oc]])
                flush_pending()
                stat_chunk(oc, statb, lambda c: xa(p, c), lambda c: X[p][c], eng="pool", defer=True)
            return ev

        def mem_attention(L, p, OB, ZB, zbanks):
            for hc in range(2):
                pts = []
                for half in range(2):
                    hh = hc * 2 + half
                    pr = slice(half * 64, half * 64 + 64)
                    for mb in range(2):
                        zk = zbanks[len(pts) % len(zbanks)]
                        bslot = (hc * 4 + len(pts)) % 4
                        P.op("pe", mm(ps[zk][:, :], mka(L, hh, mb), qa(6 + hc, pr=pr), True, True),
                             reads=[MK, Q[6 + hc]], writes=[PS[zk]])
                        yield
                        P.op("act", lambda e, zk=zk, bslot=bslot: e.activation(out=Bs[bslot][:, :], in_=ps[zk][:, :], func=AF.Exp),
                             reads=[PS[zk]], writes=[BT[bslot]])
                        pts.append((bslot, hh, mb, half))
                for n, (bslot, hh, mb, half) in enumerate(pts):
                    P.op("pe", mm(ps[OB][:, :], mvpa(L, mb, hh), Bs[bslot][:, :], n == 0, n == 3),
                         reads=[MVP, BT[bslot]], writes=[PS[OB]])
                    yield
                for n, (bslot, hh, mb, half) in enumerate(pts):
                    P.op("pe", mm(ps[ZB][:, :], ONESP0 if half == 0 else ONESP1, Bs[bslot][:, :], n == 0, n == 3),
                         reads=[CST, BT[bslot]], writes=[PS[ZB]])
                    yield
                P.op("act", lambda e: e.activation(out=Fs[3][:, 0:T], in_=ps[ZB][:, :], func=AF.Ln), reads=[PS[ZB]], writes=[FT[3]])
                P.op("act", lambda e: e.activation(out=Fs[3][:, 0:T], in_=Fs[3][:, 0:T], func=AF.Exp, scale=-1.0), reads=[FT[3]], writes=[FT[3]])
                P.op("dve", lambda e, hc=hc: e.tensor_tensor(out=cata(p, 6 + hc), in0=ps[OB][:, :], in1=Fs[3][:, 0:T], op=ALU.mult),
                     reads=[PS[OB], FT[3]], writes=[CAT[p][6 + hc]])

        def ffn(bk, kind, tile, p, gk, o_gu, o_d, after_down=None):
            rstd_from(bk.statb)
            norm_apply(p, gk)
            for f in range(NF):
                kg = bk.get()
                ku = bk.get()
                for c in range(NCH):
                    wap, wt = wblk(kind, tile, o_gu + f * 16 + c)
                    P.op("pe", mm(ps[kg][:, :], wap, ha(c), c == 0, c == NCH - 1), reads=[wt, H[c]], writes=[PS[kg]])
                    yield
                for c in range(NCH):
                    wap, wt = wblk(kind, tile, o_gu + f * 16 + 8 + c)
                    P.op("pe", mm(ps[ku][:, :], wap, ha(c), c == 0, c == NCH - 1), reads=[wt, H[c]], writes=[PS[ku]])
                    yield
                fs = f % 2
                P.op("act", lambda e, kg=kg, fs=fs: e.activation(out=Fs[fs][:, 0:T], in_=ps[kg][:, :], func=AF.Exp, scale=-1.0),
                     reads=[PS[kg]], writes=[FT[fs]])
                P.op("act", lambda e, fs=fs: e.activation(out=Fs[fs][:, 0:T], in_=Fs[fs][:, 0:T], func=AF.Ln, bias=1.0),
                     reads=[FT[fs]], writes=[FT[fs]])
                P.op("act", lambda e, fs=fs: e.activation(out=Fs[fs][:, 0:T], in_=Fs[fs][:, 0:T], func=AF.Exp, scale=-1.0),
                     reads=[FT[fs]], writes=[FT[fs]])
                P.op("dve", lambda e, kg=kg, fs=fs: e.tensor_tensor(out=Fs[fs][:, 0:T], in0=Fs[fs][:, 0:T], in1=ps[kg][:, :], op=ALU.mult),
                     reads=[FT[fs], PS[kg]], writes=[FT[fs]])
                P.op("dve", lambda e, ku=ku, fs=fs, f=f: e.tensor_tensor(out=hida(f), in0=Fs[fs][:, 0:T], in1=ps[ku][:, :], op=ALU.mult),
                     reads=[FT[fs], PS[ku]], writes=[HID[f]] + big_ov(HID[f]))
            for oc in range(NCH):
                k = bk.get()
                for f in range(NF):
                    wap, wt = wblk(kind, tile, o_d + oc * NF + f)
                    P.op("pe", mm(ps[k][:, :], wap, hida(f), f == 0, f == NF - 1), reads=[wt, HID[f]], writes=[PS[k]])
                    yield
                resid_add(p, oc, bk.statb)(k)

        POOLCFG = [(0, 0, 128, 2), (1, 0, 128, 4), (2, 0, 128, 8), (3, 0, 128, 16),
                   (4, 0, 64, 2), (4, 64, 128, 4), (5, 0, 64, 8), (5, 64, 128, 16)]

        def pool_chunk(i, c, eng, fa, fb):
            cfgs = [cf for cf in POOLCFG if cf[0] == c]
            wmax = max(cf[3] for cf in cfgs)
            src_ap = lambda lo, hi: ua(c, lo, hi)
            src_t = U[c]
            slots = [fa, fb]
            sums = {}
            step = 1
            n = 0
            while step < wmax:
                dst = slots[n % 2]
                lo = 2 * step - 1
                P.op(eng, lambda e, dst=dst, lo=lo, step=step, src_ap=src_ap: e.tensor_tensor(
                    out=Fs[dst][:, lo:528], in0=src_ap(lo, 528), in1=src_ap(lo - step, 528 - step), op=ALU.add),
                    reads=[src_t], writes=[FT[dst]])
                step *= 2
                sums[step] = dst
                src_ap = (lambda lo, hi, dst=dst: Fs[dst][:, lo:hi])
                src_t = FT[dst]
                n += 1
            for (_, p0, p1, w) in cfgs:
                s = sums[w]
                pr = slice(p0, p1)
                widx = {2: 0, 4: 1, 8: 2, 16: 3}[w]
                if i == 0:
                    P.op(eng, lambda e, s=s, pr=pr, widx=widx: e.tensor_tensor(
                        out=Fs[s][pr, 16:32], in0=Fs[s][pr, 16:32], in1=fac_sb[pr, widx * 16:(widx + 1) * 16], op=ALU.mult),
                        reads=[FT[s], FAC], writes=[FT[s]])
                P.op("dve", lambda e, s=s, pr=pr, w=w: e.scalar_tensor_tensor(
                    out=pla(c, pr=pr), in0=Fs[s][pr, 16:528], scalar=1.0 / w, in1=ua(c, 16, 528, pr=pr),
                    op0=ALU.mult, op1=ALU.subtract),
                    reads=[FT[s], U[c]], writes=[PL[c]] + big_ov(PL[c]))
            P.op(eng, lambda e: e.tensor_copy(out=halo_sb[:, c * 16:(c + 1) * 16], in_=ua(c, 512, 528)), reads=[U[c]], writes=[HALO[c]])

        GROUP_PLAN = [(0, [(0, 0), (1, 4)]), (1, [(2, 1), (3, 4)]), (2, [(4, 2), (5, 5)]), (3, [(6, 3), (7, 5)]),
                      (4, [(8, 0), (9, 1), (10, 4)]), (5, [(11, 2), (12, 3), (13, 5)])]

        def layer_a(i):
            p = i % 2
            bk = BK_A
            t0 = i * T
            P.dma("sp", d_x[p], xs[p][:].rearrange("p (c t) -> p c t", c=NCH),
                  xT_d.rearrange("(c p) s -> p c s", p=128)[:, :, t0:t0 + T], reads=[], writes=X[p])
            k = bk.get()
            for c in range(NCH):
                stat_chunk(c, k, lambda c: xa(p, c), lambda c: X[p][c])
                yield
            rstd_from(k)
            norm_apply(p, G_AMIX)
            for oi, oc in enumerate(WIN_ORDER):
                if oc < 6:
                    def ev(k, oc=oc):
                        P.op("dve", lambda e: e.tensor_copy(out=ua(oc, 16, 528), in_=ps[k][:, :]),
                             reads=[PS[k]], writes=[U[oc]] + big_ov(U[oc]))
                        P.op("pool", lambda e: e.tensor_copy(out=ua(oc, 0, 16), in_=halo_sb[:, oc * 16:(oc + 1) * 16]),
                             reads=[HALO[oc]], writes=[U[oc]])
                else:
                    def ev(k, oc=oc):
                        P.op("dve", lambda e: e.tensor_scalar_mul(out=qa(oc), in0=ps[k][:, :], scalar1=0.125),
                             reads=[PS[k]], writes=[Q[oc]])
                yield from proj(bk, "A", i, O_WIN, oi, ev, ha, H)
            for c in (4, 5):
                pool_chunk(i, c, "pool", 2, 3)
            for c in range(4):
                pool_chunk(i, c, "dve", 0, 1)
            yield from mem_attention(0, p, 0, 1, (2,))
            bk.n = 0
            for (oc, plan) in GROUP_PLAN:
                k = bk.get()
                for n, (bi, pc) in enumerate(plan):
                    P.op("pe", mm(ps[k][:, :], wg_sb[:, bi * 128:(bi + 1) * 128], pla(pc), n == 0, n == len(plan) - 1),
                         reads=[WG, PL[pc]], writes=[PS[k]])
                    yield
                P.op("dve", lambda e, k=k, oc=oc: e.tensor_scalar_mul(out=cata(p, oc), in0=ps[k][:, :],
                                                                       scalar1=gvec_sb[:, G_SCALE + oc:G_SCALE + oc + 1]),
                     reads=[PS[k], GV], writes=[CAT[p][oc]])
            for oc in range(NCH):
                yield from proj(bk, "A", i, O_WOUTA, oc, resid_add(p, oc, bk.statb), lambda c: cata(p, c), CAT[p])
            yield from ffn(bk, "A", i, p, G_AFFN, O_WGUA, O_WDA)
            rstd_from(bk.statb, rkv_sb[:, :], RKV)
            yield

        def kvq(i):
            p = i % 2
            bk = BK_MAIN
            t0 = i * T
            norm_apply(p, G_KV, rkv_sb[:, :], RKV)
            for c in range(NCH):
                P.op("dve", lambda e, c=c: e.scalar_tensor_tensor(out=hqa(c), in0=xa(p, c), scalar=gcol(G_BMIX, c), in1=rkv_sb[:, :],
                                                                    op0=ALU.mult, op1=ALU.mult),
                     reads=[X[p][c], GV, RKV], writes=[HQ[c]] + big_ov(HQ[c]))
            for hc in range(6):
                def ev(k, hc=hc):
                    eng = "dve" if hc % 2 == 0 else "act"
                    if eng == "dve":
                        P.op("dve", lambda e: e.tensor_copy(out=cata(p, hc), in_=ps[k][:, :]), reads=[PS[k]], writes=[CAT[p][hc]])
                    else:
                        P.op("act", lambda e: e.activation(out=cata(p, hc), in_=ps[k][:, :], func=AF.Copy), reads=[PS[k]], writes=[CAT[p][hc]])
                    P.dma("sp", d_kd[hc], kTs_d[hc * 128:(hc + 1) * 128, t0:t0 + T], cata(p, hc), reads=[CAT[p][hc]], writes=[KD[hc]])
                yield from proj(bk, "KVQ", i, O_WK, hc, ev, ha, H)
            kbanks = [bk.get() for _ in range(4)]
            for c in range(NCH):
                wap, wt = wblk("KVQ", i, O_WV + c * 4, 4)
                for tb in range(4):
                    P.op("pe", mm(ps[kbanks[tb]][:, :], ha(c, tb * 128, tb * 128 + 128), wap, c == 0, c == NCH - 1),
                         reads=[H[c], wt], writes=[PS[kbanks[tb]]])
                    yield
            for tb in range(4):
                kb = 4 * i + tb
                if tb % 2 == 0:
                    P.op("dve", lambda e, tb=tb, kb=kb: e.tensor_copy(out=v_sb[:, kb * 768:kb * 768 + 512], in_=ps[kbanks[tb]][:, :]),
                         reads=[PS[kbanks[tb]]], writes=[VT[kb]])
                else:
                    P.op("act", lambda e, tb=tb, kb=kb: e.activation(out=v_sb[:, kb * 768:kb * 768 + 512], in_=ps[kbanks[tb]][:, :], func=AF.Copy),
                         reads=[PS[kbanks[tb]]], writes=[VT[kb]])
            k2 = [bk.get() for _ in range(4)]
            for c in range(NCH):
                wap, wt = wblk("KVQ", i, O_WV + 32 + c * 2, 2)
                for tb in range(4):
                    kk = k2[tb]
                    col = 0
                    P.op("pe", mm(ps[kk][:, col:col + 256], ha(c, tb * 128, tb * 128 + 128), wap, c == 0, c == NCH - 1),
                         reads=[H[c], wt], writes=[PS[kk]])
                    yield
            for tb in range(4):
                kb = 4 * i + tb
                kk = k2[tb]
                col = 0
                P.op("act", lambda e, kk=kk, col=col, kb=kb: e.activation(out=v_sb[:, kb * 768 + 512:kb * 768 + 768], in_=ps[kk][:, col:col + 256], func=AF.Copy),
                     reads=[PS[kk]], writes=[VT[kb]])
            for oc in range(NCH):
                def ev(k, oc=oc):
                    if oc % 2 == 0:
                        P.op("act", lambda e: e.activation(out=qa(oc), in_=ps[k][:, :], func=AF.Copy, scale=0.125),
                             reads=[PS[k]], writes=[Q[oc]])
                    else:
                        P.op("dve", lambda e: e.tensor_scalar_mul(out=qa(oc), in0=ps[k][:, :], scalar1=0.125),
                             reads=[PS[k]], writes=[Q[oc]])
                yield from proj(bk, "KVQ", i, O_WQ, oc, ev, hqa, HQ)

        def sb_tile(i, gen):
            p = i % 2
            nkb = 4 * (i + 1)
            npre = 4 * i * 128
            steps = [(hh, kb) for hh in range(NHEAD) for kb in range(nkb - 1, -1, -1)]
            N = len(steps)
            info = {}
            ACC, OB = 4, 5
            live = {"gen": gen}

            def pull():
                g = live["gen"]
                if g is None:
                    return False
                try:
                    next(g)
                    return True
                except StopIteration:
                    live["gen"] = None
                    return False

            def load_kstage(hc):
                s = hc % 2
                if npre > 0:
                    P.dma("sp", d_ks[s], kst_sb[:, s * S:s * S + npre], kTs_d[hc * 128:(hc + 1) * 128, 0:npre],
                          reads=[KD[hc]], writes=[KS[s]])
                P.dma("sp", d_ks[s], kst_sb[:, s * S + npre:s * S + npre + T], cata(p, hc), reads=[CAT[p][hc]], writes=[KS[s]])

            load_kstage(0)
            load_kstage(1)

            def hparams(hh):
                hc, half = divmod(hh, 2)
                pr = slice(half * 64, half * 64 + 64)
                return hc, pr

            def S1(n):
                hh, kb = steps[n]
                hc, pr = hparams(hh)
                if hh % 2 == 0 and kb == nkb - 1 and 1 <= hc and hc + 1 < 6:
                    load_kstage(hc + 1)
                j = kb - 4 * i
                c0 = 128 * j if j > 0 else 0
                zk = 6 + n % 2
                es = n % 3
                bs = n % 3
                s = hc % 2
                ka = kst_sb[pr, s * S + kb * 128:s * S + kb * 128 + 128]
                if j >= 0:
                    P.op("pe", mm(ps[zk][:, c0:c0 + 128], IDENT, MASKB, True, False), reads=[CST], writes=[PS[zk]])
                    P.op("pe", mm(ps[zk][:, c0:c0 + 128], ka, qa(hc, c0, c0 + 128, pr=pr), False, True),
                         reads=[KS[s], Q[hc]], writes=[PS[zk]])
                    if c0 + 128 < T:
                        P.op("pe", mm(ps[zk][:, c0 + 128:T], ka, qa(hc, c0 + 128, T, pr=pr), True, True),
                             reads=[KS[s], Q[hc]], writes=[PS[zk]])
                else:
                    P.op("pe", mm(ps[zk][:, :], ka, qa(hc, pr=pr), True, True), reads=[KS[s], Q[hc]], writes=[PS[zk]])
                P.op("act", lambda e: e.activation(out=sE(es, c0, T), in_=ps[zk][:, c0:T], func=AF.Exp),
                     reads=[PS[zk]], writes=[SE[es]])
                P.op("act", lambda e: e.activation(out=sS(bs, c0, T), in_=sE(es, c0, T), func=AF.Ln, bias=1.0),
                     reads=[SE[es]], writes=[SS[bs]])
                info[n] = (c0, es, bs)

            def S2(n):
                hh, kb = steps[n]
                hc, pr = hparams(hh)
                c0, es, bs = info[n]
                ws = n % 2
                wb = n % 2
                if kb == nkb - 1:
                    P.op("pe", mm(ps[ACC][:, :], ZERO, qa(0), True, False, sgc=True), reads=[CST, Q[0]], writes=[PS[ACC]])
                    P.op("pe", mm(ps[OB][:, :], ZERO, qa(0), True, False, sgc=True), reads=[CST, Q[0]], writes=[PS[OB]])
                P.op("pe", mm(ps[ACC][:, c0:T], NEGINCL, sS(bs, c0, T), False, True, sgc=True), reads=[CST, SS[bs]], writes=[PS[ACC]])
                P.op("act", lambda e: e.activation(out=sW(ws, c0, T), in_=ps[ACC][:, c0:T], func=AF.Exp),
                     reads=[PS[ACC]], writes=[SW[ws]])
                P.op("dve", lambda e: e.tensor_tensor(out=sT(wb, c0, T), in0=sE(es, c0, T), in1=sW(ws, c0, T), op=ALU.mult),
                     reads=[SE[es], SW[ws]], writes=[SWT[wb]])

            def S3a(n):
                hh, kb = steps[n]
                c0, es, bs = info[n]
                if kb > 0:
                    P.op("pe", mm(ps[ACC][:, c0:T], NEGREST, sS(bs, c0, T), False, True, sgc=True), reads=[CST, SS[bs]], writes=[PS[ACC]])

            def S3b(n):
                hh, kb = steps[n]
                hc, pr = hparams(hh)
                c0, es, bs = info[n]
                wb = n % 2
                P.op("pe", mm(ps[OB][:, c0:T], v_sb[:, kb * 768 + hc * 128:kb * 768 + hc * 128 + 128], sT(wb, c0, T), False, kb == 0, sgc=True),
                     reads=[VT[kb], SWT[wb]], writes=[PS[OB]])
                if kb == 0:
                    P.op("dve", lambda e: e.tensor_copy(out=cata(p, hc, pr=pr), in_=ps[OB][pr, :]), reads=[PS[OB]], writes=[CAT[p][hc]])

            def filler():
                P.op("pe", mm(ps[OB][:, 0:FILL_N], ZERO, qa(0, 0, FILL_N), False, False, sgc=True), reads=[CST, Q[0]], writes=[PS[OB]])

            def gap_work(npull, nfill):
                got = 0
                for _ in range(npull):
                    if pull():
                        got += 1
                if live["gen"] is None:
                    for _ in range(max(0, nfill - got)):
                        filler()

            for k in range(-2, N):
                if 0 <= k + 2 < N:
                    S1(k + 2)
                if 0 <= k < N:
                    gap_work(PULL_A, 2)
                    S3a(k)
                head_start = (0 <= k + 1 < N) and steps[k + 1][1] == nkb - 1
                if head_start and 0 <= k < N:
                    gap_work(RPULL - PULL_A, 1)
                    S3b(k)
                    S2(k + 1)
                else:
                    if 0 <= k + 1 < N:
                        S2(k + 1)
                    if 0 <= k < N:
                        gap_work(RPULL - PULL_A, 1)
                        S3b(k)
            return live["gen"]

        drain(layer_a(0))
        for i in range(NT):
            p = i % 2
            t0 = i * T
            drain(kvq(i))
            drain(mem_attention(1, p, 4, 5, (6, 7)))
            gen = layer_a(i + 1) if i + 1 < NT else None
            gen = sb_tile(i, gen)
            drain(gen)
            for oc in range(NCH):
                drain(proj(BK_MAIN, "B", i, O_WOUTB, oc, resid_add(p, oc, BK_MAIN.statb), lambda c: cata(p, c), CAT[p]))
            drain(ffn(BK_MAIN, "B", i, p, G_BFFN, O_WGUB, O_WDB))
            rstd_from(BK_MAIN.statb)
            for c in range(NCH):
                P.op("dve", lambda e, c=c, p=p: e.scalar_tensor_tensor(out=big_sb[:, c * T:(c + 1) * T], in0=xa(p, c), scalar=gcol(G_FINAL, c),
                                                                    in1=Fs[2][:, 0:T], op0=ALU.mult, op1=ALU.mult),
                     reads=[X[p][c], GV, FT[2]], writes=[OST[c]] + big_ov(OST[c]))
            P.dma("sp", d_out, outT_d.rearrange("(c p) s -> p c s", p=128)[:, :, t0:t0 + T],
                  big_sb[:, 0:NCH * T].rearrange("p (c t) -> p c t", c=NCH), reads=OST, writes=[OUTT])
        P.wait_all("sp", [OUTT])
        P.emit(nc)
    return nc


def _blk(W, kc, mc):
    return W[kc * 128:(kc + 1) * 128, mc * 128:(mc + 1) * 128]


def _pool_perm():
    perm = []
    for g in range(4):
        perm.extend(range(192 * g, 192 * g + 128))
    for g in range(4):
        perm.extend(range(192 * g + 128, 192 * g + 192))
    return np.array(perm, dtype=np.int64)


def pack_weights(inp):
    f32 = np.float32
    perm = _pool_perm()
    full_perm = np.concatenate([perm, np.arange(768, 1024)])
    blocks = []
    w_in = np.asarray(inp["a_w_in"][0], f32)[:, full_perm]
    for oc in WIN_ORDER:
        for c in range(8):
            blocks.append(_blk(w_in, c, oc))
    w_out_a = np.asarray(inp["a_w_out"][0], f32)[full_perm, :]
    for oc in range(8):
        for c in range(8):
            blocks.append(_blk(w_out_a, c, oc))

    def ffn_blocks(w_gu, w_d):
        for f in range(NF):
            for c in range(8):
                blocks.append(_blk(w_gu, c, f))
            for c in range(8):
                blocks.append(_blk(w_gu, c, NF + f))
        for oc in range(8):
            for f in range(NF):
                blocks.append(_blk(w_d, f, oc))

    ffn_blocks(np.asarray(inp["a_w_gu"][0], f32), np.asarray(inp["a_w_down"][0], f32))
    w_kv = np.asarray(inp["w_kv"], f32)
    for hc in range(6):
        for c in range(8):
            blocks.append(_blk(w_kv, c, hc))
    for c in range(8):
        for j in range(4):
            blocks.append(_blk(w_kv, c, 6 + j))
    for c in range(8):
        for j in range(2):
            blocks.append(_blk(w_kv, c, 10 + j))
    w_q = np.asarray(inp["b_w_q"][0], f32)
    for oc in range(8):
        for c in range(8):
            blocks.append(_blk(w_q, c, oc))
    w_out_b = np.asarray(inp["b_w_out"][0], f32)
    for oc in range(8):
        for c in range(8):
            blocks.append(_blk(w_out_b, c, oc))
    ffn_blocks(np.asarray(inp["b_w_gu"][0], f32), np.asarray(inp["b_w_down"][0], f32))
    assert len(blocks) == NBLK
    wall = np.ascontiguousarray(np.stack(blocks, axis=1).reshape(128, NBLK * 128))

    wgp = np.asarray(inp["a_w_group"][0], f32)
    z = np.zeros((128, 128), f32)
    gb = []
    for g in range(4):
        a = wgp[g][0:128, 0:128]
        b = z.copy()
        r0 = (g % 2) * 64
        b[r0:r0 + 64, :] = wgp[g][128:192, 0:128]
        gb.extend([a, b])
    for pair in range(2):
        g0, g1 = 2 * pair, 2 * pair + 1
        c0 = z.copy(); c0[:, 0:64] = wgp[g0][0:128, 128:192]
        c1 = z.copy(); c1[:, 64:128] = wgp[g1][0:128, 128:192]
        dd = z.copy(); dd[0:64, 0:64] = wgp[g0][128:192, 128:192]; dd[64:128, 64:128] = wgp[g1][128:192, 128:192]
        gb.extend([c0, c1, dd])
    wgrp = np.ascontiguousarray(np.stack(gb, axis=1).reshape(128, 14 * 128))

    wm = []
    for key in ("a_w_mem_kv", "b_w_mem_kv"):
        w = np.asarray(inp[key][0], f32)
        wm.append(w.reshape(8, 128, 512).transpose(1, 0, 2).reshape(128, 8 * 512))
    wmem = np.ascontiguousarray(np.concatenate(wm, axis=1))

    idx = np.arange(128)
    ident = np.eye(128, dtype=f32)
    onesd = np.full((128, 128), 1.0 / D, f32)
    negincl = np.where(idx[:, None] >= idx[None, :], -1.0, 0.0).astype(f32)
    negrest = np.where(idx[:, None] < idx[None, :], -1.0, 0.0).astype(f32)
    maskb = np.where(idx[:, None] < idx[None, :], 0.0, -30000.0).astype(f32)
    onesp0 = np.zeros((128, 128), f32); onesp0[:, 0:64] = 1.0
    onesp1 = np.zeros((128, 128), f32); onesp1[:, 64:128] = 1.0
    cst = np.ascontiguousarray(np.concatenate([ident, onesd, negincl, negrest, maskb, onesp0, onesp1, z], axis=1))

    gvec = np.zeros((128, 64), f32)
    for k, g in enumerate([inp["a_norm_mix"][0], inp["a_norm_ffn"][0], inp["kv_norm"], inp["b_norm_mix"][0],
                           inp["b_norm_ffn"][0], inp["final_norm"], inp["mem_norm"]]):
        gvec[:, 8 * k:8 * k + 8] = np.asarray(g, f32).reshape(8, 128).T
    gvec[:, G_SCALE:G_SCALE + 6] = np.asarray(inp["a_scale"][0], f32)[perm].reshape(6, 128).T
    fac = np.zeros((128, 64), f32)
    for widx, w in enumerate((2, 4, 8, 16)):
        t = np.arange(16)
        fac[:, widx * 16:(widx + 1) * 16] = (w / np.minimum(t + 1, w)).astype(f32)[None, :]
    return dict(wall=wall, wgrp=wgrp, wmem=wmem, cst=cst, gvec=gvec, fac=fac)


def kernel(**inputs):
    x = np.asarray(inputs["x"], np.float32)
    mem = np.asarray(inputs["mem"], np.float32)
    B, S, _ = x.shape
    NT = S // T
    shared = pack_weights(inputs)
    nc = build_nc(NT)
    in_maps = []
    for b in range(B):
        m = dict(shared)
        m["xT"] = np.ascontiguousarray(x[b].T)
        m["memT"] = np.ascontiguousarray(mem[b].T)
        in_maps.append(m)
    res = run_bass_kernel_spmd(nc, in_maps, core_ids=list(range(B)))
    out = np.stack([np.asarray(r["outT"], np.float32).T for r in res.results], axis=0)
    return np.ascontiguousarray(out)
```

```python
from contextlib import ExitStack

import numpy as np
import concourse.bass as bass
import concourse.mybir as mybir
from concourse.bass_utils import run_bass_kernel_spmd

F32 = mybir.dt.float32
BF16 = mybir.dt.bfloat16
AF = mybir.ActivationFunctionType
ALU = mybir.AluOpType

D = 1024
NCH = 8
T = 512
DFF = 2816
NF = 22
NHEAD = 12
EPS = 1e-6
FILL = 2
FILL_N = 256
RPULL = 4
PULL_A = 2
NSLOT = 4
PIECE = 16
CONVB = 32
NBLK = 1408
NPIECE = NBLK // PIECE
NCONV = NBLK // CONVB
SEC_A = (0, 656)
SEC_KVQ = (656, 816)
SEC_B = (816, 1408)

O_WIN = 0
O_WOUTA = 64
O_WGUA = 128
O_WDA = 480
O_WK = 656
O_WV = 704
O_WQ = 752
O_WOUTB = 816
O_WGUB = 880
O_WDB = 1232

WIN_ORDER = (6, 7, 4, 5, 0, 1, 2, 3)

G_AMIX, G_AFFN, G_KV, G_BMIX, G_BFFN, G_FINAL, G_MEM = range(7)
G_SCALE = 56


class Tile:
    __slots__ = ("name", "w", "readers")

    def __init__(self, name):
        self.name = name
        self.w = None
        self.readers = []


class DmaSem:
    __slots__ = ("name", "count", "handle")

    def __init__(self, name):
        self.name = name
        self.count = 0
        self.handle = None


class _Instr:
    __slots__ = ("fn", "waits", "late", "signal", "dsem", "idx")

    def __init__(self, fn):
        self.fn = fn
        self.waits = []
        self.late = []
        self.signal = False
        self.dsem = None
        self.idx = 0


ENGS = ("pe", "act", "dve", "pool", "sp")
LATE_DEFAULT = 1


class Prog:
    def __init__(self):
        self.ins = {e: [] for e in ENGS}
        self.clock = {e: {} for e in ENGS}
        self.dsems = []

    def dsem(self, name):
        d = DmaSem(name)
        self.dsems.append(d)
        return d

    def _record(self, eng, fn, reads, writes, dsem=None, update=True, nlhs=None):
        ins = _Instr(fn)
        lst = self.ins[eng]
        lst.append(ins)
        ins.idx = len(lst)
        clock = self.clock[eng]
        need = []
        early_keys = set()
        for n, t in enumerate(reads):
            if t.w is not None:
                need.append(t.w)
                if nlhs is not None and n < nlhs:
                    early_keys.add(t.w[0])
        for t in writes:
            if t.w is not None:
                need.append(t.w)
            need.extend(t.readers)
        best = {}
        for ev in need:
            key, val, evclock = ev
            if key == "pe" and eng == "pe":
                continue
            if clock.get(key, 0) >= val:
                continue
            if best.get(key, (0, None))[0] < val:
                best[key] = (val, evclock)
        for key, (val, evclock) in best.items():
            if clock.get(key, 0) >= val:
                continue
            if nlhs is not None and key not in early_keys:
                ins.late.append((key, val))
            else:
                ins.waits.append((key, val))
            if isinstance(key, str):
                self.ins[key][val - 1].signal = True
            for k2, v2 in evclock.items():
                if clock.get(k2, 0) < v2:
                    clock[k2] = v2
            clock[key] = max(clock.get(key, 0), val)
        if dsem is not None:
            dsem.count += 16
            ins.dsem = dsem
            evclock = dict(clock)
            evclock[dsem] = dsem.count
            ev = (dsem, dsem.count, evclock)
        else:
            evclock = dict(clock)
            evclock[eng] = ins.idx
            ev = (eng, ins.idx, evclock)
        if update:
            for t in writes:
                t.w = ev
                t.readers = []
            for t in reads:
                t.readers.append(ev)
        return ins

    def op(self, eng, fn, reads=(), writes=(), nlhs=None):
        if eng == "pe" and nlhs is None:
            nlhs = LATE_DEFAULT
        if eng != "pe":
            nlhs = None
        return self._record(eng, fn, list(reads), list(writes), nlhs=nlhs)

    def dma(self, eng, dsem, out_ap, in_ap, reads=(), writes=()):
        def fn(e, out_ap=out_ap, in_ap=in_ap):
            return e.dma_start(out=out_ap, in_=in_ap)

        return self._record(eng, fn, list(reads), list(writes), dsem=dsem)

    def wait_all(self, eng, tiles):
        return self._record(eng, None, list(tiles), list(tiles), update=False)

    def emit(self, nc):
        with ExitStack() as st:
            sems = {}
            for e in ENGS:
                sems[e] = st.enter_context(nc.semaphore("s_" + e))
            for d in self.dsems:
                d.handle = st.enter_context(nc.semaphore("d_" + d.name))
            block = st.enter_context(nc.Block())
            rank = {}
            for e in ENGS:
                r = 0
                rk = []
                for ins in self.ins[e]:
                    if ins.signal:
                        r += 1
                    rk.append(r)
                rank[e] = rk

            def semval(key, val):
                if isinstance(key, str):
                    return sems[key], rank[key][val - 1]
                return key.handle, val

            def run(e, eh):
                for ins in self.ins[e]:
                    late = list(ins.late)
                    late.sort(key=lambda kv: 1 if isinstance(kv[0], str) and kv[0] in ("dve", "act") else 0)
                    attach = late.pop() if (late and ins.fn is not None) else None
                    early = list(ins.waits)
                    if attach is None and e in ("act", "dve") and early and ins.fn is not None and ins.dsem is None:
                        attach = early.pop()
                    for key, val in early + late:
                        sh, sv = semval(key, val)
                        eh.wait_ge(sh, sv)
                    if ins.fn is None:
                        continue
                    bi = ins.fn(eh)
                    if attach is not None:
                        sh, sv = semval(*attach)
                        bi._wait_ge(sh, sv)
                    if ins.dsem is not None:
                        bi.then_inc(ins.dsem.handle, 16)
                    elif ins.signal:
                        bi.then_inc(sems[e], 1)

            @block.tensor
            def _(eh):
                run("pe", eh)

            @block.scalar
            def _(eh):
                run("act", eh)

            @block.vector
            def _(eh):
                run("dve", eh)

            @block.gpsimd
            def _(eh):
                run("pool", eh)

            @block.sync
            def _(eh):
                run("sp", eh)


def build_nc(NT):
    S = NT * T
    NKB = S // 128
    nc = bass.Bass("TRN2", target_bir_lowering=False)
    xT_d = nc.dram_tensor("xT", [D, S], F32, kind="ExternalInput").ap()
    memT_d = nc.dram_tensor("memT", [D, 256], F32, kind="ExternalInput").ap()
    wall_d = nc.dram_tensor("wall", [128, NBLK * 128], F32, kind="ExternalInput").ap()
    wgrp_d = nc.dram_tensor("wgrp", [128, 14 * 128], F32, kind="ExternalInput").ap()
    wmem_d = nc.dram_tensor("wmem", [128, 2 * 8 * 512], F32, kind="ExternalInput").ap()
    cst_d = nc.dram_tensor("cst", [128, 8 * 128], F32, kind="ExternalInput").ap()
    gvec_d = nc.dram_tensor("gvec", [128, 64], F32, kind="ExternalInput").ap()
    fac_d = nc.dram_tensor("fac", [128, 64], F32, kind="ExternalInput").ap()
    outT_d = nc.dram_tensor("outT", [D, S], F32, kind="ExternalOutput").ap()
    wsc_d = nc.dram_tensor("wsc", [128, NBLK * 128], BF16).ap()
    kTs_d = nc.dram_tensor("kTs", [6 * 128, S], BF16).ap()

    P = Prog()
    with ExitStack() as st:
        def sb(name, shape, dt):
            return st.enter_context(nc.sbuf_tensor(name, shape, dt))

        v_sb = sb("vc", [128, NKB * 768], BF16)
        kst_sb = sb("kst", [128, 2 * S], BF16)
        xs = [sb(f"x{p}", [128, NCH * T], F32) for p in range(2)]
        h_sb = sb("h", [128, NCH * T], BF16)
        q_sb = sb("q", [128, NCH * T], BF16)
        cats = [sb(f"cat{p}", [128, NCH * T], BF16) for p in range(2)]
        big_sb = sb("big", [128, 5632], F32)
        sbt_sb = sb("sbt", [128, 3840], F32)
        ring_sb = sb("ring", [128, NSLOT * PIECE * 128], BF16)
        wg_sb = sb("wg", [128, 14 * 128], BF16)
        mk_sb = sb("mk", [128, 2 * 2 * 256], BF16)
        mvp_sb = sb("mvp", [128, 2 * 2 * 4 * 128], BF16)
        cst_sb = sb("cstb", [128, 8 * 128], BF16)
        gvec_sb = sb("gvec_s", [128, 64], F32)
        fac_sb = sb("fac_s", [128, 64], F32)
        halo_sb = sb("halo", [128, 6 * 16], F32)
        rkv_sb = sb("rkv", [128, T], F32)
        Fs = [sb(f"fs{k}", [128, 528], F32) for k in range(4)]
        Bs = [sb(f"bs{k}", [128, 512], BF16) for k in range(4)]
        ps = [st.enter_context(nc.psum_tensor(f"ps{k}", [128, 512], F32)) for k in range(8)]

        hid_ap = big_sb[:].bitcast(BF16)
        big_bf = hid_ap
        pooled_ap = big_sb[:, 3168:3168 + 1536].bitcast(BF16)
        sbt_bf = sbt_sb[:].bitcast(BF16)

        X = [[Tile(f"x{p}_{c}") for c in range(NCH)] for p in range(2)]
        CAT = [[Tile(f"cat{p}_{c}") for c in range(NCH)] for p in range(2)]
        H = [Tile(f"h{c}") for c in range(NCH)]
        Q = [Tile(f"q{c}") for c in range(NCH)]
        HID = [Tile(f"hid{f}") for f in range(NF)]
        U = [Tile(f"u{c}") for c in range(6)]
        PL = [Tile(f"pl{c}") for c in range(6)]
        HQ = [Tile(f"hq{c}") for c in range(NCH)]
        OST = [Tile(f"ost{c}") for c in range(NCH)]
        RING = [Tile(f"ring{k}") for k in range(NSLOT)]
        FT = [Tile(f"F{k}") for k in range(4)]
        BT = [Tile(f"B{k}") for k in range(4)]
        PS = [Tile(f"PS{k}") for k in range(8)]
        KS = [Tile(f"ks{s}") for s in range(2)]
        KD = [Tile(f"kd{hc}") for hc in range(6)]
        VT = [Tile(f"v{kb}") for kb in range(NKB)]
        WG = Tile("wg"); MK = Tile("mk"); MVP = Tile("mvp"); CST = Tile("cst"); GV = Tile("gvec"); FAC = Tile("fac")
        RKV = Tile("rkv")
        WSC = [Tile(f"wsc{p}") for p in range(NPIECE)]
        OUTT = Tile("outT")
        HALO = [Tile(f"halo{c}") for c in range(6)]
        SE = [Tile(f"sbE{j}") for j in range(3)]
        SW = [Tile(f"sbW{j}") for j in range(2)]
        SS = [Tile(f"sbS{j}") for j in range(3)]
        SWT = [Tile(f"sbT{j}") for j in range(2)]

        BIGREG = []
        for c in range(6):
            BIGREG.append((U[c], c * 2112, (c + 1) * 2112, "u"))
            BIGREG.append((PL[c], 12672 + c * 1024, 12672 + (c + 1) * 1024, "u"))
        for f in range(NF):
            BIGREG.append((HID[f], f * 1024, (f + 1) * 1024, "hid"))
        for c in range(NCH):
            BIGREG.append((HQ[c], c * 1024, (c + 1) * 1024, "hq"))
            BIGREG.append((OST[c], c * 2048, (c + 1) * 2048, "out"))

        def big_ov(tile):
            for (t, lo, hi, fam) in BIGREG:
                if t is tile:
                    break
            return [t2 for (t2, lo2, hi2, fam2) in BIGREG if fam2 != fam and lo < hi2 and lo2 < hi]

        def sE(j, lo, hi):
            return sbt_sb[:, j * 512 + lo:j * 512 + hi]

        def sW(j, lo, hi):
            return sbt_sb[:, 1536 + j * 512 + lo:1536 + j * 512 + hi]

        def sS(j, lo, hi):
            return sbt_bf[:, 5120 + j * 512 + lo:5120 + j * 512 + hi]

        def sT(j, lo, hi):
            return sbt_bf[:, 6656 + j * 512 + lo:6656 + j * 512 + hi]

        def xa(p, c, lo=0, hi=T):
            return xs[p][:, c * T + lo:c * T + hi]

        def cata(p, c, lo=0, hi=T, pr=slice(0, 128)):
            return cats[p][pr, c * T + lo:c * T + hi]

        def ha(c, lo=0, hi=T):
            return h_sb[:, c * T + lo:c * T + hi]

        def hqa(c):
            return big_bf[:, c * T:(c + 1) * T]

        def qa(c, lo=0, hi=T, pr=slice(0, 128)):
            return q_sb[pr, c * T + lo:c * T + hi]

        def hida(f):
            return hid_ap[:, f * T:(f + 1) * T]

        def ua(c, lo, hi, pr=slice(0, 128)):
            return big_sb[pr, c * 528 + lo:c * 528 + hi]

        def pla(c, lo=0, hi=T, pr=slice(0, 128)):
            return pooled_ap[pr, c * T + lo:c * T + hi]

        def csta(k):
            return cst_sb[:, k * 128:(k + 1) * 128]

        IDENT, ONESD, NEGINCL, NEGREST, MASKB, ONESP0, ONESP1, ZERO = [csta(k) for k in range(8)]

        def gcol(k, c):
            return gvec_sb[:, 8 * k + c:8 * k + c + 1]

        def mm(out_ap, lhsT, rhs, start, stop, sgc=False):
            if sgc:
                return lambda e: e.matmul(out_ap, lhsT=lhsT, rhs=rhs, start=start, stop=stop, skip_group_check=True)
            return lambda e: e.matmul(out_ap, lhsT=lhsT, rhs=rhs, start=start, stop=stop)

        d_cst = P.dsem("cst"); d_gv = P.dsem("gv"); d_fac = P.dsem("fac"); d_wg = P.dsem("wg")
        d_wm = [P.dsem("wm0"), P.dsem("wm1")]
        d_x = [P.dsem("x0"), P.dsem("x1")]
        d_out = P.dsem("out")
        d_ring = [P.dsem(f"ring{k}") for k in range(NSLOT)]
        d_ring2 = [P.dsem(f"ringb{k}") for k in range(NSLOT)]
        d_wst = [P.dsem(f"wst{k}") for k in range(NSLOT)]
        d_ks = [P.dsem("ks0"), P.dsem("ks1")]
        d_kd = [P.dsem(f"kd{hc}") for hc in range(6)]

        class Banks:
            def __init__(self, banks, statb):
                self.banks = banks
                self.statb = statb
                self.n = 0

            def get(self):
                k = self.banks[self.n % len(self.banks)]
                self.n += 1
                return k

        BK_MAIN = Banks((0, 1, 2, 3, 4, 5, 6), 7)
        BK_A = Banks((0, 1, 2), 3)

        def sec_pieces(sec):
            return list(range(sec[0] // PIECE, sec[1] // PIECE))

        seq = []
        sec_start = {}
        sec_start[("A", 0)] = len(seq); seq += sec_pieces(SEC_A)
        for i in range(NT):
            sec_start[("KVQ", i)] = len(seq); seq += sec_pieces(SEC_KVQ)
            if i + 1 < NT:
                sec_start[("A", i + 1)] = len(seq); seq += sec_pieces(SEC_A)
            sec_start[("B", i)] = len(seq); seq += sec_pieces(SEC_B)
        SECS = {"A": SEC_A, "KVQ": SEC_KVQ, "B": SEC_B}
        wstate = {"issued": 0, "cur": -1}

        seen_piece = set()

        def issue_piece(g):
            p = seq[g]
            s = g % NSLOT
            slot = ring_sb[:, s * PIECE * 128:(s + 1) * PIECE * 128]
            if p not in seen_piece:
                seen_piece.add(p)
                P.dma("pool", d_ring[s], slot, wall_d[:, p * PIECE * 128:(p + 1) * PIECE * 128], reads=[], writes=[RING[s]])
                P.dma("sp", d_wst[s], wsc_d[:, p * PIECE * 128:(p + 1) * PIECE * 128], slot, reads=[RING[s]], writes=[WSC[p]])
            else:
                P.dma("sp", d_ring2[s], slot, wsc_d[:, p * PIECE * 128:(p + 1) * PIECE * 128], reads=[WSC[p]], writes=[RING[s]])

        def wblk(kind, tile, j, n=1):
            sec = SECS[kind]
            assert sec[0] <= j < sec[1] and (j % PIECE) + n <= PIECE
            g = sec_start[(kind, tile)] + (j - sec[0]) // PIECE
            assert g >= wstate["cur"], (kind, tile, j, g, wstate["cur"])
            if g > wstate["cur"]:
                wstate["cur"] = g
                while wstate["issued"] < min(g + NSLOT, len(seq)):
                    issue_piece(wstate["issued"])
                    wstate["issued"] += 1
            s = g % NSLOT
            off = s * PIECE * 128 + (j % PIECE) * 128
            return ring_sb[:, off:off + n * 128], RING[s]

        def drain(gen):
            if gen is None:
                return
            for _ in gen:
                pass

        P.dma("pool", d_cst, cst_sb[:], cst_d, writes=[CST])
        P.dma("sp", d_gv, gvec_sb[:], gvec_d, writes=[GV])
        P.dma("sp", d_fac, fac_sb[:], fac_d, writes=[FAC])
        P.dma("pool", d_wg, wg_sb[:], wgrp_d, writes=[WG])
        for L in range(2):
            P.dma("pool", d_wm[L], ring_sb[:, L * 4096:(L + 1) * 4096], wmem_d[:, L * 4096:(L + 1) * 4096],
                  writes=[RING[2 * L], RING[2 * L + 1]])
        P.dma("sp", d_x[0], xs[0][:, 0:2048].rearrange("p (c m) -> p c m", c=8),
              memT_d.rearrange("(c p) m -> p c m", p=128), writes=X[0][0:4])
        P.op("dve", lambda e: e.memset(mvp_sb[:], 0.0), writes=[MVP])
        P.op("dve", lambda e: e.memset(halo_sb[:], 0.0), writes=HALO)

        def memx(c):
            return xs[0][:, c * 256:(c + 1) * 256]

        def memxt(c):
            return X[0][c // 2]

        msp = BK_MAIN.get()
        for c in range(NCH):
            b = c % 2
            eng = "pool" if c % 2 == 0 else "dve"
            P.op(eng, lambda e, c=c, b=b: e.tensor_tensor(out=Bs[b][:, 0:256], in0=memx(c), in1=memx(c), op=ALU.mult),
                 reads=[memxt(c)], writes=[BT[b]])
            P.op("pe", mm(ps[msp][:, 0:256], ONESD, Bs[b][:, 0:256], c == 0, c == NCH - 1), reads=[CST, BT[b]], writes=[PS[msp]])
        P.op("act", lambda e: e.activation(out=Fs[3][:, 0:256], in_=ps[msp][:, 0:256], func=AF.Ln, bias=EPS),
             reads=[PS[msp]], writes=[FT[3]])
        P.op("act", lambda e: e.activation(out=Fs[2][:, 0:256], in_=Fs[3][:, 0:256], func=AF.Exp, scale=-0.5),
             reads=[FT[3]], writes=[FT[2]])
        for c in range(NCH):
            P.op("dve", lambda e, c=c: e.scalar_tensor_tensor(out=ha(c, 0, 256), in0=memx(c), scalar=gcol(G_MEM, c),
                                                                in1=Fs[2][:, 0:256], op0=ALU.mult, op1=ALU.mult),
                 reads=[memxt(c), GV, FT[2]], writes=[H[c]])
        for L in range(2):
            RL = [RING[2 * L], RING[2 * L + 1]]

            def wm(c, j, n=1, L=L):
                off = L * 4096 + c * 512 + j * 128
                return ring_sb[:, off:off + n * 128]
            for fc in range(2):
                k = BK_MAIN.get()
                for c in range(NCH):
                    P.op("pe", mm(ps[k][:, 0:256], wm(c, fc), ha(c, 0, 256), c == 0, c == NCH - 1),
                         reads=RL + [H[c]], writes=[PS[k]], nlhs=2)
                P.op("act", lambda e, k=k, L=L, fc=fc: e.activation(
                    out=mk_sb[:, (L * 2 + fc) * 256:(L * 2 + fc + 1) * 256], in_=ps[k][:, 0:256], func=AF.Copy),
                    reads=[PS[k]], writes=[MK])
            for mb in range(2):
                k = BK_MAIN.get()
                for c in range(NCH):
                    P.op("pe", mm(ps[k][:, 0:256], ha(c, mb * 128, mb * 128 + 128), wm(c, 2, 2), c == 0, c == NCH - 1),
                         reads=[H[c]] + RL, writes=[PS[k]])
                for hh in range(4):
                    base = ((L * 2 + mb) * 4 + hh) * 128 + (hh % 2) * 64
                    P.op("dve", lambda e, k=k, base=base, hh=hh: e.tensor_copy(
                        out=mvp_sb[:, base:base + 64], in_=ps[k][:, hh * 64:(hh + 1) * 64]),
                        reads=[PS[k]], writes=[MVP])

        def mka(L, hh, mb):
            hc, half = divmod(hh, 2)
            base = (L * 2 + hc) * 256 + mb * 128
            return mk_sb[half * 64:half * 64 + 64, base:base + 128]

        def mvpa(L, mb, hh):
            base = ((L * 2 + mb) * 4 + hh) * 128
            return mvp_sb[:, base:base + 128]

        pending = []

        def flush_pending():
            while pending:
                pending.pop(0)()

        sqrot = [0]

        def stat_chunk(c, k, src_ap, src_tile, eng=None, defer=False):
            b = sqrot[0] % 2
            sqrot[0] += 1
            if eng is None:
                eng = "pool" if c % 2 == 0 else "dve"
            P.op(eng, lambda e, c=c, b=b: e.tensor_tensor(out=Bs[b][:, 0:T], in0=src_ap(c), in1=src_ap(c), op=ALU.mult),
                 reads=[src_tile(c)], writes=[BT[b]])

            def do_mm():
                P.op("pe", mm(ps[k][:, 0:T], ONESD, Bs[b][:, 0:T], c == 0, c == NCH - 1),
                     reads=[CST, BT[b]], writes=[PS[k]])
            if defer:
                pending.append(do_mm)
            else:
                do_mm()

        def rstd_from(k, dst_ap=None, dst_tile=None):
            flush_pending()
            if dst_ap is None:
                dst_ap, dst_tile = Fs[2][:, 0:T], FT[2]
            P.op("act", lambda e: e.activation(out=Fs[3][:, 0:T], in_=ps[k][:, 0:T], func=AF.Ln, bias=EPS),
                 reads=[PS[k]], writes=[FT[3]])
            P.op("act", lambda e: e.activation(out=dst_ap, in_=Fs[3][:, 0:T], func=AF.Exp, scale=-0.5),
                 reads=[FT[3]], writes=[dst_tile])

        def norm_apply(p, gk, rap=None, rtile=None):
            if rap is None:
                rap, rtile = Fs[2][:, 0:T], FT[2]
            for c in range(NCH):
                P.op("dve", lambda e, c=c: e.scalar_tensor_tensor(out=ha(c), in0=xa(p, c), scalar=gcol(gk, c), in1=rap,
                                                                    op0=ALU.mult, op1=ALU.mult),
                     reads=[X[p][c], GV, rtile], writes=[H[c]])

        def proj(bk, kind, tile, o_base, oi, evac, sap, stl):
            k = bk.get()
            for c in range(NCH):
                wap, wt = wblk(kind, tile, o_base + oi * NCH + c)
                P.op("pe", mm(ps[k][:, :], wap, sap(c), c == 0, c == NCH - 1), reads=[wt, stl[c]], writes=[PS[k]])
                yield
            evac(k)

        def resid_add(p, oc, statb):
            def ev(k):
                P.op("dve", lambda e: e.tensor_tensor(out=xa(p, oc), in0=xa(p, oc), in1=ps[k][:, :], op=ALU.add),
                     reads=[X[p][oc], PS[k]], writes=[X[p][oc]])
                flush_pending()
                stat_chunk(oc, statb, lambda c: xa(p, c), lambda c: X[p][c], eng="pool", defer=True)
            return ev

        def mem_attention(L, p, OB, ZB, zbanks):
            for hc in range(2):
                pts = []
                for half in range(2):
                    hh = hc * 2 + half
                    pr = slice(half * 64, half * 64 + 64)
                    for mb in range(2):
                        zk = zbanks[len(pts) % len(zbanks)]
                        bslot = (hc * 4 + len(pts)) % 4
                        P.op("pe", mm(ps[zk][:, :], mka(L, hh, mb), qa(6 + hc, pr=pr), True, True),
                             reads=[MK, Q[6 + hc]], writes=[PS[zk]])
                        yield
                        P.op("act", lambda e, zk=zk, bslot=bslot: e.activation(out=Bs[bslot][:, :], in_=ps[zk][:, :], func=AF.Exp),
                             reads=[PS[zk]], writes=[BT[bslot]])
                        pts.append((bslot, hh, mb, half))
                for n, (bslot, hh, mb, half) in enumerate(pts):
                    P.op("pe", mm(ps[OB][:, :], mvpa(L, mb, hh), Bs[bslot][:, :], n == 0, n == 3),
                         reads=[MVP, BT[bslot]], writes=[PS[OB]])
                    yield
                for n, (bslot, hh, mb, half) in enumerate(pts):
                    P.op("pe", mm(ps[ZB][:, :], ONESP0 if half == 0 else ONESP1, Bs[bslot][:, :], n == 0, n == 3),
                         reads=[CST, BT[bslot]], writes=[PS[ZB]])
                    yield
                P.op("act", lambda e: e.activation(out=Fs[3][:, 0:T], in_=ps[ZB][:, :], func=AF.Ln), reads=[PS[ZB]], writes=[FT[3]])
                P.op("act", lambda e: e.activation(out=Fs[3][:, 0:T], in_=Fs[3][:, 0:T], func=AF.Exp, scale=-1.0), reads=[FT[3]], writes=[FT[3]])
                P.op("dve", lambda e, hc=hc: e.tensor_tensor(out=cata(p, 6 + hc), in0=ps[OB][:, :], in1=Fs[3][:, 0:T], op=ALU.mult),
                     reads=[PS[OB], FT[3]], writes=[CAT[p][6 + hc]])

        def ffn(bk, kind, tile, p, gk, o_gu, o_d, after_down=None):
            rstd_from(bk.statb)
            norm_apply(p, gk)
            for f in range(NF):
                kg = bk.get()
                ku = bk.get()
                for c in range(NCH):
                    wap, wt = wblk(kind, tile, o_gu + f * 16 + c)
                    P.op("pe", mm(ps[kg][:, :], wap, ha(c), c == 0, c == NCH - 1), reads=[wt, H[c]], writes=[PS[kg]])
                    yield
                for c in range(NCH):
                    wap, wt = wblk(kind, tile, o_gu + f * 16 + 8 + c)
                    P.op("pe", mm(ps[ku][:, :], wap, ha(c), c == 0, c == NCH - 1), reads=[wt, H[c]], writes=[PS[ku]])
                    yield
                fs = f % 2
                P.op("act", lambda e, kg=kg, fs=fs: e.activation(out=Fs[fs][:, 0:T], in_=ps[kg][:, :], func=AF.Exp, scale=-1.0),
                     reads=[PS[kg]], writes=[FT[fs]])
                P.op("act", lambda e, fs=fs: e.activation(out=Fs[fs][:, 0:T], in_=Fs[fs][:, 0:T], func=AF.Ln, bias=1.0),
                     reads=[FT[fs]], writes=[FT[fs]])
                P.op("act", lambda e, fs=fs: e.activation(out=Fs[fs][:, 0:T], in_=Fs[fs][:, 0:T], func=AF.Exp, scale=-1.0),
                     reads=[FT[fs]], writes=[FT[fs]])
                P.op("dve", lambda e, kg=kg, fs=fs: e.tensor_tensor(out=Fs[fs][:, 0:T], in0=Fs[fs][:, 0:T], in1=ps[kg][:, :], op=ALU.mult),
                     reads=[FT[fs], PS[kg]], writes=[FT[fs]])
                P.op("dve", lambda e, ku=ku, fs=fs, f=f: e.tensor_tensor(out=hida(f), in0=Fs[fs][:, 0:T], in1=ps[ku][:, :], op=ALU.mult),
                     reads=[FT[fs], PS[ku]], writes=[HID[f]] + big_ov(HID[f]))
            for oc in range(NCH):
                k = bk.get()
                for f in range(NF):
                    wap, wt = wblk(kind, tile, o_d + oc * NF + f)
                    P.op("pe", mm(ps[k][:, :], wap, hida(f), f == 0, f == NF - 1), reads=[wt, HID[f]], writes=[PS[k]])
                    yield
                resid_add(p, oc, bk.statb)(k)

        POOLCFG = [(0, 0, 128, 2), (1, 0, 128, 4), (2, 0, 128, 8), (3, 0, 128, 16),
                   (4, 0, 64, 2), (4, 64, 128, 4), (5, 0, 64, 8), (5, 64, 128, 16)]

        def pool_chunk(i, c, eng, fa, fb):
            cfgs = [cf for cf in POOLCFG if cf[0] == c]
            wmax = max(cf[3] for cf in cfgs)
            src_ap = lambda lo, hi: ua(c, lo, hi)
            src_t = U[c]
            slots = [fa, fb]
            sums = {}
            step = 1
            n = 0
            while step < wmax:
                dst = slots[n % 2]
                lo = 2 * step - 1
                P.op(eng, lambda e, dst=dst, lo=lo, step=step, src_ap=src_ap: e.tensor_tensor(
                    out=Fs[dst][:, lo:528], in0=src_ap(lo, 528), in1=src_ap(lo - step, 528 - step), op=ALU.add),
                    reads=[src_t], writes=[FT[dst]])
                step *= 2
                sums[step] = dst
                src_ap = (lambda lo, hi, dst=dst: Fs[dst][:, lo:hi])
                src_t = FT[dst]
                n += 1
            for (_, p0, p1, w) in cfgs:
                s = sums[w]
                pr = slice(p0, p1)
                widx = {2: 0, 4: 1, 8: 2, 16: 3}[w]
                if i == 0:
                    P.op(eng, lambda e, s=s, pr=pr, widx=widx: e.tensor_tensor(
                        out=Fs[s][pr, 16:32], in0=Fs[s][pr, 16:32], in1=fac_sb[pr, widx * 16:(widx + 1) * 16], op=ALU.mult),
                        reads=[FT[s], FAC], writes=[FT[s]])
                P.op("dve", lambda e, s=s, pr=pr, w=w: e.scalar_tensor_tensor(
                    out=pla(c, pr=pr), in0=Fs[s][pr, 16:528], scalar=1.0 / w, in1=ua(c, 16, 528, pr=pr),
                    op0=ALU.mult, op1=ALU.subtract),
                    reads=[FT[s], U[c]], writes=[PL[c]] + big_ov(PL[c]))
            P.op(eng, lambda e: e.tensor_copy(out=halo_sb[:, c * 16:(c + 1) * 16], in_=ua(c, 512, 528)), reads=[U[c]], writes=[HALO[c]])

        GROUP_PLAN = [(0, [(0, 0), (1, 4)]), (1, [(2, 1), (3, 4)]), (2, [(4, 2), (5, 5)]), (3, [(6, 3), (7, 5)]),
                      (4, [(8, 0), (9, 1), (10, 4)]), (5, [(11, 2), (12, 3), (13, 5)])]

        def layer_a(i):
            p = i % 2
            bk = BK_A
            t0 = i * T
            P.dma("sp", d_x[p], xs[p][:].rearrange("p (c t) -> p c t", c=NCH),
                  xT_d.rearrange("(c p) s -> p c s", p=128)[:, :, t0:t0 + T], reads=[], writes=X[p])
            k = bk.get()
            for c in range(NCH):
                stat_chunk(c, k, lambda c: xa(p, c), lambda c: X[p][c])
                yield
            rstd_from(k)
            norm_apply(p, G_AMIX)
            for oi, oc in enumerate(WIN_ORDER):
                if oc < 6:
                    def ev(k, oc=oc):
                        P.op("dve", lambda e: e.tensor_copy(out=ua(oc, 16, 528), in_=ps[k][:, :]),
                             reads=[PS[k]], writes=[U[oc]] + big_ov(U[oc]))
                        P.op("pool", lambda e: e.tensor_copy(out=ua(oc, 0, 16), in_=halo_sb[:, oc * 16:(oc + 1) * 16]),
                             reads=[HALO[oc]], writes=[U[oc]])
                else:
                    def ev(k, oc=oc):
                        P.op("dve", lambda e: e.tensor_scalar_mul(out=qa(oc), in0=ps[k][:, :], scalar1=0.125),
                             reads=[PS[k]], writes=[Q[oc]])
                yield from proj(bk, "A", i, O_WIN, oi, ev, ha, H)
            for c in (4, 5):
                pool_chunk(i, c, "pool", 2, 3)
            for c in range(4):
                pool_chunk(i, c, "dve", 0, 1)
            yield from mem_attention(0, p, 0, 1, (2,))
            bk.n = 0
            for (oc, plan) in GROUP_PLAN:
                k = bk.get()
                for n, (bi, pc) in enumerate(plan):
                    P.op("pe", mm(ps[k][:, :], wg_sb[:, bi * 128:(bi + 1) * 128], pla(pc), n == 0, n == len(plan) - 1),
                         reads=[WG, PL[pc]], writes=[PS[k]])
                    yield
                P.op("dve", lambda e, k=k, oc=oc: e.tensor_scalar_mul(out=cata(p, oc), in0=ps[k][:, :],
                                                                       scalar1=gvec_sb[:, G_SCALE + oc:G_SCALE + oc + 1]),
                     reads=[PS[k], GV], writes=[CAT[p][oc]])
            for oc in range(NCH):
                yield from proj(bk, "A", i, O_WOUTA, oc, resid_add(p, oc, bk.statb), lambda c: cata(p, c), CAT[p])
            yield from ffn(bk, "A", i, p, G_AFFN, O_WGUA, O_WDA)
            rstd_from(bk.statb, rkv_sb[:, :], RKV)
            yield

        def kvq(i):
            p = i % 2
            bk = BK_MAIN
            t0 = i * T
            norm_apply(p, G_KV, rkv_sb[:, :], RKV)
            for c in range(NCH):
                P.op("dve", lambda e, c=c: e.scalar_tensor_tensor(out=hqa(c), in0=xa(p, c), scalar=gcol(G_BMIX, c), in1=rkv_sb[:, :],
                                                                    op0=ALU.mult, op1=ALU.mult),
                     reads=[X[p][c], GV, RKV], writes=[HQ[c]] + big_ov(HQ[c]))
            for hc in range(6):
                def ev(k, hc=hc):
                    eng = "dve" if hc % 2 == 0 else "act"
                    if eng == "dve":
                        P.op("dve", lambda e: e.tensor_copy(out=cata(p, hc), in_=ps[k][:, :]), reads=[PS[k]], writes=[CAT[p][hc]])
                    else:
                        P.op("act", lambda e: e.activation(out=cata(p, hc), in_=ps[k][:, :], func=AF.Copy), reads=[PS[k]], writes=[CAT[p][hc]])
                    P.dma("sp", d_kd[hc], kTs_d[hc * 128:(hc + 1) * 128, t0:t0 + T], cata(p, hc), reads=[CAT[p][hc]], writes=[KD[hc]])
                yield from proj(bk, "KVQ", i, O_WK, hc, ev, ha, H)
            kbanks = [bk.get() for _ in range(4)]
            for c in range(NCH):
                wap, wt = wblk("KVQ", i, O_WV + c * 4, 4)
                for tb in range(4):
                    P.op("pe", mm(ps[kbanks[tb]][:, :], ha(c, tb * 128, tb * 128 + 128), wap, c == 0, c == NCH - 1),
                         reads=[H[c], wt], writes=[PS[kbanks[tb]]])
                    yield
            for tb in range(4):
                kb = 4 * i + tb
                if tb % 2 == 0:
                    P.op("dve", lambda e, tb=tb, kb=kb: e.tensor_copy(out=v_sb[:, kb * 768:kb * 768 + 512], in_=ps[kbanks[tb]][:, :]),
                         reads=[PS[kbanks[tb]]], writes=[VT[kb]])
                else:
                    P.op("act", lambda e, tb=tb, kb=kb: e.activation(out=v_sb[:, kb * 768:kb * 768 + 512], in_=ps[kbanks[tb]][:, :], func=AF.Copy),
                         reads=[PS[kbanks[tb]]], writes=[VT[kb]])
            k2 = [bk.get() for _ in range(4)]
            for c in range(NCH):
                wap, wt = wblk("KVQ", i, O_WV + 32 + c * 2, 2)
                for tb in range(4):
                    kk = k2[tb]
                    col = 0
                    P.op("pe", mm(ps[kk][:, col:col + 256], ha(c, tb * 128, tb * 128 + 128), wap, c == 0, c == NCH - 1),
                         reads=[H[c], wt], writes=[PS[kk]])
                    yield
            for tb in range(4):
                kb = 4 * i + tb
                kk = k2[tb]
                col = 0
                P.op("act", lambda e, kk=kk, col=col, kb=kb: e.activation(out=v_sb[:, kb * 768 + 512:kb * 768 + 768], in_=ps[kk][:, col:col + 256], func=AF.Copy),
                     reads=[PS[kk]], writes=[VT[kb]])
            for oc in range(NCH):
                def ev(k, oc=oc):
                    if oc % 2 == 0:
                        P.op("act", lambda e: e.activation(out=qa(oc), in_=ps[k][:, :], func=AF.Copy, scale=0.125),
                             reads=[PS[k]], writes=[Q[oc]])
                    else:
                        P.op("dve", lambda e: e.tensor_scalar_mul(out=qa(oc), in0=ps[k][:, :], scalar1=0.125),
                             reads=[PS[k]], writes=[Q[oc]])
                yield from proj(bk, "KVQ", i, O_WQ, oc, ev, hqa, HQ)

        def sb_tile(i, gen):
            p = i % 2
            nkb = 4 * (i + 1)
            npre = 4 * i * 128
            steps = [(hh, kb) for hh in range(NHEAD) for kb in range(nkb - 1, -1, -1)]
            N = len(steps)
            info = {}
            ACC, OB = 4, 5
            live = {"gen": gen}

            def pull():
                g = live["gen"]
                if g is None:
                    return False
                try:
                    next(g)
                    return True
                except StopIteration:
                    live["gen"] = None
                    return False

            def load_kstage(hc):
                s = hc % 2
                if npre > 0:
                    P.dma("sp", d_ks[s], kst_sb[:, s * S:s * S + npre], kTs_d[hc * 128:(hc + 1) * 128, 0:npre],
                          reads=[KD[hc]], writes=[KS[s]])
                P.dma("sp", d_ks[s], kst_sb[:, s * S + npre:s * S + npre + T], cata(p, hc), reads=[CAT[p][hc]], writes=[KS[s]])

            load_kstage(0)
            load_kstage(1)

            def hparams(hh):
                hc, half = divmod(hh, 2)
                pr = slice(half * 64, half * 64 + 64)
                return hc, pr

            def S1(n):
                hh, kb = steps[n]
                hc, pr = hparams(hh)
                if hh % 2 == 0 and kb == nkb - 1 and 1 <= hc and hc + 1 < 6:
                    load_kstage(hc + 1)
                j = kb - 4 * i
                c0 = 128 * j if j > 0 else 0
                zk = 6 + n % 2
                es = n % 3
                bs = n % 3
                s = hc % 2
                ka = kst_sb[pr, s * S + kb * 128:s * S + kb * 128 + 128]
                if j >= 0:
                    P.op("pe", mm(ps[zk][:, c0:c0 + 128], IDENT, MASKB, True, False), reads=[CST], writes=[PS[zk]])
                    P.op("pe", mm(ps[zk][:, c0:c0 + 128], ka, qa(hc, c0, c0 + 128, pr=pr), False, True),
                         reads=[KS[s], Q[hc]], writes=[PS[zk]])
                    if c0 + 128 < T:
                        P.op("pe", mm(ps[zk][:, c0 + 128:T], ka, qa(hc, c0 + 128, T, pr=pr), True, True),
                             reads=[KS[s], Q[hc]], writes=[PS[zk]])
                else:
                    P.op("pe", mm(ps[zk][:, :], ka, qa(hc, pr=pr), True, True), reads=[KS[s], Q[hc]], writes=[PS[zk]])
                P.op("act", lambda e: e.activation(out=sE(es, c0, T), in_=ps[zk][:, c0:T], func=AF.Exp),
                     reads=[PS[zk]], writes=[SE[es]])
                P.op("act", lambda e: e.activation(out=sS(bs, c0, T), in_=sE(es, c0, T), func=AF.Ln, bias=1.0),
                     reads=[SE[es]], writes=[SS[bs]])
                info[n] = (c0, es, bs)

            def S2(n):
                hh, kb = steps[n]
                hc, pr = hparams(hh)
                c0, es, bs = info[n]
                ws = n % 2
                wb = n % 2
                if kb == nkb - 1:
                    P.op("pe", mm(ps[ACC][:, :], ZERO, qa(0), True, False, sgc=True), reads=[CST, Q[0]], writes=[PS[ACC]])
                    P.op("pe", mm(ps[OB][:, :], ZERO, qa(0), True, False, sgc=True), reads=[CST, Q[0]], writes=[PS[OB]])
                P.op("pe", mm(ps[ACC][:, c0:T], NEGINCL, sS(bs, c0, T), False, True, sgc=True), reads=[CST, SS[bs]], writes=[PS[ACC]])
                P.op("act", lambda e: e.activation(out=sW(ws, c0, T), in_=ps[ACC][:, c0:T], func=AF.Exp),
                     reads=[PS[ACC]], writes=[SW[ws]])
                P.op("dve", lambda e: e.tensor_tensor(out=sT(wb, c0, T), in0=sE(es, c0, T), in1=sW(ws, c0, T), op=ALU.mult),
                     reads=[SE[es], SW[ws]], writes=[SWT[wb]])

            def S3a(n):
                hh, kb = steps[n]
                c0, es, bs = info[n]
                if kb > 0:
                    P.op("pe", mm(ps[ACC][:, c0:T], NEGREST, sS(bs, c0, T), False, True, sgc=True), reads=[CST, SS[bs]], writes=[PS[ACC]])

            def S3b(n):
                hh, kb = steps[n]
                hc, pr = hparams(hh)
                c0, es, bs = info[n]
                wb = n % 2
                P.op("pe", mm(ps[OB][:, c0:T], v_sb[:, kb * 768 + hc * 128:kb * 768 + hc * 128 + 128], sT(wb, c0, T), False, kb == 0, sgc=True),
                     reads=[VT[kb], SWT[wb]], writes=[PS[OB]])
                if kb == 0:
                    P.op("dve", lambda e: e.tensor_copy(out=cata(p, hc, pr=pr), in_=ps[OB][pr, :]), reads=[PS[OB]], writes=[CAT[p][hc]])

            def filler():
                P.op("pe", mm(ps[OB][:, 0:FILL_N], ZERO, qa(0, 0, FILL_N), False, False, sgc=True), reads=[CST, Q[0]], writes=[PS[OB]])

            def gap_work(npull, nfill):
                got = 0
                for _ in range(npull):
                    if pull():
                        got += 1
                if live["gen"] is None:
                    for _ in range(max(0, nfill - got)):
                        filler()

            for k in range(-2, N):
                if 0 <= k + 2 < N:
                    S1(k + 2)
                if 0 <= k < N:
                    gap_work(PULL_A, 2)
                    S3a(k)
                head_start = (0 <= k + 1 < N) and steps[k + 1][1] == nkb - 1
                if head_start and 0 <= k < N:
                    gap_work(RPULL - PULL_A, 1)
                    S3b(k)
                    S2(k + 1)
                else:
                    if 0 <= k + 1 < N:
                        S2(k + 1)
                    if 0 <= k < N:
                        gap_work(RPULL - PULL_A, 1)
                        S3b(k)
            return live["gen"]

        drain(layer_a(0))
        for i in range(NT):
            p = i % 2
            t0 = i * T
            drain(kvq(i))
            drain(mem_attention(1, p, 4, 5, (6, 7)))
            gen = layer_a(i + 1) if i + 1 < NT else None
            gen = sb_tile(i, gen)
            drain(gen)
            for oc in range(NCH):
                drain(proj(BK_MAIN, "B", i, O_WOUTB, oc, resid_add(p, oc, BK_MAIN.statb), lambda c: cata(p, c), CAT[p]))
            drain(ffn(BK_MAIN, "B", i, p, G_BFFN, O_WGUB, O_WDB))
            rstd_from(BK_MAIN.statb)
            for c in range(NCH):
                P.op("dve", lambda e, c=c, p=p: e.scalar_tensor_tensor(out=big_sb[:, c * T:(c + 1) * T], in0=xa(p, c), scalar=gcol(G_FINAL, c),
                                                                    in1=Fs[2][:, 0:T], op0=ALU.mult, op1=ALU.mult),
                     reads=[X[p][c], GV, FT[2]], writes=[OST[c]] + big_ov(OST[c]))
            P.dma("sp", d_out, outT_d.rearrange("(c p) s -> p c s", p=128)[:, :, t0:t0 + T],
                  big_sb[:, 0:NCH * T].rearrange("p (c t) -> p c t", c=NCH), reads=OST, writes=[OUTT])
        P.wait_all("sp", [OUTT])
        P.emit(nc)
    return nc


def _blk(W, kc, mc):
    return W[kc * 128:(kc + 1) * 128, mc * 128:(mc + 1) * 128]


def _pool_perm():
    perm = []
    for g in range(4):
        perm.extend(range(192 * g, 192 * g + 128))
    for g in range(4):
        perm.extend(range(192 * g + 128, 192 * g + 192))
    return np.array(perm, dtype=np.int64)


def pack_weights(inp):
    f32 = np.float32
    perm = _pool_perm()
    full_perm = np.concatenate([perm, np.arange(768, 1024)])
    blocks = []
    w_in = np.asarray(inp["a_w_in"][0], f32)[:, full_perm]
    for oc in WIN_ORDER:
        for c in range(8):
            blocks.append(_blk(w_in, c, oc))
    w_out_a = np.asarray(inp["a_w_out"][0], f32)[full_perm, :]
    for oc in range(8):
        for c in range(8):
            blocks.append(_blk(w_out_a, c, oc))

    def ffn_blocks(w_gu, w_d):
        for f in range(NF):
            for c in range(8):
                blocks.append(_blk(w_gu, c, f))
            for c in range(8):
                blocks.append(_blk(w_gu, c, NF + f))
        for oc in range(8):
            for f in range(NF):
                blocks.append(_blk(w_d, f, oc))

    ffn_blocks(np.asarray(inp["a_w_gu"][0], f32), np.asarray(inp["a_w_down"][0], f32))
    w_kv = np.asarray(inp["w_kv"], f32)
    for hc in range(6):
        for c in range(8):
            blocks.append(_blk(w_kv, c, hc))
    for c in range(8):
        for j in range(4):
            blocks.append(_blk(w_kv, c, 6 + j))
    for c in range(8):
        for j in range(2):
            blocks.append(_blk(w_kv, c, 10 + j))
    w_q = np.asarray(inp["b_w_q"][0], f32)
    for oc in range(8):
        for c in range(8):
            blocks.append(_blk(w_q, c, oc))
    w_out_b = np.asarray(inp["b_w_out"][0], f32)
    for oc in range(8):
        for c in range(8):
            blocks.append(_blk(w_out_b, c, oc))
    ffn_blocks(np.asarray(inp["b_w_gu"][0], f32), np.asarray(inp["b_w_down"][0], f32))
    assert len(blocks) == NBLK
    wall = np.ascontiguousarray(np.stack(blocks, axis=1).reshape(128, NBLK * 128))

    wgp = np.asarray(inp["a_w_group"][0], f32)
    z = np.zeros((128, 128), f32)
    gb = []
    for g in range(4):
        a = wgp[g][0:128, 0:128]
        b = z.copy()
        r0 = (g % 2) * 64
        b[r0:r0 + 64, :] = wgp[g][128:192, 0:128]
        gb.extend([a, b])
    for pair in range(2):
        g0, g1 = 2 * pair, 2 * pair + 1
        c0 = z.copy(); c0[:, 0:64] = wgp[g0][0:128, 128:192]
        c1 = z.copy(); c1[:, 64:128] = wgp[g1][0:128, 128:192]
        dd = z.copy(); dd[0:64, 0:64] = wgp[g0][128:192, 128:192]; dd[64:128, 64:128] = wgp[g1][128:192, 128:192]
        gb.extend([c0, c1, dd])
    wgrp = np.ascontiguousarray(np.stack(gb, axis=1).reshape(128, 14 * 128))

    wm = []
    for key in ("a_w_mem_kv", "b_w_mem_kv"):
        w = np.asarray(inp[key][0], f32)
        wm.append(w.reshape(8, 128, 512).transpose(1, 0, 2).reshape(128, 8 * 512))
    wmem = np.ascontiguousarray(np.concatenate(wm, axis=1))

    idx = np.arange(128)
    ident = np.eye(128, dtype=f32)
    onesd = np.full((128, 128), 1.0 / D, f32)
    negincl = np.where(idx[:, None] >= idx[None, :], -1.0, 0.0).astype(f32)
    negrest = np.where(idx[:, None] < idx[None, :], -1.0, 0.0).astype(f32)
    maskb = np.where(idx[:, None] < idx[None, :], 0.0, -30000.0).astype(f32)
    onesp0 = np.zeros((128, 128), f32); onesp0[:, 0:64] = 1.0
    onesp1 = np.zeros((128, 128), f32); onesp1[:, 64:128] = 1.0
    cst = np.ascontiguousarray(np.concatenate([ident, onesd, negincl, negrest, maskb, onesp0, onesp1, z], axis=1))

    gvec = np.zeros((128, 64), f32)
    for k, g in enumerate([inp["a_norm_mix"][0], inp["a_norm_ffn"][0], inp["kv_norm"], inp["b_norm_mix"][0],
                           inp["b_norm_ffn"][0], inp["final_norm"], inp["mem_norm"]]):
        gvec[:, 8 * k:8 * k + 8] = np.asarray(g, f32).reshape(8, 128).T
    gvec[:, G_SCALE:G_SCALE + 6] = np.asarray(inp["a_scale"][0], f32)[perm].reshape(6, 128).T
    fac = np.zeros((128, 64), f32)
    for widx, w in enumerate((2, 4, 8, 16)):
        t = np.arange(16)
        fac[:, widx * 16:(widx + 1) * 16] = (w / np.minimum(t + 1, w)).astype(f32)[None, :]
    return dict(wall=wall, wgrp=wgrp, wmem=wmem, cst=cst, gvec=gvec, fac=fac)


def kernel(**inputs):
    x = np.asarray(inputs["x"], np.float32)
    mem = np.asarray(inputs["mem"], np.float32)
    B, S, _ = x.shape
    NT = S // T
    shared = pack_weights(inputs)
    nc = build_nc(NT)
    in_maps = []
    for b in range(B):
        m = dict(shared)
        m["xT"] = np.ascontiguousarray(x[b].T)
        m["memT"] = np.ascontiguousarray(mem[b].T)
        in_maps.append(m)
    res = run_bass_kernel_spmd(nc, in_maps, core_ids=list(range(B)))
    out = np.stack([np.asarray(r["outT"], np.float32).T for r in res.results], axis=0)
    return np.ascontiguousarray(out)
```
